# Optimizing a Trainium2 kernel written in Bass

```python
import jax, jax.numpy as jnp
from jax import lax
import numpy as np

D_MODEL = 2048
BATCH = 4
SEQ = 2048
DEPTH = 4

GRID_W = 64
CTX_LEN = 256
N_MOD = 9
D_FF = 5632
EPS = 1e-6
N_BR = 4
W_BR = D_MODEL // 4

A_HD = 64
A_HEADS = W_BR // A_HD
A_W_RANK = 64
A_A_RANK = 64
A_G_RANK = 128
A_LN_EPS = 64e-5
A_COLS = 3 * W_BR + 2 * A_W_RANK + 2 * A_A_RANK + A_G_RANK

B_CHUNK = 128
B_GD = 128
B_GROUPS = W_BR // B_GD
B_COLS = 2 * W_BR

C_HEADS = 4
C_HD = W_BR // C_HEADS
C_CONV = 3
C_CHUNK = 64
C_COLS = 4 * W_BR + 4 * C_HEADS

D_HEADS = 4
D_DV = W_BR // D_HEADS
D_DK = D_DV // 2
D_RANK = 16
D_TAU = 16.0
D_CHUNK = 64
D_COLS = 2 * D_HEADS * D_DK + 2 * W_BR + 2 * D_RANK

D_IN = A_COLS + B_COLS + C_COLS + D_COLS + N_BR * D_MODEL

kernel_name = "hybrid_rwkv7_gmlp_mlstm_gla_dit"


def _split(z, sizes):
    idx = [int(i) for i in np.cumsum(sizes)[:-1]]
    return jnp.split(z, idx, axis=-1)


def _rms(x, g):
    xf = x.astype(jnp.float32)
    y = xf * lax.rsqrt(jnp.mean(xf * xf, axis=-1, keepdims=True) + EPS)
    return (y * g.astype(jnp.float32)).astype(x.dtype)


def _head_ln(z, g, b, eps):
    H, d = z.shape[-2], z.shape[-1]
    zf = z.astype(jnp.float32)
    mu = jnp.mean(zf, axis=-1, keepdims=True)
    zc = zf - mu
    y = zc * lax.rsqrt(jnp.mean(zc * zc, axis=-1, keepdims=True) + eps) * g.reshape(H, d)
    if b is not None:
        y = y + b.reshape(H, d)
    return y


def _head_rms(z, g, eps):
    H, d = z.shape[-2], z.shape[-1]
    zf = z.astype(jnp.float32)
    return zf * lax.rsqrt(jnp.mean(zf * zf, axis=-1, keepdims=True) + eps) * g.reshape(H, d)


def _modulate(hn, shift, scale, L):
    B = hn.shape[0]
    return jnp.concatenate([hn[:, :L] * (1.0 + scale[B:, None]) + shift[B:, None],
                            hn[:, L:] * (1.0 + scale[:B, None]) + shift[:B, None]], axis=1)


def _gated_add(h, y, gate, L):
    B = h.shape[0]
    return h + jnp.concatenate([y[:, :L] * gate[B:, None], y[:, L:] * gate[:B, None]], axis=1)


def _swiglu(x, w1, w3, w2):
    return (jax.nn.silu(x @ w1) * (x @ w3)) @ w2


def _shift_grid(z):
    B, T, C = z.shape
    rows = T // GRID_W
    zr = z.reshape(B, rows, GRID_W, C // 4, 4)
    left = jnp.pad(zr[..., 0], ((0, 0), (0, 0), (1, 0), (0, 0)))[:, :, :-1]
    right = jnp.pad(zr[..., 1], ((0, 0), (0, 0), (0, 1), (0, 0)))[:, :, 1:]
    up = jnp.pad(zr[..., 2], ((0, 0), (1, 0), (0, 0), (0, 0)))[:, :-1]
    down = jnp.pad(zr[..., 3], ((0, 0), (0, 1), (0, 0), (0, 0)))[:, 1:]
    return jnp.stack([left, right, up, down], axis=-1).reshape(B, T, C)


def _shift_seq(z):
    B, T, C = z.shape
    zr = z.reshape(B, T, C // 2, 2)
    prev = jnp.pad(zr[..., 0], ((0, 0), (1, 0), (0, 0)))[:, :-1]
    nxt = jnp.pad(zr[..., 1], ((0, 0), (0, 1), (0, 0)))[:, 1:]
    return jnp.stack([prev, nxt], axis=-1).reshape(B, T, C)


def _conv_centred(z, w, b):
    C = z.shape[-1]
    pad = w.shape[0] // 2
    y = lax.conv_general_dilated(z, w[:, None, :].astype(z.dtype), window_strides=(1,),
                                 padding=((pad, pad),), dimension_numbers=("NWC", "WIO", "NWC"),
                                 feature_group_count=C)
    return y + b


def _to_chunks(z, Lc):
    B, T, H, d = z.shape
    return jnp.moveaxis(z.reshape(B, T // Lc, Lc, H, d), (1, 3), (0, 2))


def _from_chunks(z):
    nc, B, H, Lc, d = z.shape
    return jnp.moveaxis(z, (0, 2), (1, 3)).reshape(B, nc * Lc, H, d)


def _rwkv7_scan(S0, xs, reverse):
    def step(S, inp):
        r, w, kk, b, kt, v = inp
        sa = jnp.einsum("bhvk,bhk->bhv", S, kk)
        S = S * w[:, :, None, :] - sa[..., None] * b[:, :, None, :] + v[..., None] * kt[:, :, None, :]
        return S, jnp.einsum("bhvk,bhk->bhv", S, r)
    return lax.scan(step, S0, xs, reverse=reverse)


def _rwkv7_branch(pa, L, mu, w0, wup, a0, aup, gup, k_k, k_a, r_k, ln_g, ln_b):
    dt = pa.dtype
    pa = pa.astype(jnp.float32)
    pc, px = pa[:, :L], pa[:, L:]
    za = jnp.concatenate([pc + (_shift_seq(pc) - pc) * mu, px + (_shift_grid(px) - px) * mu], axis=1)
    r, k, v, wd_f, wd_b, ad_f, ad_b, gd = _split(
        za, [W_BR, W_BR, W_BR, A_W_RANK, A_W_RANK, A_A_RANK, A_A_RANK, A_G_RANK])
    B, T, _ = za.shape
    hd = lambda z: z.reshape(B, T, A_HEADS, A_HD)
    tm = lambda z: jnp.moveaxis(z, 1, 0)
    r_h, k_h, v_h = hd(r), hd(k), hd(v)
    kk = hd(k * k_k)
    kk = kk * lax.rsqrt(jnp.maximum(jnp.sum(kk * kk, axis=-1, keepdims=True), 1e-24))
    ys = []
    for d, (wd, ad) in enumerate(((wd_f, ad_f), (wd_b, ad_b))):
        w = jnp.exp(-jnp.exp(-jax.nn.softplus(-(w0[d] + jnp.tanh(wd) @ wup[d])) - 0.5))
        a = jax.nn.sigmoid(a0[d] + ad @ aup[d])
        kt = k * (1.0 + (a - 1.0) * k_a)
        xs = tuple(tm(z) for z in (r_h, hd(w), kk, kk * hd(a), hd(kt), v_h))
        S0 = jnp.zeros((B, A_HEADS, A_HD, A_HD), jnp.float32)
        S_ctx, y_ctx = _rwkv7_scan(S0, tuple(z[:L] for z in xs), d == 1)
        _, y_lat = _rwkv7_scan(S_ctx, tuple(z[L:] for z in xs), d == 1)
        ys.append(jnp.concatenate([y_ctx, y_lat], axis=0))
    y = _head_ln(jnp.moveaxis(ys[0] + ys[1], 0, 1), ln_g, ln_b, A_LN_EPS)
    bonus = jnp.sum(r_h * k_h * r_k, axis=-1, keepdims=True) * v_h
    g = jax.nn.sigmoid(gd) @ gup
    return ((y + bonus).reshape(B, T, W_BR) * g).astype(dt)


def _chunk_mix(vs, ws, bs):
    B, T, _ = vs.shape
    vg = vs.reshape(B, T // B_CHUNK, B_CHUNK, B_GROUPS, B_GD)
    s = jnp.einsum("gpq,bnqgc->bnpgc", ws, vg) + bs.T[None, None, :, :, None]
    return s.reshape(B, T, W_BR)


def _gmlp_branch(pb, L, ws, bs, ln_g, ln_b):
    z = jax.nn.gelu(pb)
    u, v = _split(z, [W_BR, W_BR])
    vf = v.astype(jnp.float32)
    mu = jnp.mean(vf, axis=-1, keepdims=True)
    vc = vf - mu
    vn = (vc * lax.rsqrt(jnp.mean(vc * vc, axis=-1, keepdims=True) + 1e-5) * ln_g + ln_b).astype(pb.dtype)
    s = jnp.concatenate([_chunk_mix(vn[:, :L], ws, bs), _chunk_mix(vn[:, L:], ws, bs)], axis=1)
    return (u * s).astype(pb.dtype)


def _mlstm_chunked(state, q, k, v, ig, lf, reverse):
    if reverse:
        q, k, v, ig, lf = [jnp.flip(z, 1) for z in (q, k, v, ig, lf)]
    tril = jnp.tril(jnp.ones((C_CHUNK, C_CHUNK), bool))

    def body(carry, inp):
        Cm, n, m = carry
        qc, kc, vc, ic, fc = inp
        ic, fc = ic[..., 0], fc[..., 0]
        b = jnp.cumsum(fc, axis=-1)
        Dlog = jnp.where(tril, b[..., :, None] - b[..., None, :] + ic[..., None, :], -jnp.inf)
        inter = b + m[..., None]
        m_t = jnp.maximum(inter, jnp.max(Dlog, axis=-1))
        Dw = jnp.exp(Dlog - m_t[..., None])
        iw = jnp.exp(inter - m_t)
        s = jnp.einsum("bhtd,bhsd->bhts", qc, kc) * Dw
        num = jnp.einsum("bhts,bhsv->bhtv", s, vc) + iw[..., None] * jnp.einsum("bhvk,bhtk->bhtv", Cm, qc)
        den = jnp.sum(s, axis=-1) + iw * jnp.einsum("bhk,bhtk->bht", n, qc)
        den = jnp.maximum(jnp.abs(den), jnp.exp(-m_t))
        h = num / den[..., None]
        bL = b[..., -1]
        gl = bL[..., None] - b + ic
        m_new = jnp.maximum(bL + m, jnp.max(gl, axis=-1))
        sw = jnp.exp(gl - m_new[..., None])
        dec = jnp.exp(bL + m - m_new)
        Cm = dec[..., None, None] * Cm + jnp.einsum("bhs,bhsv,bhsk->bhvk", sw, vc, kc)
        n = dec[..., None] * n + jnp.einsum("bhs,bhsk->bhk", sw, kc)
        return (Cm, n, m_new), h

    xs = (_to_chunks(q, C_CHUNK), _to_chunks(k, C_CHUNK), _to_chunks(v, C_CHUNK),
          _to_chunks(ig[..., None], C_CHUNK), _to_chunks(lf[..., None], C_CHUNK))
    state, h = lax.scan(body, state, xs)
    h = _from_chunks(h)
    if reverse:
        h = jnp.flip(h, 1)
    return h, state


def _mlstm_branch(pmc, L, conv_w, conv_b, gate_b, ln_g):
    dt = pmc.dtype
    pmc = pmc.astype(jnp.float32)
    qk, v, o, gts = _split(pmc, [2 * W_BR, W_BR, W_BR, 4 * C_HEADS])
    qk = jax.nn.silu(jnp.concatenate([_conv_centred(qk[:, :L], conv_w, conv_b),
                                      _conv_centred(qk[:, L:], conv_w, conv_b)], axis=1))
    q, k = _split(qk, [W_BR, W_BR])
    B, T, _ = pmc.shape
    q = q.reshape(B, T, C_HEADS, C_HD)
    k = k.reshape(B, T, C_HEADS, C_HD) * C_HD ** -0.5
    v = v.reshape(B, T, C_HEADS, C_HD)
    gts = gts.reshape(B, T, 2, 2, C_HEADS) + gate_b
    hs = []
    for d in range(2):
        ig = gts[:, :, d, 0]
        lf = jax.nn.log_sigmoid(gts[:, :, d, 1])
        st0 = (jnp.zeros((B, C_HEADS, C_HD, C_HD), jnp.float32),
               jnp.zeros((B, C_HEADS, C_HD), jnp.float32),
               jnp.zeros((B, C_HEADS), jnp.float32))
        h_c, st = _mlstm_chunked(st0, q[:, :L], k[:, :L], v[:, :L], ig[:, :L], lf[:, :L], d == 1)
        h_x, _ = _mlstm_chunked(st, q[:, L:], k[:, L:], v[:, L:], ig[:, L:], lf[:, L:], d == 1)
        hs.append(jnp.concatenate([h_c, h_x], axis=1))
    h = _head_ln(hs[0] + hs[1], ln_g, None, 1e-5)
    return (h.reshape(B, T, W_BR) * jax.nn.sigmoid(o)).astype(dt)


def _gla_chunked(S, q, k, v, la, reverse):
    if reverse:
        q, k, v, la = [jnp.flip(z, 1) for z in (q, k, v, la)]
    tril = jnp.tril(jnp.ones((D_CHUNK, D_CHUNK), bool))

    def body(S, inp):
        qc, kc, vc, lac = inp
        bc = jnp.cumsum(lac, axis=2)
        rel = jnp.where(tril[:, :, None], bc[:, :, :, None, :] - bc[:, :, None, :, :], -jnp.inf)
        A = jnp.einsum("bhtk,bhsk,bhtsk->bhts", qc, kc, jnp.exp(rel))
        o = jnp.einsum("bhts,bhsv->bhtv", A, vc) + jnp.einsum("bhtk,bhkv->bhtv", qc * jnp.exp(bc), S)
        bL = bc[:, :, -1]
        S = jnp.exp(bL)[..., None] * S + jnp.einsum("bhsk,bhsv->bhkv", kc * jnp.exp(bL[:, :, None] - bc), vc)
        return S, o

    xs = (_to_chunks(q, D_CHUNK), _to_chunks(k, D_CHUNK), _to_chunks(v, D_CHUNK), _to_chunks(la, D_CHUNK))
    S, o = lax.scan(body, S, xs)
    o = _from_chunks(o)
    if reverse:
        o = jnp.flip(o, 1)
    return o, S


def _gla_branch(pd, L, aup, ab, ln_g):
    dt = pd.dtype
    pd = pd.astype(jnp.float32)
    q, k, v, g, ad_f, ad_b = _split(pd, [D_HEADS * D_DK, D_HEADS * D_DK, W_BR, W_BR, D_RANK, D_RANK])
    B, T, _ = pd.shape
    q = q.reshape(B, T, D_HEADS, D_DK) * D_DK ** -0.5
    k = k.reshape(B, T, D_HEADS, D_DK)
    v = v.reshape(B, T, D_HEADS, D_DV)
    outs = []
    for d, ad in enumerate((ad_f, ad_b)):
        la = (jax.nn.log_sigmoid(ad @ aup[d] + ab[d]) / D_TAU).reshape(B, T, D_HEADS, D_DK)
        S0 = jnp.zeros((B, D_HEADS, D_DK, D_DV), jnp.float32)
        o_c, S_c = _gla_chunked(S0, q[:, :L], k[:, :L], v[:, :L], la[:, :L], d == 1)
        o_x, _ = _gla_chunked(S_c, q[:, L:], k[:, L:], v[:, L:], la[:, L:], d == 1)
        outs.append(jnp.concatenate([o_c, o_x], axis=1))
    o = _head_rms(outs[0] + outs[1], ln_g, 1e-6)
    return (o.reshape(B, T, W_BR) * jax.nn.silu(g)).astype(dt)


def _merge(hs, pg, br_w, out_w):
    B, T, _ = pg.shape
    gates = jax.nn.sigmoid(pg).reshape(B, T, N_BR, D_MODEL)
    y = gates[:, :, 0] * (hs[0] @ br_w[0])
    for n in range(1, N_BR):
        y = y + gates[:, :, n] * (hs[n] @ br_w[n])
    return y @ out_w


def setup_inputs(seed: int = 0) -> dict:
    key = jax.random.key(seed)
    ks = iter(jax.random.split(key, 40))
    f32 = jnp.float32

    def nrm(shape, s):
        return jax.random.normal(next(ks), shape, f32) * s

    Dm = D_MODEL
    x = nrm((BATCH, SEQ, Dm), 1.0)
    c = nrm((BATCH, Dm), 1.0)
    ctx = nrm((BATCH, CTX_LEN, Dm), 1.0)
    c_ctx = nrm((Dm,), 1.0)
    ada_w = nrm((DEPTH, Dm, N_MOD * Dm), 0.5 * Dm ** -0.5)
    ada_b = nrm((DEPTH, N_MOD * Dm), 0.02)
    norm_g = 1.0 + nrm((DEPTH, 3, Dm), 0.05)
    ffn_w1 = nrm((DEPTH, 2, Dm, D_FF), Dm ** -0.5)
    ffn_w3 = nrm((DEPTH, 2, Dm, D_FF), Dm ** -0.5)
    ffn_w2 = nrm((DEPTH, 2, D_FF, Dm), D_FF ** -0.5)
    in_w = nrm((DEPTH, Dm, D_IN), Dm ** -0.5)
    in_b = nrm((DEPTH, D_IN), 0.02)
    a_mu = jax.random.uniform(next(ks), (DEPTH, A_COLS), f32)
    a_w0 = jnp.linspace(-6.5, -1.5, W_BR, dtype=f32) + nrm((DEPTH, 2, W_BR), 0.1)
    a_wup = nrm((DEPTH, 2, A_W_RANK, W_BR), 0.1)
    a_a0 = nrm((DEPTH, 2, W_BR), 0.1)
    a_aup = nrm((DEPTH, 2, A_A_RANK, W_BR), 0.1)
    a_gup = nrm((DEPTH, A_G_RANK, W_BR), A_G_RANK ** -0.5)
    a_kk = 0.85 + nrm((DEPTH, W_BR), 0.05)
    a_ka = 1.0 + nrm((DEPTH, W_BR), 0.05)
    a_rk = nrm((DEPTH, A_HEADS, A_HD), 0.1)
    a_ln_g = 1.0 + nrm((DEPTH, W_BR), 0.05)
    a_ln_b = nrm((DEPTH, W_BR), 0.02)
    b_ws = nrm((DEPTH, B_GROUPS, B_CHUNK, B_CHUNK), 0.5 * B_CHUNK ** -0.5)
    b_bs = 1.0 + nrm((DEPTH, B_GROUPS, B_CHUNK), 0.1)
    b_ln_g = 1.0 + nrm((DEPTH, W_BR), 0.05)
    b_ln_b = nrm((DEPTH, W_BR), 0.02)
    c_conv_w = nrm((DEPTH, C_CONV, 2 * W_BR), C_CONV ** -0.5)
    c_conv_b = nrm((DEPTH, 2 * W_BR), 0.02)
    c_gate_b = jnp.stack([nrm((DEPTH, 2, C_HEADS), 0.1),
                          jnp.linspace(3.0, 6.0, C_HEADS, dtype=f32) + nrm((DEPTH, 2, C_HEADS), 0.1)],
                         axis=2)
    c_ln_g = 1.0 + nrm((DEPTH, W_BR), 0.05)
    d_aup = nrm((DEPTH, 2, D_RANK, D_HEADS * D_DK), D_RANK ** -0.5)
    d_ab = nrm((DEPTH, 2, D_HEADS * D_DK), 0.1)
    d_ln_g = 1.0 + nrm((DEPTH, W_BR), 0.05)
    br_w = nrm((DEPTH, N_BR, W_BR, Dm), W_BR ** -0.5)
    out_w = nrm((DEPTH, Dm, Dm), Dm ** -0.5)
    final_g = 1.0 + nrm((Dm,), 0.05)
    return {"x": x, "c": c, "ctx": ctx, "c_ctx": c_ctx, "ada_w": ada_w, "ada_b": ada_b,
            "norm_g": norm_g, "ffn_w1": ffn_w1, "ffn_w3": ffn_w3, "ffn_w2": ffn_w2,
            "in_w": in_w, "in_b": in_b, "a_mu": a_mu, "a_w0": a_w0, "a_wup": a_wup, "a_a0": a_a0,
            "a_aup": a_aup, "a_gup": a_gup, "a_kk": a_kk, "a_ka": a_ka, "a_rk": a_rk,
            "a_ln_g": a_ln_g, "a_ln_b": a_ln_b, "b_ws": b_ws, "b_bs": b_bs, "b_ln_g": b_ln_g,
            "b_ln_b": b_ln_b, "c_conv_w": c_conv_w, "c_conv_b": c_conv_b, "c_gate_b": c_gate_b,
            "c_ln_g": c_ln_g, "d_aup": d_aup, "d_ab": d_ab, "d_ln_g": d_ln_g, "br_w": br_w,
            "out_w": out_w, "final_g": final_g}


def reference(x, c, ctx, c_ctx, ada_w, ada_b, norm_g, ffn_w1, ffn_w3, ffn_w2, in_w, in_b,
              a_mu, a_w0, a_wup, a_a0, a_aup, a_gup, a_kk, a_ka, a_rk, a_ln_g, a_ln_b,
              b_ws, b_bs, b_ln_g, b_ln_b, c_conv_w, c_conv_b, c_gate_b, c_ln_g,
              d_aup, d_ab, d_ln_g, br_w, out_w, final_g):
    B = x.shape[0]
    L = ctx.shape[1]
    h = jnp.concatenate([ctx, x], axis=1)
    cond = jax.nn.silu(jnp.concatenate([c, c_ctx[None]], axis=0))
    for i in range(DEPTH):
        mod = (cond @ ada_w[i] + ada_b[i]).reshape(B + 1, N_MOD, D_MODEL)
        hn = _modulate(_rms(h, norm_g[i, 0]), mod[:, 0], mod[:, 1], L)
        h = _gated_add(h, 0.5 * _swiglu(hn, ffn_w1[i, 0], ffn_w3[i, 0], ffn_w2[i, 0]), mod[:, 2], L)
        hn = _modulate(_rms(h, norm_g[i, 1]), mod[:, 3], mod[:, 4], L)
        p = hn @ in_w[i] + in_b[i]
        pa, pb, pmc, pd, pg = _split(p, [A_COLS, B_COLS, C_COLS, D_COLS, N_BR * D_MODEL])
        hs = [_rwkv7_branch(pa, L, a_mu[i], a_w0[i], a_wup[i], a_a0[i], a_aup[i], a_gup[i],
                            a_kk[i], a_ka[i], a_rk[i], a_ln_g[i], a_ln_b[i]),
              _gmlp_branch(pb, L, b_ws[i], b_bs[i], b_ln_g[i], b_ln_b[i]),
              _mlstm_branch(pmc, L, c_conv_w[i], c_conv_b[i], c_gate_b[i], c_ln_g[i]),
              _gla_branch(pd, L, d_aup[i], d_ab[i], d_ln_g[i])]
        if i == DEPTH - 1:
            hs = [z[:, L:] for z in hs]
            pg = pg[:, L:]
            h = h[:, L:]
            L = 0
        h = _gated_add(h, _merge(hs, pg, br_w[i], out_w[i]), mod[:, 5], L)
        hn = _modulate(_rms(h, norm_g[i, 2]), mod[:, 6], mod[:, 7], L)
        h = _gated_add(h, 0.5 * _swiglu(hn, ffn_w1[i, 1], ffn_w3[i, 1], ffn_w2[i, 1]), mod[:, 8], L)
    return _rms(h, final_g)
```

```python
import contextlib
import numpy as np
import concourse.bass as bass
import concourse.mybir as mybir
from concourse.bass_utils import run_bass_kernel_spmd

F32 = mybir.dt.float32
BF16 = mybir.dt.bfloat16
AF = mybir.ActivationFunctionType
ALU = mybir.AluOpType
AX = mybir.AxisListType

ENGS = ("pe", "dve", "act", "pool", "sp")
NDMASEM = 24

T = 2304
LC = 256
D = 2048
NCH = 16
DFF = 5632
NFC = 44
DEPTH = 4
A_COLS, B_COLS, C_COLS, D_COLS = 1920, 1024, 2064, 1568
OFF_A, OFF_B, OFF_C, OFF_D, OFF_G = 0, 1920, 2944, 5008, 6576
D_IN = 14768
TG = [(0, 256), (256, 512), (768, 512), (1280, 512), (1792, 512)]
NT = 18


class Sched:
    uid = 0

    def __init__(self, nc):
        self.nc = nc
        self.ops = {e: [] for e in ENGS}
        self.cnt = {e: 0 for e in ENGS}
        self.dcnt = {e: 0 for e in ENGS}
        self.last_w = {}
        self.readers = {}
        self.seen = {e: {} for e in ENGS}

    def _deps(self, reads, writes):
        deps = set()
        for k in reads:
            if k in self.last_w:
                deps.add(self.last_w[k])
        for k in writes:
            if k in self.last_w:
                deps.add(self.last_w[k])
            for r in self.readers.get(k, ()):
                deps.add(r)
        return deps

    def _update(self, tok, reads, writes):
        for k in writes:
            self.last_w[k] = tok
            self.readers[k] = []
        for k in reads:
            self.readers.setdefault(k, []).append(tok)

    def _waits(self, eng, deps):
        waits = []
        seen = self.seen[eng]
        best = {}
        for d in deps:
            if d[0] == "c":
                _, e, idx = d
                if e == eng and e == "pe":
                    continue
                key = ("c", e)
                if best.get(key, 0) < idx:
                    best[key] = idx
            else:
                _, q, i = d
                slot = i % NDMASEM
                val = 16 * (i // NDMASEM + 1)
                key = ("d", q, slot)
                if best.get(key, 0) < val:
                    best[key] = val
        for key, val in best.items():
            if seen.get(key, 0) >= val:
                continue
            seen[key] = val
            waits.append((key, val))
        return waits

    def op(self, eng, fn, reads=(), writes=()):
        deps = self._deps(reads, writes)
        waits = self._waits(eng, deps)
        self.cnt[eng] += 1
        tok = ("c", eng, self.cnt[eng])
        self.ops[eng].append((waits, fn, "c", None))
        self._update(tok, reads, writes)
        return tok

    def dma(self, q, fn, reads=(), writes=()):
        deps = self._deps(reads, writes)
        i = self.dcnt[q]
        if i >= NDMASEM:
            deps.add(("d", q, i - NDMASEM))
        waits = self._waits(q, deps)
        self.dcnt[q] += 1
        tok = ("d", q, i)
        self.ops[q].append((waits, fn, "d", i % NDMASEM))
        self._update(tok, reads, writes)
        return tok

    def emit(self):
        nc = self.nc
        Sched.uid += 1
        u = Sched.uid
        csem = {e: nc.alloc_semaphore(f"c{u}_{e}") for e in ENGS}
        dsem = {}
        for q in ENGS:
            for s in range(min(NDMASEM, self.dcnt[q])):
                dsem[(q, s)] = nc.alloc_semaphore(f"d{u}_{q}_{s}")
        with nc.Block() as block:

            def run(engname, eng):
                for waits, fn, kind, slot in self.ops[engname]:
                    for key, val in waits:
                        if key[0] == "c":
                            eng.wait_ge(csem[key[1]], val)
                        else:
                            eng.wait_ge(dsem[(key[1], key[2])], val)
                    ins = fn(eng)
                    if kind == "c":
                        ins.then_inc(csem[engname], 1)
                    else:
                        ins.then_inc(dsem[(engname, slot)], 16)
                n = self.dcnt[engname]
                for s in range(min(NDMASEM, n)):
                    c = (n - 1 - s) // NDMASEM + 1
                    eng.wait_ge(dsem[(engname, s)], 16 * c)

            if self.ops["sp"]:
                @block.sync
                def _(e):
                    run("sp", e)
            if self.ops["pool"]:
                @block.gpsimd
                def _(e):
                    run("pool", e)
            if self.ops["act"]:
                @block.scalar
                def _(e):
                    run("act", e)
            if self.ops["dve"]:
                @block.vector
                def _(e):
                    run("dve", e)
            if self.ops["pe"]:
                @block.tensor
                def _(e):
                    run("pe", e)
        nc.all_engine_barrier()
        nc.clear_and_free_semaphores(list(csem.values()) + list(dsem.values()))
        nc.all_engine_barrier()


class Tile:
    def __init__(self, t, key):
        self.t = t
        self.k = key

    def __getitem__(self, idx):
        return self.t[idx]


class Rot:
    def __init__(self, tiles):
        self.tiles = tiles
        self.i = 0

    def next(self):
        t = self.tiles[self.i % len(self.tiles)]
        self.i += 1
        return t


class Phase:
    count = 0

    def __init__(self, nc):
        Phase.count += 1
        self.nc = nc
        self.id = Phase.count
        self.n = 0
        self.st = contextlib.ExitStack()
        self.S = Sched(nc)

    def sb(self, shape, dt=F32):
        self.n += 1
        name = f"p{self.id}_{self.n}"
        return Tile(self.st.enter_context(self.nc.sbuf_tensor(name, list(shape), dt)), name)

    def ps(self, shape, dt=F32):
        self.n += 1
        name = f"q{self.id}_{self.n}"
        return Tile(self.st.enter_context(self.nc.psum_tensor(name, list(shape), dt)), name)

    def psq(self, n, width=128, dt=F32):
        per = (512 if dt == F32 else 1024) // width
        tiles = []
        bank = None
        for i in range(n):
            if i % per == 0:
                bank = self.ps([128, 512 if dt == F32 else 1024], dt)
            j = i % per
            tiles.append(Tile(bank.t[:, j * width:(j + 1) * width], bank.k))
        return Rot(tiles)

    def rot(self, n, shape, dt=F32, psum=False):
        return Rot([(self.ps if psum else self.sb)(shape, dt) for _ in range(n)])

    def end(self):
        self.S.emit()
        self.st.close()
        self.nc.all_engine_barrier()

    def mm(self, out, lhsT, rhs, start, stop, r, w):
        self.S.op("pe", lambda e: e.matmul(out, lhsT=lhsT, rhs=rhs, start=start, stop=stop), r, w)

    def tr(self, out, in_, ident, r, w):
        self.S.op("pe", lambda e: e.transpose(out, in_, ident), r, w)

    def act(self, out, in_, func, r, w, bias=None, scale=None, accum=None):
        kw = {}
        if bias is not None:
            kw["bias"] = bias
        if scale is not None:
            kw["scale"] = scale
        if accum is not None:
            kw["accum_out"] = accum
        self.S.op("act", lambda e: e.activation(out=out, in_=in_, func=func, **kw), r, w)

    def tt(self, out, a, b, op, r, w, eng="dve"):
        self.S.op(eng, lambda e: e.tensor_tensor(out=out, in0=a, in1=b, op=op), r, w)

    def ts(self, out, a, s1, s2, op0, op1, r, w, eng="dve"):
        if s2 is None:
            self.S.op(eng, lambda e: e.tensor_scalar(out=out, in0=a, scalar1=s1, scalar2=None, op0=op0), r, w)
        else:
            self.S.op(eng, lambda e: e.tensor_scalar(out=out, in0=a, scalar1=s1, scalar2=s2, op0=op0, op1=op1), r, w)

    def stt(self, out, a, s, b, op0, op1, r, w, eng="dve"):
        self.S.op(eng, lambda e: e.scalar_tensor_tensor(out=out, in0=a, scalar=s, in1=b, op0=op0, op1=op1), r, w)

    def cp(self, out, in_, r, w, eng="dve"):
        self.S.op(eng, lambda e: e.tensor_copy(out=out, in_=in_), r, w)

    def recip(self, out, in_, r, w):
        self.S.op("dve", lambda e: e.reciprocal(out=out, in_=in_), r, w)

    def memset(self, out, val, r, w, eng="dve"):
        self.S.op(eng, lambda e: e.memset(out, val), r, w)

    def dma(self, out, in_, r, w, q="sp"):
        self.S.dma(q, lambda e: e.dma_start(out=out, in_=in_), r, w)


class Ctx:
    pass


def fm(ap2d):
    return ap2d.rearrange("(c p) t -> p c t", p=128)


def phase_init(G):
    nc = G.nc
    P = Phase(nc)
    ident = G.ident
    xin = P.rot(2, [128, D])
    stg = P.rot(2, [128, NCH, 128])
    pst = P.rot(2, [128, 512], psum=True)
    for tt in range(NT):
        xi = xin.next()
        src = G.ctx_b[tt * 128:(tt + 1) * 128, :] if tt < 2 else G.x_b[(tt - 2) * 128:(tt - 1) * 128, :]
        P.dma(xi[:], src, [], [xi.k])
        so = stg.next()
        for c4 in range(4):
            pt = pst.next()
            for j in range(4):
                c = c4 * 4 + j
                P.tr(pt[:, j * 128:(j + 1) * 128], xi[:, c * 128:(c + 1) * 128], ident[:], [xi.k, "ident"], [pt.k])
            P.cp(so[:, c4 * 4:(c4 + 1) * 4, :], pt[:].rearrange("p (j t) -> p j t", j=4), [pt.k], [so.k],
                 eng="dve" if c4 % 2 == 0 else "act_copy")
        P.dma(fm(G.hT_d)[:, :, tt * 128:(tt + 1) * 128], so[:], [so.k], ["hT_d"], q="pool")
    craw = P.sb([32, 128])
    P.dma(craw[0:16, :], G.c_b.rearrange("o (c p) -> (o c) p", p=128), [], [craw.k])
    P.dma(craw[16:32, :], G.c_ctx.rearrange("o (c p) -> (o c) p", p=128), [], [craw.k])
    pc = P.ps([128, 32])
    P.tr(pc[:], craw[:], ident[0:32, 0:32], [craw.k, "ident"], [pc.k])
    condT = P.sb([128, NCH, 2])
    for r in range(2):
        P.act(condT[:, :, r], pc[:, r * 16:(r + 1) * 16], AF.Silu, [pc.k], [condT.k])
    wst = P.rot(3, [128, D])
    pm = P.rot(2, [128, NCH, 2], psum=True)
    maccs = P.rot(2, [128, NCH, 2])
    braw = P.rot(2, [72, 128])
    pb = P.rot(1, [128, 72], psum=True)
    badT = P.sb([128, DEPTH, 144])
    for l in range(DEPTH):
        for hlf in range(2):
            br = braw.next()
            P.dma(br[:], G.ada_b[l:l + 1, hlf * 9216:(hlf + 1) * 9216].rearrange("o (j p) -> (o j) p", p=128), [], [br.k])
            pbt = pb.next()
            P.tr(pbt[:], br[:], ident[0:72, 0:72], [br.k, "ident"], [pbt.k])
            P.cp(badT[:, l, hlf * 72:(hlf + 1) * 72], pbt[:], [pbt.k], [badT.k])
        for n in range(9):
            macc = maccs.next()
            for c in range(NCH):
                w = wst.next()
                P.dma(w[:], G.ada_w[l, c * 128:(c + 1) * 128, n * D:(n + 1) * D], [], [w.k])
                pmt = pm.next()
                for cc in range(NCH):
                    P.mm(pmt[:, cc, :], w[:, cc * 128:(cc + 1) * 128], condT[:, c, :], True, True,
                         [w.k, condT.k], [pmt.k])
                if c == 0:
                    P.cp(macc[:], pmt[:], [pmt.k], [macc.k])
                else:
                    P.tt(macc[:], macc[:], pmt[:], ALU.add, [pmt.k, macc.k], [macc.k])
            for r in range(2):
                P.tt(G.modT[:, l, n * 16:(n + 1) * 16, r], macc[:, :, r], badT[:, l, n * 16:(n + 1) * 16], ALU.add,
                     [macc.k, badT.k], ["modT"])
    png = P.ps([128, DEPTH * 3 * NCH])
    ngv = G.norm_g.rearrange("l s (c p) -> (l s c) p", p=128)
    for half in range(2):
        ngraw = P.sb([96, 128])
        P.dma(ngraw[:], ngv[half * 96:(half + 1) * 96, :], [], [ngraw.k])
        P.tr(png[:, half * 96:(half + 1) * 96], ngraw[:], ident[0:96, 0:96], [ngraw.k, "ident"], [png.k])
    for l in range(DEPTH):
        for s in range(3):
            for r in range(2):
                P.stt(G.G1[:, l, s, :, r], G.modT[:, l, (3 * s + 1) * 16:(3 * s + 2) * 16, r], 1.0,
                      png[:, (l * 3 + s) * 16:(l * 3 + s + 1) * 16], ALU.add, ALU.mult, ["modT", png.k], ["G1"])
                P.ts(G.gate[:, l, s, :, r], G.modT[:, l, (3 * s + 2) * 16:(3 * s + 3) * 16, r],
                     0.5 if s != 1 else 1.0, None, ALU.mult, None, ["modT"], ["gate"])
    fgraw = P.sb([16, 128])
    P.dma(fgraw[:], G.final_g.rearrange("o (c p) -> (o c) p", p=128), [], [fgraw.k])
    P.tr(pc[:, 0:16], fgraw[:], ident[0:16, 0:16], [fgraw.k, "ident"], [pc.k])
    P.cp(G.fgT[:], pc[:, 0:16], [pc.k], ["fgT"])
    P.end()


def norm_to_hnT(P, G, hnT, g1_of_chunk, shift_of_chunk):
    hin = P.rot(2, [128, NCH, 512])
    sq = P.rot(2, [128, 512])
    pss = P.rot(2, [128, 512], psum=True)
    rstd = P.rot(2, [128, 512])
    tmp = P.rot(3, [128, 512])
    for (t0, n) in TG:
        row = 1 if t0 < LC else 0
        hi = hin.next()
        P.dma(hi[:, :, 0:n], fm(G.hT_d)[:, :, t0:t0 + n], ["hT_d"], [hi.k])
        ps = pss.next()
        for c in range(NCH):
            s = sq.next()
            P.act(s[:, 0:n], hi[:, c, 0:n], AF.Square, [hi.k], [s.k])
            P.mm(ps[:, 0:n], G.ones[:], s[:, 0:n], c == 0, c == NCH - 1, [s.k, "ones"], [ps.k])
        rs = rstd.next()
        P.act(rs[:, 0:n], ps[:, 0:n], AF.Sqrt, [ps.k], [rs.k], bias=G.epsc[:, 0:1], scale=1.0 / D)
        P.recip(rs[:, 0:n], rs[:, 0:n], [rs.k], [rs.k])
        for c in range(NCH):
            tm = tmp.next()
            P.stt(tm[:, 0:n], hi[:, c, 0:n], g1_of_chunk(c, row), rs[:, 0:n], ALU.mult, ALU.mult,
                  [hi.k, rs.k, "G1"], [tm.k])
            P.act(hnT[:, c, t0:t0 + n], tm[:, 0:n], AF.Identity, [tm.k, "modT"], [hnT.k + f"_{t0}"],
                  bias=shift_of_chunk(c, row))


def phase_norm(G, hnT, l, s):
    P = Phase(G.nc)
    norm_to_hnT(P, G, hnT,
                lambda c, row: G.G1[:, l, s, c, row:row + 1],
                lambda c, row: G.modT[:, l, 3 * s * 16 + c, row:row + 1])
    P.end()


def phase_ffn_up(G, hnT, l, s, which):
    nc = G.nc
    P = Phase(nc)
    FB = 256
    wstg = P.rot(4, [128, NCH, FB])
    wbf = P.rot(4, [128, NCH, FB], BF16)
    ps1 = P.rot(2, [128, 512], psum=True)
    ps3 = P.rot(2, [128, 512], psum=True)
    sil = P.rot(2, [128, 512])
    h1 = P.rot(3, [128, 512], BF16)
    hkeys = [hnT.k + f"_{t0}" for (t0, n) in TG]
    for fb in range(DFF // FB):
        wb = []
        for wsrc in (G.ffn_w1, G.ffn_w3):
            st_ = wstg.next()
            P.dma(st_[:], fm(wsrc[l, which])[:, :, fb * FB:(fb + 1) * FB], [], [st_.k])
            b = wbf.next()
            P.cp(b[:], st_[:], [st_.k], [b.k], eng="pool")
            wb.append(b)
        for gi, (t0, n) in enumerate(TG):
            for fc in range(FB // 128):
                p1 = ps1.next()
                p3 = ps3.next()
                for c in range(NCH):
                    P.mm(p1[:, 0:n], wb[0][:, c, fc * 128:(fc + 1) * 128], hnT[:, c, t0:t0 + n], c == 0, c == NCH - 1,
                         [wb[0].k, hkeys[gi]], [p1.k])
                for c in range(NCH):
                    P.mm(p3[:, 0:n], wb[1][:, c, fc * 128:(fc + 1) * 128], hnT[:, c, t0:t0 + n], c == 0, c == NCH - 1,
                         [wb[1].k, hkeys[gi]], [p3.k])
                sl = sil.next()
                P.act(sl[:, 0:n], p1[:, 0:n], AF.Silu, [p1.k], [sl.k])
                ho = h1.next()
                P.tt(ho[:, 0:n], p3[:, 0:n], sl[:, 0:n], ALU.mult, [p3.k, sl.k], [ho.k])
                f0 = fb * FB + fc * 128
                P.dma(G.h1T_d[f0:f0 + 128, t0:t0 + n], ho[:, 0:n], [ho.k], [], q="pool")
    P.end()


def phase_ffn_down(G, l, s, which):
    nc = G.nc
    P = Phase(nc)
    wstg = P.rot(3, [128, 4, 512])
    w2q = P.sb([128, NFC, 512], BF16)
    h1t = P.rot(2, [128, NFC, 512], BF16)
    hold = P.rot(2, [128, 4, 512])
    hnew = P.rot(2, [128, 4, 512])
    pso = P.rot(4, [128, 512], psum=True)
    for dq in range(4):
        for f4 in range(NFC // 4):
            st_ = wstg.next()
            P.dma(st_[:], fm(G.ffn_w2[l, which])[:, f4 * 4:(f4 + 1) * 4, dq * 512:(dq + 1) * 512], [], [st_.k])
            P.cp(w2q[:, f4 * 4:(f4 + 1) * 4, :], st_[:], [st_.k], [w2q.k + f"_{f4}"], eng="pool")
        wkeys = [w2q.k + f"_{f4}" for f4 in range(NFC // 4)]
        for (t0, n) in TG:
            row = 1 if t0 < LC else 0
            ht = h1t.next()
            for q4 in range(4):
                P.dma(ht[:, q4 * 11:(q4 + 1) * 11, 0:n], fm(G.h1T_d)[:, q4 * 11:(q4 + 1) * 11, t0:t0 + n], [], [ht.k])
            ho = hold.next()
            P.dma(ho[:, :, 0:n], fm(G.hT_d)[:, dq * 4:(dq + 1) * 4, t0:t0 + n], ["hT_d"], [ho.k])
            hn = hnew.next()
            for dc in range(4):
                ps = pso.next()
                for fc in range(NFC):
                    P.mm(ps[:, 0:n], w2q[:, fc, dc * 128:(dc + 1) * 128], ht[:, fc, 0:n], fc == 0, fc == NFC - 1,
                         [wkeys[fc // 4], ht.k], [ps.k])
                P.stt(hn[:, dc, 0:n], ps[:, 0:n], G.gate[:, l, s, dq * 4 + dc, row:row + 1], ho[:, dc, 0:n],
                      ALU.mult, ALU.add, [ps.k, ho.k, "gate"], [hn.k])
            P.dma(fm(G.hT_d)[:, dq * 4:(dq + 1) * 4, t0:t0 + n], hn[:, :, 0:n], [hn.k], ["hT_d"], q="pool")
    P.end()


def in_segments():
    segs = []
    c = 0
    while c < OFF_G:
        n = min(128, OFF_G - c)
        segs.append((c, n, False))
        c += n
    while c < D_IN:
        segs.append((c, 128, True))
        c += 128
    return segs


def phase_inproj(G, hnT, l):
    P = Phase(G.nc)
    segs = in_segments()
    inbT = P.sb([128, len(segs)])
    pbt = P.ps([128, 64])
    braw = P.rot(2, [64, 128])
    r1 = braw.next()
    P.dma(r1[0:51, :], G.in_b[l:l + 1, 0:6528].rearrange("o (j p) -> (o j) p", p=128), [], [r1.k])
    P.tr(pbt[:, 0:51], r1[0:51, :], G.ident[0:51, 0:51], [r1.k, "ident"], [pbt.k])
    P.cp(inbT[:, 0:51], pbt[:, 0:51], [pbt.k], [inbT.k])
    P.dma(inbT[0:48, 51:52], G.in_b[l:l + 1, 6528:6576].rearrange("o p -> p o"), [], [inbT.k])
    r2 = braw.next()
    P.dma(r2[:], G.in_b[l:l + 1, OFF_G:D_IN].rearrange("o (j p) -> (o j) p", p=128), [], [r2.k])
    P.tr(pbt[:, 0:64], r2[:], G.ident[0:64, 0:64], [r2.k, "ident"], [pbt.k])
    P.cp(inbT[:, 52:116], pbt[:, 0:64], [pbt.k], [inbT.k])
    wstg = P.rot(3, [128, NCH, 256])
    wbf = P.rot(3, [128, NCH, 256], BF16)
    pso = P.rot(4, [128, 512], psum=True)
    ot = P.rot(4, [128, 512])
    hkeys = ["hnT" + f"_{t0}" for (t0, n) in TG]
    si = 0
    while si < len(segs):
        blk = [si]
        if si + 1 < len(segs) and segs[si][1] == 128:
            blk.append(si + 1)
        c0 = segs[blk[0]][0]
        w = sum(segs[j][1] for j in blk)
        st_ = wstg.next()
        P.dma(st_[:, :, 0:w], fm(G.in_w[l])[:, :, c0:c0 + w], [], [st_.k])
        b = wbf.next()
        P.cp(b[:, :, 0:w], st_[:, :, 0:w], [st_.k], [b.k], eng="pool")
        for gi, (t0, n) in enumerate(TG):
            for j in blk:
                col, ns, sig = segs[j]
                off = col - c0
                ps = pso.next()
                for c in range(NCH):
                    P.mm(ps[0:ns, 0:n], b[:, c, off:off + ns], hnT[:, c, t0:t0 + n], c == 0, c == NCH - 1,
                         [b.k, hkeys[gi]], [ps.k])
                o = ot.next()
                if sig:
                    P.act(o[0:ns, 0:n], ps[0:ns, 0:n], AF.Sigmoid, [ps.k, inbT.k], [o.k], bias=inbT[0:ns, j:j + 1])
                else:
                    P.ts(o[0:ns, 0:n], ps[0:ns, 0:n], inbT[0:ns, j:j + 1], None, ALU.add, None, [ps.k, inbT.k], [o.k])
                P.dma(G.pT_d[col:col + ns, t0:t0 + n], o[0:ns, 0:n], [o.k], [], q="pool")
        si += len(blk)
    P.end()


def phase_gmlp(G, l):
    P = Phase(G.nc)
    ident = G.ident
    wsT = P.sb([128, 4, 128])
    pw = P.rot(2, [128, 128], psum=True)
    wraw = P.rot(2, [128, 128])
    bsbc = P.sb([128, 4, 128])
    for g in range(4):
        wr = wraw.next()
        P.dma(wr[:], G.b_ws[l, g], [], [wr.k])
        p_ = pw.next()
        P.tr(p_[:], wr[:], ident[:], [wr.k, "ident"], [p_.k])
        P.cp(wsT[:, g, :], p_[:], [p_.k], [wsT.k])
        brow = P.sb([1, 128])
        P.dma(brow[:], G.b_bs[l, g:g + 1, :], [], [brow.k])
        p2 = pw.next()
        P.mm(p2[:], G.ones[0:1, :], brow[:], True, True, [brow.k, "ones"], [p2.k])
        P.cp(bsbc[:, g, :], p2[:], [p2.k], [bsbc.k])
    lng = P.sb([128, 8])
    lraw = P.sb([8, 128])
    P.dma(lraw[0:4, :], G.b_ln_g[l:l + 1, :].rearrange("o (j p) -> (o j) p", p=128), [], [lraw.k])
    P.dma(lraw[4:8, :], G.b_ln_b[l:l + 1, :].rearrange("o (j p) -> (o j) p", p=128), [], [lraw.k])
    p_ = pw.next()
    P.tr(p_[:, 0:8], lraw[:], ident[0:8, 0:8], [lraw.k, "ident"], [p_.k])
    P.cp(lng[:], p_[:, 0:8], [p_.k], [lng.k])
    zin = P.rot(2, [128, 8, 512])
    zz = P.rot(2, [128, 8, 512])
    t1 = P.rot(2, [128, 512])
    t2 = P.rot(2, [128, 512])
    ps1 = P.rot(1, [128, 512], psum=True)
    ps2 = P.rot(1, [128, 512], psum=True)
    mean = P.rot(2, [128, 512])
    rstd = P.rot(2, [128, 512])
    vn = P.rot(2, [128, 4, 512])
    vtok = P.rot(3, [128, 128])
    pt = P.rot(2, [128, 128], psum=True)
    pm = P.rot(2, [128, 128], psum=True)
    sres = P.rot(2, [128, 128])
    hb = P.rot(2, [128, 4, 512], BF16)
    for (t0, n) in TG:
        zi = zin.next()
        P.dma(zi[:, :, 0:n], fm(G.pT_d[OFF_B:OFF_B + 1024, :])[:, :, t0:t0 + n], [], [zi.k])
        z = zz.next()
        for c in range(8):
            a = t1.next()
            P.act(a[:, 0:n], zi[:, c, 0:n], AF.Square, [zi.k], [a.k])
            P.ts(a[:, 0:n], a[:, 0:n], 0.044715, 1.0, ALU.mult, ALU.add, [a.k], [a.k])
            P.tt(a[:, 0:n], a[:, 0:n], zi[:, c, 0:n], ALU.mult, [a.k, zi.k], [a.k])
            P.act(a[:, 0:n], a[:, 0:n], AF.Sigmoid, [a.k], [a.k], scale=1.5957691216057308)
            P.tt(z[:, c, 0:n], a[:, 0:n], zi[:, c, 0:n], ALU.mult, [a.k, zi.k], [z.k], eng="pool")
        p1 = ps1.next()
        p2 = ps2.next()
        for c in range(4):
            P.mm(p1[:, 0:n], G.ones[:], z[:, 4 + c, 0:n], c == 0, c == 3, [z.k, "ones"], [p1.k])
        for c in range(4):
            b = t2.next()
            P.act(b[:, 0:n], z[:, 4 + c, 0:n], AF.Square, [z.k], [b.k])
            P.mm(p2[:, 0:n], G.ones[:], b[:, 0:n], c == 0, c == 3, [b.k, "ones"], [p2.k])
        mu = mean.next()
        P.ts(mu[:, 0:n], p1[:, 0:n], 1.0 / 512, None, ALU.mult, None, [p1.k], [mu.k])
        rs = rstd.next()
        P.tt(rs[:, 0:n], mu[:, 0:n], mu[:, 0:n], ALU.mult, [mu.k], [rs.k])
        P.stt(rs[:, 0:n], p2[:, 0:n], 1.0 / 512, rs[:, 0:n], ALU.mult, ALU.subtract, [p2.k, rs.k], [rs.k])
        P.act(rs[:, 0:n], rs[:, 0:n], AF.Sqrt, [rs.k], [rs.k], bias=G.eps5[:, 0:1])
        P.recip(rs[:, 0:n], rs[:, 0:n], [rs.k], [rs.k])
        v = vn.next()
        for c in range(4):
            P.tt(v[:, c, 0:n], z[:, 4 + c, 0:n], mu[:, 0:n], ALU.subtract, [z.k, mu.k], [v.k])
            P.tt(v[:, c, 0:n], v[:, c, 0:n], rs[:, 0:n], ALU.mult, [v.k, rs.k], [v.k])
            P.ts(v[:, c, 0:n], v[:, c, 0:n], lng[:, c:c + 1], lng[:, 4 + c:5 + c], ALU.mult, ALU.add, [v.k, lng.k], [v.k])
        h = hb.next()
        for j in range(n // 128):
            for g in range(4):
                p_ = pt.next()
                P.tr(p_[:], v[:, g, j * 128:(j + 1) * 128], ident[:], [v.k, "ident"], [p_.k])
                vt = vtok.next()
                P.cp(vt[:], p_[:], [p_.k], [vt.k], eng="act_copy")
                pq = pm.next()
                P.mm(pq[:], vt[:], wsT[:, g, :], True, True, [vt.k, wsT.k], [pq.k])
                sr = sres.next()
                P.tt(sr[:], pq[:], bsbc[:, g, :], ALU.add, [pq.k, bsbc.k], [sr.k])
                P.tt(h[:, g, j * 128:(j + 1) * 128], sr[:], z[:, g, j * 128:(j + 1) * 128], ALU.mult, [sr.k, z.k], [h.k], eng="pool")
        P.dma(fm(G.hsT_d[1])[:, :, t0:t0 + n], h[:, :, 0:n], [h.k], [], q="pool")
    P.end()


import os
STOP = int(os.environ.get("KSTOP", "0"))
NIT = int(os.environ.get("KNIT", "7"))
ORDER = {0: list(range(NT)), 1: [1, 0] + list(range(NT - 1, 1, -1))}


def load_cols(P, dst, src_row_ap, nrows, pw, ident):
    raw = P.sb([nrows, 128])
    P.dma(raw[:], src_row_ap.rearrange("o (j p) -> (o j) p", p=128), [], [raw.k])
    p_ = pw.next()
    P.tr(p_[:, 0:nrows], raw[:], ident[0:nrows, 0:nrows], [raw.k, "ident"], [p_.k])
    P.cp(dst, p_[:, 0:nrows], [p_.k], ["cols"])


def head_norm_store(P, G, hsum, n_heads_chunks, eps_col, gcols, bcols, gate_rows, gate_func, dst, sub_mean, pw):
    sq = P.rot(2, [128, 512])
    ps1 = P.rot(1, [128, 512], psum=True)
    ps2 = P.rot(1, [128, 512], psum=True)
    mu = P.rot(2, [128, 512])
    rs = P.rot(2, [128, 512])
    xc = P.rot(2, [128, 512])
    gin = P.rot(2, [128, 512])
    ob = P.rot(2, [128, 512], BF16)
    for (t0, n) in TG:
        for h in range(4):
            x = hsum[:, h, t0:t0 + n]
            xk = hsum.k
            c = xc.next()
            r = rs.next()
            if sub_mean:
                p1 = ps1.next()
                P.mm(p1[:, 0:n], G.ones[:], x, True, True, [xk, "ones"], [p1.k])
                m = mu.next()
                P.ts(m[:, 0:n], p1[:, 0:n], 1.0 / 128, None, ALU.mult, None, [p1.k], [m.k])
                P.tt(c[:, 0:n], x, m[:, 0:n], ALU.subtract, [xk, m.k], [c.k])
            else:
                P.cp(c[:, 0:n], x, [xk], [c.k], eng="pool")
            q = sq.next()
            P.act(q[:, 0:n], c[:, 0:n], AF.Square, [c.k], [q.k])
            p2 = ps2.next()
            P.mm(p2[:, 0:n], G.ones[:], q[:, 0:n], True, True, [q.k, "ones"], [p2.k])
            P.act(r[:, 0:n], p2[:, 0:n], AF.Sqrt, [p2.k], [r.k], bias=eps_col, scale=1.0 / 128)
            P.recip(r[:, 0:n], r[:, 0:n], [r.k], [r.k])
            P.tt(c[:, 0:n], c[:, 0:n], r[:, 0:n], ALU.mult, [c.k, r.k], [c.k])
            if bcols is not None:
                P.ts(c[:, 0:n], c[:, 0:n], gcols[:, h:h + 1], bcols[:, h:h + 1], ALU.mult, ALU.add, [c.k, "cols"], [c.k])
            else:
                P.ts(c[:, 0:n], c[:, 0:n], gcols[:, h:h + 1], None, ALU.mult, None, [c.k, "cols"], [c.k])
            if STOP == 4:
                continue
            g = gin.next()
            P.dma(g[:, 0:n], G.pT_d[gate_rows + h * 128:gate_rows + (h + 1) * 128, t0:t0 + n], [], [g.k])
            P.act(g[:, 0:n], g[:, 0:n], gate_func, [g.k], [g.k])
            if STOP == 6:
                continue
            o = ob.next()
            P.tt(o[:, 0:n], c[:, 0:n], g[:, 0:n], ALU.mult, [c.k, g.k], [o.k], eng="pool")
            if STOP == 5:
                continue
            P.dma(dst[h * 128:(h + 1) * 128, t0:t0 + n], o[:, 0:n], [o.k], [], q="sp")


def phase_rwkv(G, l):
    P = Phase(G.nc)
    ident = G.ident
    bones = G.bones
    q8 = P.psq(8)
    pw = Rot(q8.tiles[0:2])
    mu = P.sb([128, 15])
    load_cols(P, mu[:], G.a_mu[l:l + 1, :], 15, pw, ident)
    mu1 = P.sb([128, 15])
    P.ts(mu1[:], mu[:], -1.0, 1.0, ALU.mult, ALU.add, ["cols"], [mu1.k])
    mum = P.sb([128, 6, 15])
    for j in range(6):
        P.ts(mum[:, j, :], mu[:], G.pmask[:, j:j + 1], None, ALU.mult, None, ["cols", "pmask"], [mum.k])
    cols = P.sb([128, 12, 4])
    for i_, src in enumerate((G.a_w0[l, 0:1, :], G.a_w0[l, 1:2, :], G.a_a0[l, 0:1, :], G.a_a0[l, 1:2, :],
                              G.a_kk[l:l + 1, :], G.a_ka[l:l + 1, :], G.a_rk[l:l + 1, :], G.a_ln_g[l:l + 1, :],
                              G.a_ln_b[l:l + 1, :])):
        load_cols(P, cols[:, i_, :], src, 4, pw, ident)
    P.ts(cols[:, 9, :], cols[:, 5, :], -1.0, 1.0, ALU.mult, ALU.add, ["cols"], ["cols"])
    wup = P.sb([128, 512])
    aup = P.sb([128, 512])
    gup = P.sb([128, 512])
    for d in range(2):
        P.dma(wup[d * 64:(d + 1) * 64, :], G.a_wup[l, d], [], [wup.k])
        P.dma(aup[d * 64:(d + 1) * 64, :], G.a_aup[l, d], [], [aup.k])
    P.dma(gup[:], G.a_gup[l], [], [gup.k])

    def mix(dst, chunk):
        x = xin.next()
        P.dma(x[:], G.pT_d[chunk * 128:(chunk + 1) * 128, :], [], [x.k])
        P.ts(dst[:], x[:], mu1[:, chunk:chunk + 1], None, ALU.mult, None, [x.k, mu1.k], [dst.k])
        m = lambda j: mum[:, j, chunk:chunk + 1]
        rk_ = [x.k, dst.k, mum.k]
        P.stt(dst[:, 1:LC], x[:, 0:LC - 1], m(4), dst[:, 1:LC], ALU.mult, ALU.add, rk_, [dst.k])
        P.stt(dst[:, 0:LC - 1], x[:, 1:LC], m(5), dst[:, 0:LC - 1], ALU.mult, ALU.add, rk_, [dst.k])
        xg = x[:, LC:T].rearrange("p (r w) -> p r w", w=64)
        dg = dst[:, LC:T].rearrange("p (r w) -> p r w", w=64)
        P.stt(dg[:, :, 1:64], xg[:, :, 0:63], m(0), dg[:, :, 1:64], ALU.mult, ALU.add, rk_, [dst.k])
        P.stt(dg[:, :, 0:63], xg[:, :, 1:64], m(1), dg[:, :, 0:63], ALU.mult, ALU.add, rk_, [dst.k])
        P.stt(dst[:, LC + 64:T], x[:, LC:T - 64], m(2), dst[:, LC + 64:T], ALU.mult, ALU.add, rk_, [dst.k])
        P.stt(dst[:, LC:T - 64], x[:, LC + 64:T], m(3), dst[:, LC:T - 64], ALU.mult, ALU.add, rk_, [dst.k])

    xin = P.rot(1, [128, T])
    tw = P.sb([128, T])
    adz = P.sb([128, T])
    sg = P.sb([128, T])
    mix(tw, 12)
    P.act(tw[:], tw[:], AF.Tanh, [tw.k], [tw.k])
    mix(adz, 13)
    mix(sg, 14)
    P.act(sg[:], sg[:], AF.Sigmoid, [sg.k], [sg.k])
    if STOP == 21:
        P.end()
        return
    rT = P.sb([128, T])
    kT = P.sb([128, T])
    vT = P.sb([128, T])
    kkT = P.sb([128, T])
    lw = P.sb([128, T])
    aT = P.sb([128, T])
    ktT = P.sb([128, T])
    bT = P.sb([128, T])
    ysum = P.sb([128, T])
    Vboth = P.sb([128, NT, 128])
    Vpad = P.sb([128, NT, 2, 128])
    Upad = P.sb([128, 2, 128])
    STbd = P.sb([128, 128])
    pbig = P.rot(2, [128, 512], psum=True)
    tmp = P.rot(4, [128, 512])
    sm = P.rot(6, [128, 128])
    pq = Rot([P.ps([128, 128]) for _ in range(4)] + [Tile(q8.tiles[4].t, q8.tiles[4].k)])
    E = P.rot(3, [128, 3, 128])
    ops4 = P.rot(2, [128, 4, 128])
    tokT = P.rot(2, [128, 3, 128])
    Ysb = P.rot(4, [128, 128])
    YTsb = P.rot(4, [128, 128])
    Rsb = P.rot(4, [128, 128])
    msk = P.rot(4, [128, 3, 128])
    Wsb = P.rot(2, [128, 64])
    ob = P.rot(2, [128, 512], BF16)
    P.memset(Upad[:], 0.0, [], [Upad.k + "0", Upad.k + "1"])
    for c in range(4):
        mix(rT, c)
        mix(kT, 4 + c)
        mix(vT, 8 + c)
        if STOP == 25:
            P.end()
            return
        for (t0, n) in TG:
            a = tmp.next()
            P.ts(kkT[:, t0:t0 + n], kT[:, t0:t0 + n], cols[:, 4, c:c + 1], None, ALU.mult, None, [kT.k, "cols"], [kkT.k])
            P.act(a[:, 0:n], kkT[:, t0:t0 + n], AF.Square, [kkT.k], [a.k])
            pb_ = pbig.next()
            P.mm(pb_[:, 0:n], bones[:], a[:, 0:n], True, True, [a.k, "bones"], [pb_.k])
            P.ts(a[:, 0:n], pb_[:, 0:n], 1e-24, None, ALU.max, None, [pb_.k], [a.k])
            P.act(a[:, 0:n], a[:, 0:n], AF.Sqrt, [a.k], [a.k])
            P.recip(a[:, 0:n], a[:, 0:n], [a.k], [a.k])
            P.tt(kkT[:, t0:t0 + n], kkT[:, t0:t0 + n], a[:, 0:n], ALU.mult, [a.k, kkT.k], [kkT.k])
        if STOP == 26:
            P.end()
            return
        P.memset(Vpad[:], 0.0, [], [Vpad.k])
        if STOP == 27:
            P.end()
            return
        for t4 in range(0, NT, 4):
            nn = min(4, NT - t4)
            p_ = pbig.next()
            for j in range(nn):
                P.tr(p_[:, j * 128:(j + 1) * 128], vT[:, (t4 + j) * 128:(t4 + j + 1) * 128], ident[:], [vT.k, "ident"], [p_.k])
            pv = p_[:, 0:nn * 128].rearrange("p (j t) -> p j t", j=nn)
            P.cp(Vboth[:, t4:t4 + nn, :], pv, [p_.k], [Vboth.k], eng="act_copy")
            for j in range(2):
                P.cp(Vpad[:, t4:t4 + nn, j, j * 64:(j + 1) * 64], pv[:, :, j * 64:(j + 1) * 64], [p_.k], [Vpad.k], eng="act_copy")
        if STOP == 22:
            P.end()
            return
        for d in range(2):
            tri = G.tri[:, d, :]
            stri = G.tri[:, 2 + d, :]
            striT = G.tri[:, 3 - d, :]
            last = 127 if d == 0 else 0
            hp_d = slice(d * 64, (d + 1) * 64)
            for (t0, n) in TG:
                pb_ = pbig.next()
                P.mm(pb_[:, 0:n], wup[hp_d, c * 128:(c + 1) * 128], tw[hp_d, t0:t0 + n], True, True, [wup.k, tw.k], [pb_.k])
                P.act(lw[:, t0:t0 + n], pb_[:, 0:n], AF.Sigmoid, [pb_.k, "cols"], [lw.k], bias=cols[:, d, c:c + 1])
                P.ts(lw[:, t0:t0 + n], lw[:, t0:t0 + n], -0.6065306597126334, None, ALU.mult, None, [lw.k], [lw.k])
                pb2 = pbig.next()
                P.mm(pb2[:, 0:n], aup[hp_d, c * 128:(c + 1) * 128], adz[hp_d, t0:t0 + n], True, True, [aup.k, adz.k], [pb2.k])
                P.act(aT[:, t0:t0 + n], pb2[:, 0:n], AF.Sigmoid, [pb2.k, "cols"], [aT.k], bias=cols[:, 2 + d, c:c + 1])
                a = tmp.next()
                P.ts(a[:, 0:n], aT[:, t0:t0 + n], cols[:, 5, c:c + 1], cols[:, 9, c:c + 1], ALU.mult, ALU.add, [aT.k, "cols"], [a.k])
                P.tt(ktT[:, t0:t0 + n], kT[:, t0:t0 + n], a[:, 0:n], ALU.mult, [kT.k, a.k], [ktT.k])
                P.tt(bT[:, t0:t0 + n], kkT[:, t0:t0 + n], aT[:, t0:t0 + n], ALU.mult, [kkT.k, aT.k], [bT.k], eng="pool")
            if STOP == 23:
                P.end()
                return
            P.memset(STbd[:], 0.0, [], [STbd.k])
            for tt in ORDER[d]:
                tsl = slice(tt * 128, (tt + 1) * 128)
                tk = tokT.next()
                p_ = pq.next()
                P.tr(p_[:], lw[:, tsl], ident[:], [lw.k, "ident"], [p_.k])
                P.cp(tk[:, 0, :], p_[:], [p_.k], [tk.k + "a"], eng="act_copy")
                pci = pq.next()
                pce = pq.next()
                P.mm(pci[:], tk[:, 0, :], tri, True, True, [tk.k + "a", "tri"], [pci.k])
                P.mm(pce[:], tk[:, 0, :], stri, True, True, [tk.k + "a", "tri"], [pce.k])
                e = E.next()
                P.act(e[:, 0, :], pci[:], AF.Exp, [pci.k], [e.k])
                P.act(e[:, 1, :], pce[:], AF.Exp, [pce.k], [e.k])
                P.act(e[:, 2, :], pci[:], AF.Exp, [pci.k], [e.k], scale=-1.0)
                o4 = ops4.next()
                P.tt(o4[:, 0, :], kkT[:, tsl], e[:, 1, :], ALU.mult, [kkT.k, e.k], [o4.k])
                P.tt(o4[:, 1, :], bT[:, tsl], e[:, 2, :], ALU.mult, [bT.k, e.k], [o4.k], eng="pool")
                P.tt(o4[:, 2, :], ktT[:, tsl], e[:, 2, :], ALU.mult, [ktT.k, e.k], [o4.k])
                P.tt(o4[:, 3, :], rT[:, tsl], e[:, 0, :], ALU.mult, [rT.k, e.k], [o4.k], eng="pool")
                p_ = pq.next()
                P.tr(p_[:], o4[:, 1, :], ident[:], [o4.k, "ident"], [p_.k])
                P.ts(tk[:, 1, :], p_[:], -1.0, None, ALU.mult, None, [p_.k], [tk.k + "b"])
                p_ = pq.next()
                P.tr(p_[:], o4[:, 2, :], ident[:], [o4.k, "ident"], [p_.k])
                P.cp(tk[:, 2, :], p_[:], [p_.k], [tk.k + "c"], eng="act_copy")
                mks = []
                Ys, YTs, Rs = [None, None], [None, None], [None, None]
                for j in range(2):
                    hp = slice(j * 64, (j + 1) * 64)
                    kkg, bh, kth, rt = o4[hp, 0, :], o4[hp, 1, :], o4[hp, 2, :], o4[hp, 3, :]
                    p1 = pq.next()
                    P.mm(p1[:], bh, kkg, True, True, [o4.k], [p1.k])
                    Y = Ysb.next()
                    P.stt(Y[:], p1[:], -1.0, stri, ALU.mult, ALU.mult, [p1.k, "tri"], [Y.k])
                    p2 = pq.next()
                    P.mm(p2[:], kkg, bh, True, True, [o4.k], [p2.k])
                    YT = YTsb.next()
                    P.stt(YT[:], p2[:], -1.0, striT, ALU.mult, ALU.mult, [p2.k, "tri"], [YT.k])
                    R = Rsb.next()
                    P.tt(R[:], Y[:], ident[:], ALU.add, [Y.k, "ident"], [R.k], eng="pool")
                    Ys[j], YTs[j], Rs[j] = Y, YT, R
                for j in range(2):
                    hp = slice(j * 64, (j + 1) * 64)
                    kkg, bh, kth, rt = o4[hp, 0, :], o4[hp, 1, :], o4[hp, 2, :], o4[hp, 3, :]
                    mk = msk.next()
                    p3 = pq.next()
                    P.mm(p3[:], kth, kkg, True, True, [o4.k], [p3.k])
                    P.tt(mk[:, 0, :], p3[:], stri, ALU.mult, [p3.k, "tri"], [mk.k + "0"])
                    p4 = pq.next()
                    P.mm(p4[:], bh, rt, True, True, [o4.k], [p4.k])
                    P.stt(mk[:, 1, :], p4[:], -1.0, tri, ALU.mult, ALU.mult, [p4.k, "tri"], [mk.k + "1"])
                    p5 = pq.next()
                    P.mm(p5[:], kth, rt, True, True, [o4.k], [p5.k])
                    P.tt(mk[:, 2, :], p5[:], tri, ALU.mult, [p5.k, "tri"], [mk.k + "2"])
                    mks.append(mk)
                for it in range(1, 7):
                    for j in range(2):
                        Y, YT, R = Ys[j], YTs[j], Rs[j]
                        pyT = pq.next()
                        P.mm(pyT[:], Y[:], YT[:], True, True, [Y.k, YT.k], [pyT.k])
                        if it < 6:
                            py = pq.next()
                            P.mm(py[:], YT[:], Y[:], True, True, [Y.k, YT.k], [py.k])
                            Y2 = Ysb.next()
                            P.cp(Y2[:], py[:], [py.k], [Y2.k], eng="act_copy")
                        YT2 = YTsb.next()
                        P.tt(YT2[:], pyT[:], G.ones[:], ALU.mult, [pyT.k, "ones"], [YT2.k])
                        pr = pq.next()
                        P.mm(pr[:], YT2[:], R[:], True, True, [YT2.k, R.k], [pr.k])
                        R2 = Rsb.next()
                        P.tt(R2[:], pr[:], R[:], ALU.add, [pr.k, R.k], [R2.k])
                        Rs[j] = R2
                        YTs[j] = YT2
                        if it < 6:
                            Ys[j] = Y2
                for j in range(2):
                    hp = slice(j * 64, (j + 1) * 64)
                    kkg = o4[hp, 0, :]
                    mk = mks[j]
                    R = Rs[j]
                    pW = pq.next()
                    P.mm(pW[:, 0:64], kkg, STbd[hp, j * 64:(j + 1) * 64], True, False, [o4.k, STbd.k], [pW.k])
                    P.mm(pW[:, 0:64], mk[:, 0, :], Vboth[:, tt, j * 64:(j + 1) * 64], False, True, [mk.k + "0", Vboth.k], [pW.k])
                    W_ = Wsb.next()
                    P.cp(W_[:], pW[:, 0:64], [pW.k], [W_.k], eng="act_copy")
                    pU = pq.next()
                    P.mm(pU[:, 0:64], R[:], W_[:], True, True, [R.k, W_.k], [pU.k])
                    P.cp(Upad[:, j, j * 64:(j + 1) * 64], pU[:, 0:64], [pU.k], [Upad.k + str(j)], eng="act_copy")
                pY = pq.next()
                P.mm(pY[:], STbd[:], o4[:, 3, :], True, False, [STbd.k, o4.k], [pY.k])
                for j in range(2):
                    P.mm(pY[:], Upad[:, j, :], mks[j][:, 1, :], False, False, [Upad.k + str(j), mks[j].k + "1"], [pY.k])
                    P.mm(pY[:], Vpad[:, tt, j, :], mks[j][:, 2, :], False, j == 1, [Vpad.k, mks[j].k + "2"], [pY.k])
                if d == 0:
                    P.cp(ysum[:, tsl], pY[:], [pY.k], [ysum.k], eng="act_copy")
                else:
                    yt_ = sm.next()
                    P.cp(yt_[:], pY[:], [pY.k], [yt_.k], eng="act_copy")
                    P.tt(ysum[:, tsl], ysum[:, tsl], yt_[:], ALU.add, [yt_.k, ysum.k], [ysum.k])
                pS = pq.next()
                P.mm(pS[:], tk[:, 2, :], Vboth[:, tt, :], True, False, [tk.k + "c", Vboth.k], [pS.k])
                P.mm(pS[:], tk[:, 1, :], Upad[:, 0, :], False, False, [tk.k + "b", Upad.k + "0"], [pS.k])
                P.mm(pS[:], tk[:, 1, :], Upad[:, 1, :], False, True, [tk.k + "b", Upad.k + "1"], [pS.k])
                s_ = sm.next()
                P.cp(s_[:], pS[:], [pS.k], [s_.k], eng="act_copy")
                P.tt(s_[:], s_[:], bones[:], ALU.mult, [s_.k, "bones"], [s_.k])
                P.tt(STbd[:], STbd[:], s_[:], ALU.add, [s_.k, STbd.k], [STbd.k], eng="pool")
                P.ts(STbd[:], STbd[:], e[:, 0, last:last + 1], None, ALU.mult, None, [e.k, STbd.k], [STbd.k])
                if STOP == 24:
                    P.end()
                    return
        if G.debug == "brtest" and c == 0:
            for i_, t_ in enumerate((ysum, kkT, lw, aT, ktT, bT, rT, vT)):
                P.dma(G.dbg_a[i_], t_[:], [t_.k], [], q="sp")
        for (t0, n) in TG:
            pm_ = pbig.next()
            P.mm(pm_[:, 0:n], bones[:], ysum[:, t0:t0 + n], True, True, [ysum.k, "bones"], [pm_.k])
            xc = tmp.next()
            P.stt(xc[:, 0:n], pm_[:, 0:n], -1.0 / 64, ysum[:, t0:t0 + n], ALU.mult, ALU.add, [pm_.k, ysum.k], [xc.k])
            sq_ = tmp.next()
            P.act(sq_[:, 0:n], xc[:, 0:n], AF.Square, [xc.k], [sq_.k])
            pv_ = pbig.next()
            P.mm(pv_[:, 0:n], bones[:], sq_[:, 0:n], True, True, [sq_.k, "bones"], [pv_.k])
            P.act(sq_[:, 0:n], pv_[:, 0:n], AF.Sqrt, [pv_.k], [sq_.k], bias=G.epsa[:, 0:1], scale=1.0 / 64)
            P.recip(sq_[:, 0:n], sq_[:, 0:n], [sq_.k], [sq_.k])
            P.tt(xc[:, 0:n], xc[:, 0:n], sq_[:, 0:n], ALU.mult, [xc.k, sq_.k], [xc.k])
            P.ts(xc[:, 0:n], xc[:, 0:n], cols[:, 7, c:c + 1], cols[:, 8, c:c + 1], ALU.mult, ALU.add, [xc.k, "cols"], [xc.k])
            rk_ = tmp.next()
            P.stt(rk_[:, 0:n], rT[:, t0:t0 + n], cols[:, 6, c:c + 1], kT[:, t0:t0 + n], ALU.mult, ALU.mult, [rT.k, kT.k, "cols"], [rk_.k])
            pb_ = pbig.next()
            P.mm(pb_[:, 0:n], bones[:], rk_[:, 0:n], True, True, [rk_.k, "bones"], [pb_.k])
            P.tt(rk_[:, 0:n], pb_[:, 0:n], vT[:, t0:t0 + n], ALU.mult, [pb_.k, vT.k], [rk_.k])
            P.tt(xc[:, 0:n], xc[:, 0:n], rk_[:, 0:n], ALU.add, [xc.k, rk_.k], [xc.k], eng="pool")
            pg_ = pbig.next()
            P.mm(pg_[:, 0:n], gup[:, c * 128:(c + 1) * 128], sg[:, t0:t0 + n], True, True, [gup.k, sg.k], [pg_.k])
            o = ob.next()
            P.tt(o[:, 0:n], xc[:, 0:n], pg_[:, 0:n], ALU.mult, [xc.k, pg_.k], [o.k])
            P.dma(G.hsT_d[0][c * 128:(c + 1) * 128, t0:t0 + n], o[:, 0:n], [o.k], [], q="sp")
    P.end()


def phase_mlstm(G, l):
    P = Phase(G.nc)
    ident = G.ident
    pw = P.psq(2)
    cw = P.sb([128, 8, 4])
    for j in range(3):
        load_cols(P, cw[:, :, j], G.c_conv_w[l, j:j + 1, :], 8, pw, ident)
    load_cols(P, cw[:, :, 3], G.c_conv_b[l:l + 1, :], 8, pw, ident)
    lng = P.sb([128, 4])
    load_cols(P, lng[:], G.c_ln_g[l:l + 1, :], 4, pw, ident)
    gb = P.sb([16, 1])
    P.dma(gb[:], G.c_gate_b[l].rearrange("a b (c o) -> (a b c) o", o=1), [], [gb.k])
    qT = P.sb([128, 4, T], BF16)
    kT = P.sb([128, 4, T], BF16)
    kTok = P.sb([128, NT, 4, 128], BF16)
    vTok = P.sb([128, NT, 4, 128], BF16)
    hsum = P.sb([128, 4, T])
    xin = P.rot(1, [128, T])
    yv = P.rot(1, [128, T])
    ptr = P.rot(1, [128, 512], psum=True)
    SEG = [(0, LC), (LC, T - LC)]
    for c in range(8):
        x = xin.next()
        P.dma(x[:], G.pT_d[OFF_C + c * 128:OFF_C + (c + 1) * 128, :], [], [x.k])
        y = yv.next()
        P.ts(y[:], x[:], cw[:, c, 1:2], cw[:, c, 3:4], ALU.mult, ALU.add, [x.k, "cols"], [y.k])
        for (a, n) in SEG:
            P.stt(y[:, a + 1:a + n], x[:, a:a + n - 1], cw[:, c, 0:1], y[:, a + 1:a + n], ALU.mult, ALU.add, [x.k, y.k, "cols"], [y.k])
            P.stt(y[:, a:a + n - 1], x[:, a + 1:a + n], cw[:, c, 2:3], y[:, a:a + n - 1], ALU.mult, ALU.add, [x.k, y.k, "cols"], [y.k])
        if c < 4:
            P.act(qT[:, c, :], y[:], AF.Silu, [y.k], [qT.k])
        else:
            P.act(y[:], y[:], AF.Silu, [y.k], [y.k])
            P.ts(y[:], y[:], 128 ** -0.5, None, ALU.mult, None, [y.k], [y.k])
            P.cp(kT[:, c - 4, :], y[:], [y.k], [kT.k], eng="pool")
            for t4 in range(0, NT, 4):
                nn = min(4, NT - t4)
                p_ = ptr.next()
                for j in range(nn):
                    P.tr(p_[:, j * 128:(j + 1) * 128], y[:, (t4 + j) * 128:(t4 + j + 1) * 128], ident[:], [y.k, "ident"], [p_.k])
                P.cp(kTok[:, t4:t4 + nn, c - 4, :], p_[:, 0:nn * 128].rearrange("p (j t) -> p j t", j=nn), [p_.k], [kTok.k], eng="act_copy")
    for h in range(4):
        x = xin.next()
        P.dma(x[:], G.pT_d[OFF_C + 1024 + h * 128:OFF_C + 1024 + (h + 1) * 128, :], [], [x.k])
        for t4 in range(0, NT, 4):
            nn = min(4, NT - t4)
            p_ = ptr.next()
            for j in range(nn):
                P.tr(p_[:, j * 128:(j + 1) * 128], x[:, (t4 + j) * 128:(t4 + j + 1) * 128], ident[:], [x.k, "ident"], [p_.k])
            P.cp(vTok[:, t4:t4 + nn, h, :], p_[:, 0:nn * 128].rearrange("p (j t) -> p j t", j=nn), [p_.k], [vTok.k], eng="act_copy")
    if STOP == 1:
        P.end()
        return
    gT = Tile(xin.tiles[0].t[0:16, :], xin.tiles[0].k)
    lfT = Tile(yv.tiles[0].t[0:16, :], yv.tiles[0].k)
    P.dma(gT[:], G.pT_d[OFF_C + 2048:OFF_C + 2064, :], [], [gT.k])
    P.ts(gT[:], gT[:], gb[:, 0:1], None, ALU.add, None, [gT.k, gb.k], [gT.k])
    if STOP == 11:
        P.end()
        return
    P.act(lfT[:], gT[:], AF.Exp, [gT.k], [lfT.k], scale=-1.0)
    if STOP == 12:
        P.end()
        return
    P.act(lfT[:], lfT[:], AF.Ln, [lfT.k], [lfT.k], bias=G.onec[0:16, 0:1])
    if STOP == 13:
        P.end()
        return
    gTok = P.sb([128, NT, 16])
    lfTok = P.sb([128, NT, 16])
    for tt in range(NT):
        p_ = pw.next()
        P.mm(p_[:, 0:16], gT[:, tt * 128:(tt + 1) * 128], ident[0:16, 0:16], True, True, [gT.k, "ident"], [p_.k])
        P.mm(p_[:, 16:32], lfT[:, tt * 128:(tt + 1) * 128], ident[0:16, 0:16], True, True, [lfT.k, "ident"], [p_.k])
        P.cp(gTok[:, tt, :], p_[:, 0:16], [p_.k], [gTok.k])
        P.ts(lfTok[:, tt, :], p_[:, 16:32], -1.0, None, ALU.mult, None, [p_.k], [lfTok.k])
    if STOP == 2:
        P.end()
        return
    CT = P.sb([128, 4, 128])
    CTb = P.sb([128, 4, 128], BF16)
    NR = P.sb([128, 4, 128])
    NRb = P.sb([128, 4, 128], BF16)
    q8 = P.psq(8)
    pb4 = Rot([pw.tiles[0]])
    pbb = Rot([Tile(ptr.tiles[0].t[:, 0:128], ptr.tiles[0].k), P.ps([128, 128])])
    pss = Rot([q8.tiles[3]])
    psn = Rot([q8.tiles[4]])
    psd = Rot([q8.tiles[5]])
    pcb = P.ps([128, 512])
    psc = Rot([Tile(pcb.t[:, 0:256], pcb.k)])
    bcol = P.rot(2, [128, 4])
    ib = P.rot(2, [128, 4])
    lmask = P.rot(3, [128, 128])
    DT = P.rot(3, [128, 128])
    EB = P.rot(3, [128, 128])
    PT = P.rot(3, [128, 128], BF16)
    qs = P.rot(3, [128, 128], BF16)
    den = P.rot(3, [128, 128])
    hd = P.rot(3, [128, 128])
    bl = P.rot(3, [128, 2])
    sw = P.rot(3, [128, 1])
    ksw = P.rot(3, [128, 128], BF16)
    swbc = P.rot(3, [128, 128], BF16)
    for d in range(2):
        tri = G.tri[:, d, :]
        last = 127 if d == 0 else 0
        for h in range(4):
            P.memset(CT[:, h, :], 0.0, [], [CT.k + str(h)])
            P.memset(CTb[:, h, :], 0.0, [], [CTb.k + str(h)])
            P.memset(NR[:, h, :], 0.0, [], [NR.k + str(h)])
            P.memset(NRb[:, h, :], 0.0, [], [NRb.k + str(h)])
        for tt in ORDER[d]:
            tsl = slice(tt * 128, (tt + 1) * 128)
            p4 = pb4.next()
            P.mm(p4[:, 0:4], tri, lfTok[:, tt, d * 8 + 4:d * 8 + 8], True, True, ["tri", lfTok.k], [p4.k])
            bc = bcol.next()
            P.cp(bc[:], p4[:, 0:4], [p4.k], [bc.k])
            ibt = ib.next()
            P.tt(ibt[:], gTok[:, tt, d * 8:d * 8 + 4], bc[:], ALU.subtract, [gTok.k, bc.k], [ibt.k])
            for h in range(4):
                lm = lmask.next()
                P.ts(lm[:], tri, lfTok[:, tt, d * 8 + 4 + h:d * 8 + 5 + h], None, ALU.mult, None, ["tri", lfTok.k], [lm.k])
                pB = pbb.next()
                P.mm(pB[:], G.ones[:], lm[:], True, True, [lm.k, "ones"], [pB.k])
                dt_ = DT.next()
                P.act(dt_[:], pB[:], AF.Exp, [pB.k, ibt.k], [dt_.k], bias=ibt[:, h:h + 1])
                eb = EB.next()
                P.act(eb[:], pB[:], AF.Exp, [pB.k], [eb.k])
                b_ = bl.next()
                P.cp(b_[:, 0:1], pB[:, last:last + 1], [pB.k], [b_.k])
                P.tt(dt_[:], dt_[:], tri, ALU.mult, [dt_.k, "tri"], [dt_.k], eng="pool")
                ps_s = pss.next()
                P.mm(ps_s[:], kT[:, h, tsl], qT[:, h, tsl], True, True, [kT.k, qT.k], [ps_s.k])
                pt_ = PT.next()
                P.tt(pt_[:], ps_s[:], dt_[:], ALU.mult, [ps_s.k, dt_.k], [pt_.k])
                q_ = qs.next()
                P.tt(q_[:], qT[:, h, tsl], eb[:], ALU.mult, [qT.k, eb.k], [q_.k], eng="pool")
                pn = psn.next()
                P.mm(pn[:], vTok[:, tt, h, :], pt_[:], True, False, [vTok.k, pt_.k], [pn.k])
                P.mm(pn[:], CTb[:, h, :], q_[:], False, True, [CTb.k + str(h), q_.k], [pn.k])
                pd = psd.next()
                P.mm(pd[:], G.onesb[:], pt_[:], True, False, [pt_.k, "onesb"], [pd.k])
                P.mm(pd[:], NRb[:, h, :], q_[:], False, True, [NRb.k + str(h), q_.k], [pd.k])
                de = den.next()
                P.act(de[:], pd[:], AF.Abs, [pd.k], [de.k])
                P.ts(de[:], de[:], 1.0, None, ALU.max, None, [de.k], [de.k])
                P.recip(de[:], de[:], [de.k], [de.k])
                if d == 0:
                    P.tt(hsum[:, h, tsl], pn[:], de[:], ALU.mult, [pn.k, de.k], [hsum.k])
                else:
                    hh = hd.next()
                    P.tt(hh[:], pn[:], de[:], ALU.mult, [pn.k, de.k], [hh.k])
                    P.tt(hsum[:, h, tsl], hsum[:, h, tsl], hh[:], ALU.add, [hh.k, hsum.k], [hsum.k], eng="pool")
                s_ = sw.next()
                P.act(s_[:], ibt[:, h:h + 1], AF.Exp, [ibt.k, b_.k], [s_.k], bias=b_[:, 0:1])
                P.act(b_[:, 1:2], b_[:, 0:1], AF.Exp, [b_.k], [b_.k])
                ks = ksw.next()
                P.ts(ks[:], kTok[:, tt, h, :], s_[:, 0:1], None, ALU.mult, None, [kTok.k, s_.k], [ks.k])
                sb_ = swbc.next()
                P.ts(sb_[:], G.ones[:], s_[:, 0:1], None, ALU.mult, None, [s_.k, "ones"], [sb_.k])
                pc_ = psc.next()
                P.mm(pc_[:, 0:128], ks[:], vTok[:, tt, h, :], True, True, [ks.k, vTok.k], [pc_.k])
                P.mm(pc_[:, 128:256], kTok[:, tt, h, :], sb_[:], True, True, [kTok.k, sb_.k], [pc_.k])
                P.stt(CT[:, h, :], CT[:, h, :], b_[:, 1:2], pc_[:, 0:128], ALU.mult, ALU.add, [pc_.k, b_.k, CT.k + str(h)], [CT.k + str(h)])
                P.stt(NR[:, h, :], NR[:, h, :], b_[:, 1:2], pc_[:, 128:256], ALU.mult, ALU.add, [pc_.k, b_.k, NR.k + str(h)], [NR.k + str(h)])
                P.cp(CTb[:, h, :], CT[:, h, :], [CT.k + str(h)], [CTb.k + str(h)], eng="act_copy")
                P.cp(NRb[:, h, :], NR[:, h, :], [NR.k + str(h)], [NRb.k + str(h)], eng="act_copy")
    if STOP == 3:
        P.end()
        return
    head_norm_store(P, G, hsum, 4, G.eps5[:, 0:1], lng, None, OFF_C + 1536, AF.Sigmoid, G.hsT_d[2], True, pw)
    P.end()


def phase_gla(G, l):
    P = Phase(G.nc)
    ident = G.ident
    q8 = P.psq(8)
    pw = Rot(q8.tiles[0:2])
    lng = P.sb([128, 4])
    load_cols(P, lng[:], G.d_ln_g[l:l + 1, :], 4, pw, ident)
    aup = P.sb([17, 2, 256])
    for d in range(2):
        P.dma(aup[0:16, d, :], G.d_aup[l, d], [], [aup.k])
        P.dma(aup[16:17, d, :], G.d_ab[l, d:d + 1, :], [], [aup.k])
    adT = P.sb([17, 2, T])
    P.memset(adT[:], 1.0, [], [adT.k])
    for d in range(2):
        P.dma(adT[0:16, d, :], G.pT_d[OFF_D + 1536 + d * 16:OFF_D + 1552 + d * 16, :], [adT.k], [adT.k])
    qT = P.sb([128, 2, T])
    kT = P.sb([128, 2, T])
    for c in range(2):
        P.dma(qT[:, c, :], G.pT_d[OFF_D + c * 128:OFF_D + (c + 1) * 128, :], [], [qT.k])
        P.dma(kT[:, c, :], G.pT_d[OFF_D + 256 + c * 128:OFF_D + 256 + (c + 1) * 128, :], [], [kT.k])
    vTok = P.sb([128, NT, 4, 128], BF16)
    xin = P.rot(2, [128, T])
    ptr = P.rot(1, [128, 512], psum=True)
    for h in range(4):
        x = xin.next()
        P.dma(x[:], G.pT_d[OFF_D + 512 + h * 128:OFF_D + 512 + (h + 1) * 128, :], [], [x.k])
        for t4 in range(0, NT, 4):
            nn = min(4, NT - t4)
            p_ = ptr.next()
            for j in range(nn):
                P.tr(p_[:, j * 128:(j + 1) * 128], x[:, (t4 + j) * 128:(t4 + j + 1) * 128], ident[:], [x.k, "ident"], [p_.k])
            P.cp(vTok[:, t4:t4 + nn, h, :], p_[:, 0:nn * 128].rearrange("p (j t) -> p j t", j=nn), [p_.k], [vTok.k], eng="act_copy")
    osum = P.sb([128, 4, T])
    S2 = P.sb([128, 2, 128])
    S2b = P.sb([128, 2, 128], BF16)
    plb = P.ps([128, 512])
    pla = Rot([Tile(plb.t[:, 0:256], plb.k)])
    pS = Rot([Tile(plb.t[:, 256:512], plb.k)])
    laTok = P.rot(2, [128, 256])
    pbc = Rot(q8.tiles[2:4])
    EQ = P.rot(3, [128, 128])
    EK = P.rot(3, [128, 128])
    qt = P.rot(3, [128, 128], BF16)
    kt = P.rot(3, [128, 128], BF16)
    pkt = P.rot(1, [128, 128], psum=True, dt=BF16)
    ktTok = P.rot(2, [128, 128], BF16)
    pA = Rot(q8.tiles[4:6])
    ATm = P.rot(3, [128, 128], BF16)
    po = Rot([Tile(ptr.tiles[0].t[:, 0:128], ptr.tiles[0].k), P.ps([128, 128])])
    od = P.rot(3, [128, 128])
    for d in range(2):
        tri = G.tri[:, d, :]
        last = 127 if d == 0 else 0
        for c in range(2):
            P.memset(S2[:, c, :], 0.0, [], [S2.k + str(c)])
            P.memset(S2b[:, c, :], 0.0, [], [S2b.k + str(c)])
        for tt in ORDER[d]:
            tsl = slice(tt * 128, (tt + 1) * 128)
            pl = pla.next()
            P.mm(pl[:], adT[:, d, tsl], aup[:, d, :], True, True, [adT.k, aup.k], [pl.k])
            la = laTok.next()
            P.act(la[:], pl[:], AF.Exp, [pl.k], [la.k], scale=-1.0)
            P.act(la[:], la[:], AF.Ln, [la.k], [la.k], bias=G.onec[:, 0:1])
            for c in range(2):
                pb_ = pbc.next()
                P.mm(pb_[:], la[:, c * 128:(c + 1) * 128], tri, True, True, [la.k, "tri"], [pb_.k])
                eq = EQ.next()
                ek = EK.next()
                P.act(eq[:], pb_[:], AF.Exp, [pb_.k], [eq.k], scale=-1.0 / 16)
                P.act(ek[:], pb_[:], AF.Exp, [pb_.k], [ek.k], scale=1.0 / 16)
                q_ = qt.next()
                k_ = kt.next()
                P.stt(q_[:], qT[:, c, tsl], 0.125, eq[:], ALU.mult, ALU.mult, [qT.k, eq.k], [q_.k])
                P.tt(k_[:], kT[:, c, tsl], ek[:], ALU.mult, [kT.k, ek.k], [k_.k], eng="pool")
                pk = pkt.next()
                P.tr(pk[:], k_[:], G.identb[:], [k_.k, "identb"], [pk.k])
                kk = ktTok.next()
                P.cp(kk[:], pk[:], [pk.k], [kk.k], eng="act_copy")
                for j in range(2):
                    h = c * 2 + j
                    hp = slice(j * 64, (j + 1) * 64)
                    pa = pA.next()
                    P.mm(pa[:], k_[hp, :], q_[hp, :], True, True, [k_.k, q_.k], [pa.k])
                    am = ATm.next()
                    P.tt(am[:], pa[:], tri, ALU.mult, [pa.k, "tri"], [am.k])
                    p_o = po.next()
                    P.mm(p_o[:], vTok[:, tt, h, :], am[:], True, False, [vTok.k, am.k], [p_o.k])
                    P.mm(p_o[:], S2b[hp, c, :], q_[hp, :], False, True, [S2b.k + str(c), q_.k], [p_o.k])
                    if d == 0:
                        P.cp(osum[:, h, tsl], p_o[:], [p_o.k], [osum.k], eng="act_copy")
                    else:
                        P.tt(osum[:, h, tsl], osum[:, h, tsl], p_o[:], ALU.add, [p_o.k, osum.k], [osum.k])
                ps_ = pS.next()
                P.mm(ps_[:], kk[:], vTok[:, tt, c * 2:c * 2 + 2, :].rearrange("p a b -> p (a b)"), True, True, [kk.k, vTok.k], [ps_.k])
                for j in range(2):
                    hp = slice(j * 64, (j + 1) * 64)
                    P.tt(S2[hp, c, :], S2[hp, c, :], ps_[hp, j * 128:(j + 1) * 128], ALU.add, [ps_.k, S2.k + str(c)], [S2.k + str(c)])
                P.ts(S2[:, c, :], S2[:, c, :], eq[:, last:last + 1], None, ALU.mult, None, [eq.k, S2.k + str(c)], [S2.k + str(c)])
                P.cp(S2b[:, c, :], S2[:, c, :], [S2.k + str(c)], [S2b.k + str(c)], eng="act_copy")
    head_norm_store(P, G, osum, 4, G.epsc[:, 0:1], lng, None, OFF_D + 1024, AF.Silu, G.hsT_d[3], False, pw)
    P.end()


def phase_down(G, wsrc, nk, gate_of, rhs_res=None, rhs_dram=None):
    P = Phase(G.nc)
    wstg = P.rot(3, [128, 4, 512])
    w2q = P.sb([128, nk, 512], BF16)
    h1t = P.rot(2, [128, nk, 512], BF16) if rhs_dram is not None else None
    hold = P.rot(2, [128, 4, 512])
    hnew = P.rot(2, [128, 4, 512])
    pso = P.rot(4, [128, 512], psum=True)
    for dq in range(4):
        for f4 in range(nk // 4):
            st_ = wstg.next()
            P.dma(st_[:], fm(wsrc)[:, f4 * 4:(f4 + 1) * 4, dq * 512:(dq + 1) * 512], [], [st_.k])
            P.cp(w2q[:, f4 * 4:(f4 + 1) * 4, :], st_[:], [st_.k], [w2q.k + f"_{f4}"], eng="pool")
        wkeys = [w2q.k + f"_{f4}" for f4 in range(nk // 4)]
        for (t0, n) in TG:
            row = 1 if t0 < LC else 0
            if rhs_dram is not None:
                ht = h1t.next()
                nq = nk // 4
                for q4 in range(4):
                    P.dma(ht[:, q4 * nq:(q4 + 1) * nq, 0:n], fm(rhs_dram)[:, q4 * nq:(q4 + 1) * nq, t0:t0 + n], [], [ht.k])
                rhs = lambda fc: ht[:, fc, 0:n]
                rk = ht.k
            else:
                rhs = lambda fc: rhs_res[:, fc, t0:t0 + n]
                rk = rhs_res.k
            ho = hold.next()
            P.dma(ho[:, :, 0:n], fm(G.hT_d)[:, dq * 4:(dq + 1) * 4, t0:t0 + n], ["hT_d"], [ho.k])
            hn = hnew.next()
            for dc in range(4):
                ps = pso.next()
                for fc in range(nk):
                    P.mm(ps[:, 0:n], w2q[:, fc, dc * 128:(dc + 1) * 128], rhs(fc), fc == 0, fc == nk - 1,
                         [wkeys[fc // 4], rk], [ps.k])
                P.stt(hn[:, dc, 0:n], ps[:, 0:n], gate_of(dq * 4 + dc, row), ho[:, dc, 0:n],
                      ALU.mult, ALU.add, [ps.k, ho.k, "gate"], [hn.k])
            P.dma(fm(G.hT_d)[:, dq * 4:(dq + 1) * 4, t0:t0 + n], hn[:, :, 0:n], [hn.k], ["hT_d"], q="pool")
    P.end()


def phase_merge(G, yT, l):
    P = Phase(G.nc)
    hs = P.sb([128, 16, T], BF16)
    for n4 in range(4):
        for (t0, n) in TG:
            P.dma(hs[:, n4 * 4:(n4 + 1) * 4, t0:t0 + n], fm(G.hsT_d[n4])[:, :, t0:t0 + n], [], [hs.k])
    wstg = P.rot(2, [128, 16, 128])
    wbf = P.rot(2, [128, 16, 128], BF16)
    gt = P.rot(4, [128, 512])
    pso = P.rot(4, [128, 512], psum=True)
    acc = P.rot(2, [128, 512])
    for dc in range(NCH):
        st_ = wstg.next()
        P.dma(st_[:], G.br_w[l].rearrange("n (k p) d -> p (n k) d", p=128)[:, :, dc * 128:(dc + 1) * 128], [], [st_.k])
        b = wbf.next()
        P.cp(b[:], st_[:], [st_.k], [b.k], eng="pool")
        for (t0, n) in TG:
            a = acc.next()
            for n4 in range(4):
                g = gt.next()
                r0 = OFF_G + n4 * D + dc * 128
                P.dma(g[:, 0:n], G.pT_d[r0:r0 + 128, t0:t0 + n], [], [g.k])
                ps = pso.next()
                for k in range(4):
                    P.mm(ps[:, 0:n], b[:, n4 * 4 + k, :], hs[:, n4 * 4 + k, t0:t0 + n], k == 0, k == 3, [b.k, hs.k], [ps.k])
                if n4 == 0:
                    P.tt(a[:, 0:n], ps[:, 0:n], g[:, 0:n], ALU.mult, [ps.k, g.k], [a.k])
                else:
                    P.tt(g[:, 0:n], ps[:, 0:n], g[:, 0:n], ALU.mult, [ps.k, g.k], [g.k])
                    if n4 < 3:
                        P.tt(a[:, 0:n], a[:, 0:n], g[:, 0:n], ALU.add, [a.k, g.k], [a.k], eng="pool")
                    else:
                        P.tt(yT[:, dc, t0:t0 + n], a[:, 0:n], g[:, 0:n], ALU.add, [a.k, g.k], [yT.k], eng="pool")
    P.end()


def sub_mixer(G, l):
    Phase.count += 1
    with G.nc.sbuf_tensor(f"hnT_{Phase.count}", [128, NCH, T], BF16) as hn_t:
        hnT = Tile(hn_t, "hnT")
        phase_norm(G, hnT, l, 1)
        phase_inproj(G, hnT, l)
    if G.debug == "inproj":
        return
    for br in G.branches:
        {"a": lambda G, l: phase_rwkv(G, l), "b": phase_gmlp, "c": phase_mlstm, "d": phase_gla}[br](G, l)
    if G.debug == "branches":
        return
    Phase.count += 1
    with G.nc.sbuf_tensor(f"yT_{Phase.count}", [128, NCH, T], BF16) as y_t:
        yT = Tile(y_t, "yT")
        phase_merge(G, yT, l)
        phase_down(G, G.out_w[l], NCH, lambda ch, row: G.gate[:, l, 1, ch, row:row + 1], rhs_res=yT)


def sub_ffn(G, l, s, which):
    Phase.count += 1
    with G.nc.sbuf_tensor(f"hnT_{Phase.count}", [128, NCH, T], BF16) as hn_t:
        hnT = Tile(hn_t, "hnT")
        phase_norm(G, hnT, l, s)
        phase_ffn_up(G, hnT, l, s, which)
    phase_down(G, G.ffn_w2[l, which], NFC, lambda ch, row: G.gate[:, l, s, ch, row:row + 1], rhs_dram=G.h1T_d)


def phase_final(G):
    nc = G.nc
    P = Phase(nc)
    hnT = P.sb([128, NCH, T], F32) if False else None
    hin = P.rot(2, [128, NCH, 512])
    sq = P.rot(2, [128, 512])
    pss = P.rot(2, [128, 512], psum=True)
    rstd = P.rot(2, [128, 512])
    hn = P.rot(2, [128, NCH, 512])
    pst = P.rot(2, [128, 512], psum=True)
    ot = P.rot(2, [128, D])
    for (t0, n) in TG[1:]:
        hi = hin.next()
        P.dma(hi[:, :, 0:n], fm(G.hT_d)[:, :, t0:t0 + n], ["hT_d"], [hi.k])
        ps = pss.next()
        for c in range(NCH):
            s = sq.next()
            P.act(s[:, 0:n], hi[:, c, 0:n], AF.Square, [hi.k], [s.k])
            P.mm(ps[:, 0:n], G.ones[:], s[:, 0:n], c == 0, c == NCH - 1, [s.k, "ones"], [ps.k])
        rs = rstd.next()
        P.act(rs[:, 0:n], ps[:, 0:n], AF.Sqrt, [ps.k], [rs.k], bias=G.epsc[:, 0:1], scale=1.0 / D)
        P.recip(rs[:, 0:n], rs[:, 0:n], [rs.k], [rs.k])
        h2 = hn.next()
        for c in range(NCH):
            P.stt(h2[:, c, 0:n], hi[:, c, 0:n], G.fgT[:, c:c + 1], rs[:, 0:n], ALU.mult, ALU.mult,
                  [hi.k, rs.k, "fgT"], [h2.k])
        for tt in range(n // 128):
            o = ot.next()
            for c4 in range(4):
                pt = pst.next()
                for j in range(4):
                    c = c4 * 4 + j
                    P.tr(pt[:, j * 128:(j + 1) * 128], h2[:, c, tt * 128:(tt + 1) * 128], G.ident[:], [h2.k, "ident"], [pt.k])
                P.cp(o[:, c4 * 512:(c4 + 1) * 512], pt[:], [pt.k], [o.k], eng="dve" if c4 % 2 == 0 else "act_copy")
            tok = t0 - LC + tt * 128
            P.dma(G.out[tok:tok + 128, :], o[:], [o.k], [], q="pool")
    P.end()


_orig_cp = Phase.cp


def _cp(self, out, in_, r, w, eng="dve"):
    if eng == "act_copy":
        self.S.op("act", lambda e: e.copy(out=out, in_=in_), r, w)
    elif eng == "dve":
        self.S.op("dve", lambda e: e.tensor_scalar(out=out, in0=in_, scalar1=1.0, scalar2=None, op0=ALU.mult), r, w)
    else:
        _orig_cp(self, out, in_, r, w, eng)


Phase.cp = _cp


def build(debug=None, depth=DEPTH, branches="abcd"):
    nc = bass.Bass("TRN2", target_bir_lowering=False)
    G = Ctx()
    G.nc = nc
    BIG = ("ada_w", "ffn_w1", "ffn_w3", "ffn_w2", "in_w", "br_w", "out_w", "x_b")
    inp = lambda name, shape: None if (debug == "brtest" and name in BIG) else nc.dram_tensor(name, list(shape), F32, kind="ExternalInput").ap()
    G.x_b = inp("x_b", [2048, D])
    G.ctx_b = inp("ctx_b", [LC, D])
    G.c_b = inp("c_b", [1, D])
    G.c_ctx = inp("c_ctx", [1, D])
    G.ada_w = inp("ada_w", [DEPTH, D, 9 * D])
    G.ada_b = inp("ada_b", [DEPTH, 9 * D])
    G.norm_g = inp("norm_g", [DEPTH, 3, D])
    G.ffn_w1 = inp("ffn_w1", [DEPTH, 2, D, DFF])
    G.ffn_w3 = inp("ffn_w3", [DEPTH, 2, D, DFF])
    G.ffn_w2 = inp("ffn_w2", [DEPTH, 2, DFF, D])
    G.final_g = inp("final_g", [1, D])
    G.identd = inp("identd", [128, 128])
    G.trid = inp("trid", [128, 4, 128])
    G.bonesd = inp("bonesd", [128, 128])
    G.pmaskd = inp("pmaskd", [128, 6])
    for nm, shp in (("in_w", [DEPTH, D, D_IN]), ("in_b", [DEPTH, D_IN]), ("a_mu", [DEPTH, A_COLS]),
                    ("a_w0", [DEPTH, 2, 512]), ("a_wup", [DEPTH, 2, 64, 512]), ("a_a0", [DEPTH, 2, 512]),
                    ("a_aup", [DEPTH, 2, 64, 512]), ("a_gup", [DEPTH, 128, 512]), ("a_kk", [DEPTH, 512]),
                    ("a_ka", [DEPTH, 512]), ("a_rk", [DEPTH, 512]), ("a_ln_g", [DEPTH, 512]), ("a_ln_b", [DEPTH, 512]),
                    ("b_ws", [DEPTH, 4, 128, 128]), ("b_bs", [DEPTH, 4, 128]), ("b_ln_g", [DEPTH, 512]),
                    ("b_ln_b", [DEPTH, 512]), ("c_conv_w", [DEPTH, 3, 1024]), ("c_conv_b", [DEPTH, 1024]),
                    ("c_gate_b", [DEPTH, 2, 2, 4]), ("c_ln_g", [DEPTH, 512]), ("d_aup", [DEPTH, 2, 16, 256]),
                    ("d_ab", [DEPTH, 2, 256]), ("d_ln_g", [DEPTH, 512]), ("br_w", [DEPTH, 4, 512, D]),
                    ("out_w", [DEPTH, D, D])):
        setattr(G, nm, inp(nm, shp))
    G.debug = debug
    G.branches = branches
    G.out = None if debug == "brtest" else nc.dram_tensor("out", [2048, D], F32, kind="ExternalOutput").ap()
    G.hT_d = nc.dram_tensor("hT_d", [D, T], F32, kind="Internal").ap()
    G.h1T_d = nc.dram_tensor("h1T_d", [DFF, T], BF16, kind="Internal").ap()
    G.pT_d = nc.dram_tensor("pT_d", [D_IN, T], F32, kind="ExternalOutput" if debug == "inproj" else ("ExternalInput" if debug == "brtest" else "Internal")).ap()
    G.hsT_d = nc.dram_tensor("hsT_d", [4, 512, T], BF16, kind="ExternalOutput" if debug in ("branches", "brtest") else "Internal").ap()
    if debug == "brtest":
        G.dbg_a = nc.dram_tensor("dbg_a", [8, 128, T], F32, kind="ExternalOutput").ap()
    if debug:
        G.dbg_h = nc.dram_tensor("dbg_h", [D, T], F32, kind="ExternalOutput").ap()
    with contextlib.ExitStack() as st:
        sbt = lambda name, shape, dt=F32: st.enter_context(nc.sbuf_tensor(name, list(shape), dt))
        G.ident = sbt("ident", [128, 128])
        G.ones = sbt("ones", [128, 128])
        G.epsc = sbt("epsc", [128, 1])
        G.modT = sbt("modT", [128, DEPTH, 144, 2])
        G.G1 = sbt("G1", [128, DEPTH, 3, NCH, 2])
        G.gate = sbt("gate", [128, DEPTH, 3, NCH, 2])
        G.fgT = sbt("fgT", [128, NCH])
        G.eps5 = sbt("eps5", [128, 1])
        G.onec = sbt("onec", [128, 1])
        G.tri = sbt("tri", [128, 4, 128])
        G.onesb = sbt("onesb", [128, 128], BF16)
        G.bones = sbt("bones", [128, 128])
        G.pmask = sbt("pmask", [128, 6])
        G.epsa = sbt("epsa", [128, 1])
        G.identb = sbt("identb", [128, 128], BF16)
        P = Phase(nc)
        P.dma(G.ident[:], G.identd[:, :], [], ["ident"])
        P.dma(G.tri[:], G.trid[:, :, :], [], ["tri"])
        P.dma(G.bones[:], G.bonesd[:, :], [], ["bones"])
        P.dma(G.pmask[:], G.pmaskd[:, :], [], ["pmask"])
        P.memset(G.epsa[:], 64e-5, [], ["epsa"])
        P.memset(G.ones[:], 1.0, [], ["ones"])
        P.memset(G.onesb[:], 1.0, [], ["onesb"])
        P.memset(G.epsc[:], 1e-6, [], ["epsc"])
        P.memset(G.eps5[:], 1e-5, [], ["eps5"])
        P.memset(G.onec[:], 1.0, [], ["onec"])
        P.cp(G.identb[:], G.ident[:], ["ident"], ["identb"])
        P.end()
        if debug == "ffntest":
            sub_ffn(G, 0, 0, 0)
            return nc
        if debug == "brtest":
            for br in branches:
                {"a": phase_rwkv, "b": phase_gmlp, "c": phase_mlstm, "d": phase_gla}[br](G, 0)
            return nc
        phase_init(G)
        for l in range(depth):
            if debug == "init":
                break
            if debug == "norm":
                Phase.count += 1
                with G.nc.sbuf_tensor(f"hnT_{Phase.count}", [128, NCH, T], BF16) as hn_t:
                    phase_norm(G, Tile(hn_t, "hnT"), l, 0)
                break
            if debug == "up":
                Phase.count += 1
                with G.nc.sbuf_tensor(f"hnT_{Phase.count}", [128, NCH, T], BF16) as hn_t:
                    phase_norm(G, Tile(hn_t, "hnT"), l, 0)
                    phase_ffn_up(G, Tile(hn_t, "hnT"), l, 0, 0)
                break
            sub_ffn(G, l, 0, 0)
            if debug == "ffn1":
                break
            sub_mixer(G, l)
            if debug in ("inproj", "branches", "mix"):
                break
            sub_ffn(G, l, 2, 1)
            if debug == "ffn2":
                break
        if debug:
            P = Phase(nc)
            t = P.rot(2, [128, NCH, 512])
            for (t0, n) in TG:
                b = t.next()
                P.dma(b[:, :, 0:n], fm(G.hT_d)[:, :, t0:t0 + n], [], [b.k])
                P.dma(fm(G.dbg_h)[:, :, t0:t0 + n], b[:, :, 0:n], [b.k], [], q="pool")
            P.end()
        if debug not in ("init", "norm", "up", "inproj", "branches"):
            phase_final(G)
    return nc


def make_in_maps(inputs):
    f = lambda a: np.ascontiguousarray(a, dtype=np.float32)
    ident = np.eye(128, dtype=np.float32)
    ii = np.arange(128)
    trid = np.stack([(ii[:, None] <= ii[None, :]), (ii[:, None] >= ii[None, :]),
                     (ii[:, None] < ii[None, :]), (ii[:, None] > ii[None, :])], axis=1).astype(np.float32)
    bonesd = ((ii[:, None] // 64) == (ii[None, :] // 64)).astype(np.float32)
    pmaskd = np.stack([(ii % 4 == 0), (ii % 4 == 1), (ii % 4 == 2), (ii % 4 == 3), (ii % 2 == 0), (ii % 2 == 1)],
                      axis=1).astype(np.float32)
    shared = {
        "c_ctx": f(inputs["c_ctx"]).reshape(1, D),
        "ada_w": f(inputs["ada_w"]), "ada_b": f(inputs["ada_b"]), "norm_g": f(inputs["norm_g"]),
        "ffn_w1": f(inputs["ffn_w1"]), "ffn_w3": f(inputs["ffn_w3"]), "ffn_w2": f(inputs["ffn_w2"]),
        "final_g": f(inputs["final_g"]).reshape(1, D), "identd": ident, "trid": trid, "bonesd": bonesd, "pmaskd": pmaskd,
    }
    for nm in ("in_w", "in_b", "a_mu", "a_w0", "a_wup", "a_a0", "a_aup", "a_gup", "a_kk", "a_ka", "a_ln_g", "a_ln_b",
               "b_ws", "b_bs", "b_ln_g", "b_ln_b", "c_conv_w", "c_conv_b", "c_gate_b", "c_ln_g", "d_aup", "d_ab",
               "d_ln_g", "br_w", "out_w"):
        shared[nm] = f(inputs[nm])
    shared["a_rk"] = f(inputs["a_rk"]).reshape(DEPTH, 512)
    maps = []
    for core in range(8):
        b = core % 4
        m = dict(shared)
        m["x_b"] = f(inputs["x"][b])
        m["ctx_b"] = f(inputs["ctx"][b])
        m["c_b"] = f(inputs["c"][b]).reshape(1, D)
        maps.append(m)
    return maps


def kernel(**inputs):
    nc = build()
    maps = make_in_maps(inputs)[:4]
    res = run_bass_kernel_spmd(nc, maps, core_ids=list(range(4)))
    return np.stack([np.asarray(res.results[b]["out"], dtype=np.float32) for b in range(4)], axis=0)
```

```python
import contextlib
import numpy as np
import concourse.bass as bass
import concourse.mybir as mybir
from concourse.bass_utils import run_bass_kernel_spmd

F32 = mybir.dt.float32
BF16 = mybir.dt.bfloat16
AF = mybir.ActivationFunctionType
ALU = mybir.AluOpType
AX = mybir.AxisListType

ENGS = ("pe", "dve", "act", "pool", "sp")
NDMASEM = 24

T = 2304
LC = 256
D = 2048
NCH = 16
DFF = 5632
NFC = 44
DEPTH = 4
A_COLS, B_COLS, C_COLS, D_COLS = 1920, 1024, 2064, 1568
OFF_A, OFF_B, OFF_C, OFF_D, OFF_G = 0, 1920, 2944, 5008, 6576
D_IN = 14768
TG = [(0, 256), (256, 512), (768, 512), (1280, 512), (1792, 512)]
NT = 18


class Sched:
    uid = 0

    def __init__(self, nc):
        self.nc = nc
        self.ops = {e: [] for e in ENGS}
        self.cnt = {e: 0 for e in ENGS}
        self.dcnt = {e: 0 for e in ENGS}
        self.last_w = {}
        self.readers = {}
        self.seen = {e: {} for e in ENGS}

    def _deps(self, reads, writes):
        deps = set()
        for k in reads:
            if k in self.last_w:
                deps.add(self.last_w[k])
        for k in writes:
            if k in self.last_w:
                deps.add(self.last_w[k])
            for r in self.readers.get(k, ()):
                deps.add(r)
        return deps

    def _update(self, tok, reads, writes):
        for k in writes:
            self.last_w[k] = tok
            self.readers[k] = []
        for k in reads:
            self.readers.setdefault(k, []).append(tok)

    def _waits(self, eng, deps):
        waits = []
        seen = self.seen[eng]
        best = {}
        for d in deps:
            if d[0] == "c":
                _, e, idx = d
                if e == eng and e == "pe":
                    continue
                key = ("c", e)
                if best.get(key, 0) < idx:
                    best[key] = idx
            else:
                _, q, i = d
                slot = i % NDMASEM
                val = 16 * (i // NDMASEM + 1)
                key = ("d", q, slot)
                if best.get(key, 0) < val:
                    best[key] = val
        for key, val in best.items():
            if seen.get(key, 0) >= val:
                continue
            seen[key] = val
            waits.append((key, val))
        return waits

    def op(self, eng, fn, reads=(), writes=()):
        deps = self._deps(reads, writes)
        waits = self._waits(eng, deps)
        self.cnt[eng] += 1
        tok = ("c", eng, self.cnt[eng])
        self.ops[eng].append((waits, fn, "c", None))
        self._update(tok, reads, writes)
        return tok

    def dma(self, q, fn, reads=(), writes=()):
        deps = self._deps(reads, writes)
        i = self.dcnt[q]
        if i >= NDMASEM:
            deps.add(("d", q, i - NDMASEM))
        waits = self._waits(q, deps)
        self.dcnt[q] += 1
        tok = ("d", q, i)
        self.ops[q].append((waits, fn, "d", i % NDMASEM))
        self._update(tok, reads, writes)
        return tok

    def emit(self):
        nc = self.nc
        Sched.uid += 1
        u = Sched.uid
        csem = {e: nc.alloc_semaphore(f"c{u}_{e}") for e in ENGS}
        dsem = {}
        for q in ENGS:
            for s in range(min(NDMASEM, self.dcnt[q])):
                dsem[(q, s)] = nc.alloc_semaphore(f"d{u}_{q}_{s}")
        with nc.Block() as block:

            def run(engname, eng):
                for waits, fn, kind, slot in self.ops[engname]:
                    for key, val in waits:
                        if key[0] == "c":
                            eng.wait_ge(csem[key[1]], val)
                        else:
                            eng.wait_ge(dsem[(key[1], key[2])], val)
                    ins = fn(eng)
                    if kind == "c":
                        ins.then_inc(csem[engname], 1)
                    else:
                        ins.then_inc(dsem[(engname, slot)], 16)
                n = self.dcnt[engname]
                for s in range(min(NDMASEM, n)):
                    c = (n - 1 - s) // NDMASEM + 1
                    eng.wait_ge(dsem[(engname, s)], 16 * c)

            if self.ops["sp"]:
                @block.sync
                def _(e):
                    run("sp", e)
            if self.ops["pool"]:
                @block.gpsimd
                def _(e):
                    run("pool", e)
            if self.ops["act"]:
                @block.scalar
                def _(e):
                    run("act", e)
            if self.ops["dve"]:
                @block.vector
                def _(e):
                    run("dve", e)
            if self.ops["pe"]:
                @block.tensor
                def _(e):
                    run("pe", e)
        nc.all_engine_barrier()
        nc.clear_and_free_semaphores(list(csem.values()) + list(dsem.values()))
        nc.all_engine_barrier()


class Tile:
    def __init__(self, t, key):
        self.t = t
        self.k = key

    def __getitem__(self, idx):
        return self.t[idx]


class Rot:
    def __init__(self, tiles):
        self.tiles = tiles
        self.i = 0

    def next(self):
        t = self.tiles[self.i % len(self.tiles)]
        self.i += 1
        return t


class Phase:
    count = 0

    def __init__(self, nc):
        Phase.count += 1
        self.nc = nc
        self.id = Phase.count
        self.n = 0
        self.st = contextlib.ExitStack()
        self.S = Sched(nc)

    def sb(self, shape, dt=F32):
        self.n += 1
        name = f"p{self.id}_{self.n}"
        return Tile(self.st.enter_context(self.nc.sbuf_tensor(name, list(shape), dt)), name)

    def ps(self, shape, dt=F32):
        self.n += 1
        name = f"q{self.id}_{self.n}"
        return Tile(self.st.enter_context(self.nc.psum_tensor(name, list(shape), dt)), name)

    def psq(self, n, width=128, dt=F32):
        per = (512 if dt == F32 else 1024) // width
        tiles = []
        bank = None
        for i in range(n):
            if i % per == 0:
                bank = self.ps([128, 512 if dt == F32 else 1024], dt)
            j = i % per
            tiles.append(Tile(bank.t[:, j * width:(j + 1) * width], bank.k))
        return Rot(tiles)

    def rot(self, n, shape, dt=F32, psum=False):
        return Rot([(self.ps if psum else self.sb)(shape, dt) for _ in range(n)])

    def end(self):
        self.S.emit()
        self.st.close()
        self.nc.all_engine_barrier()

    def mm(self, out, lhsT, rhs, start, stop, r, w):
        self.S.op("pe", lambda e: e.matmul(out, lhsT=lhsT, rhs=rhs, start=start, stop=stop), r, w)

    def tr(self, out, in_, ident, r, w):
        self.S.op("pe", lambda e: e.transpose(out, in_, ident), r, w)

    def act(self, out, in_, func, r, w, bias=None, scale=None, accum=None):
        kw = {}
        if bias is not None:
            kw["bias"] = bias
        if scale is not None:
            kw["scale"] = scale
        if accum is not None:
            kw["accum_out"] = accum
        self.S.op("act", lambda e: e.activation(out=out, in_=in_, func=func, **kw), r, w)

    def tt(self, out, a, b, op, r, w, eng="dve"):
        self.S.op(eng, lambda e: e.tensor_tensor(out=out, in0=a, in1=b, op=op), r, w)

    def ts(self, out, a, s1, s2, op0, op1, r, w, eng="dve"):
        if s2 is None:
            self.S.op(eng, lambda e: e.tensor_scalar(out=out, in0=a, scalar1=s1, scalar2=None, op0=op0), r, w)
        else:
            self.S.op(eng, lambda e: e.tensor_scalar(out=out, in0=a, scalar1=s1, scalar2=s2, op0=op0, op1=op1), r, w)

    def stt(self, out, a, s, b, op0, op1, r, w, eng="dve"):
        self.S.op(eng, lambda e: e.scalar_tensor_tensor(out=out, in0=a, scalar=s, in1=b, op0=op0, op1=op1), r, w)

    def cp(self, out, in_, r, w, eng="dve"):
        self.S.op(eng, lambda e: e.tensor_copy(out=out, in_=in_), r, w)

    def recip(self, out, in_, r, w):
        self.S.op("dve", lambda e: e.reciprocal(out=out, in_=in_), r, w)

    def memset(self, out, val, r, w, eng="dve"):
        self.S.op(eng, lambda e: e.memset(out, val), r, w)

    def dma(self, out, in_, r, w, q="sp"):
        self.S.dma(q, lambda e: e.dma_start(out=out, in_=in_), r, w)


class Ctx:
    pass


def fm(ap2d):
    return ap2d.rearrange("(c p) t -> p c t", p=128)


def phase_init(G):
    nc = G.nc
    P = Phase(nc)
    ident = G.ident
    xin = P.rot(2, [128, D])
    stg = P.rot(2, [128, NCH, 128])
    pst = P.rot(2, [128, 512], psum=True)
    for tt in range(NT):
        xi = xin.next()
        src = G.ctx_b[tt * 128:(tt + 1) * 128, :] if tt < 2 else G.x_b[(tt - 2) * 128:(tt - 1) * 128, :]
        P.dma(xi[:], src, [], [xi.k])
        so = stg.next()
        for c4 in range(4):
            pt = pst.next()
            for j in range(4):
                c = c4 * 4 + j
                P.tr(pt[:, j * 128:(j + 1) * 128], xi[:, c * 128:(c + 1) * 128], ident[:], [xi.k, "ident"], [pt.k])
            P.cp(so[:, c4 * 4:(c4 + 1) * 4, :], pt[:].rearrange("p (j t) -> p j t", j=4), [pt.k], [so.k],
                 eng="dve" if c4 % 2 == 0 else "act_copy")
        P.dma(fm(G.hT_d)[:, :, tt * 128:(tt + 1) * 128], so[:], [so.k], ["hT_d"], q="pool")
    craw = P.sb([32, 128])
    P.dma(craw[0:16, :], G.c_b.rearrange("o (c p) -> (o c) p", p=128), [], [craw.k])
    P.dma(craw[16:32, :], G.c_ctx.rearrange("o (c p) -> (o c) p", p=128), [], [craw.k])
    pc = P.ps([128, 32])
    P.tr(pc[:], craw[:], ident[0:32, 0:32], [craw.k, "ident"], [pc.k])
    condT = P.sb([128, NCH, 2])
    for r in range(2):
        P.act(condT[:, :, r], pc[:, r * 16:(r + 1) * 16], AF.Silu, [pc.k], [condT.k])
    wst = P.rot(3, [128, D])
    pm = P.rot(2, [128, NCH, 2], psum=True)
    maccs = P.rot(2, [128, NCH, 2])
    braw = P.rot(2, [72, 128])
    pb = P.rot(1, [128, 72], psum=True)
    badT = P.sb([128, DEPTH, 144])
    for l in range(DEPTH):
        for hlf in range(2):
            br = braw.next()
            P.dma(br[:], G.ada_b[l:l + 1, hlf * 9216:(hlf + 1) * 9216].rearrange("o (j p) -> (o j) p", p=128), [], [br.k])
            pbt = pb.next()
            P.tr(pbt[:], br[:], ident[0:72, 0:72], [br.k, "ident"], [pbt.k])
            P.cp(badT[:, l, hlf * 72:(hlf + 1) * 72], pbt[:], [pbt.k], [badT.k])
        for n in range(9):
            macc = maccs.next()
            for c in range(NCH):
                w = wst.next()
                P.dma(w[:], G.ada_w[l, c * 128:(c + 1) * 128, n * D:(n + 1) * D], [], [w.k])
                pmt = pm.next()
                for cc in range(NCH):
                    P.mm(pmt[:, cc, :], w[:, cc * 128:(cc + 1) * 128], condT[:, c, :], True, True,
                         [w.k, condT.k], [pmt.k])
                if c == 0:
                    P.cp(macc[:], pmt[:], [pmt.k], [macc.k])
                else:
                    P.tt(macc[:], macc[:], pmt[:], ALU.add, [pmt.k, macc.k], [macc.k])
            for r in range(2):
                P.tt(G.modT[:, l, n * 16:(n + 1) * 16, r], macc[:, :, r], badT[:, l, n * 16:(n + 1) * 16], ALU.add,
                     [macc.k, badT.k], ["modT"])
    png = P.ps([128, DEPTH * 3 * NCH])
    ngv = G.norm_g.rearrange("l s (c p) -> (l s c) p", p=128)
    for half in range(2):
        ngraw = P.sb([96, 128])
        P.dma(ngraw[:], ngv[half * 96:(half + 1) * 96, :], [], [ngraw.k])
        P.tr(png[:, half * 96:(half + 1) * 96], ngraw[:], ident[0:96, 0:96], [ngraw.k, "ident"], [png.k])
    for l in range(DEPTH):
        for s in range(3):
            for r in range(2):
                P.stt(G.G1[:, l, s, :, r], G.modT[:, l, (3 * s + 1) * 16:(3 * s + 2) * 16, r], 1.0,
                      png[:, (l * 3 + s) * 16:(l * 3 + s + 1) * 16], ALU.add, ALU.mult, ["modT", png.k], ["G1"])
                P.ts(G.gate[:, l, s, :, r], G.modT[:, l, (3 * s + 2) * 16:(3 * s + 3) * 16, r],
                     0.5 if s != 1 else 1.0, None, ALU.mult, None, ["modT"], ["gate"])
    fgraw = P.sb([16, 128])
    P.dma(fgraw[:], G.final_g.rearrange("o (c p) -> (o c) p", p=128), [], [fgraw.k])
    P.tr(pc[:, 0:16], fgraw[:], ident[0:16, 0:16], [fgraw.k, "ident"], [pc.k])
    P.cp(G.fgT[:], pc[:, 0:16], [pc.k], ["fgT"])
    P.end()


def norm_to_hnT(P, G, hnT, g1_of_chunk, shift_of_chunk):
    hin = P.rot(2, [128, NCH, 512])
    sq = P.rot(2, [128, 512])
    pss = P.rot(2, [128, 512], psum=True)
    rstd = P.rot(2, [128, 512])
    tmp = P.rot(3, [128, 512])
    for (t0, n) in TG:
        row = 1 if t0 < LC else 0
        hi = hin.next()
        P.dma(hi[:, :, 0:n], fm(G.hT_d)[:, :, t0:t0 + n], ["hT_d"], [hi.k])
        ps = pss.next()
        for c in range(NCH):
            s = sq.next()
            P.act(s[:, 0:n], hi[:, c, 0:n], AF.Square, [hi.k], [s.k])
            P.mm(ps[:, 0:n], G.ones[:], s[:, 0:n], c == 0, c == NCH - 1, [s.k, "ones"], [ps.k])
        rs = rstd.next()
        P.act(rs[:, 0:n], ps[:, 0:n], AF.Sqrt, [ps.k], [rs.k], bias=G.epsc[:, 0:1], scale=1.0 / D)
        P.recip(rs[:, 0:n], rs[:, 0:n], [rs.k], [rs.k])
        for c in range(NCH):
            tm = tmp.next()
            P.stt(tm[:, 0:n], hi[:, c, 0:n], g1_of_chunk(c, row), rs[:, 0:n], ALU.mult, ALU.mult,
                  [hi.k, rs.k, "G1"], [tm.k])
            P.act(hnT[:, c, t0:t0 + n], tm[:, 0:n], AF.Identity, [tm.k, "modT"], [hnT.k + f"_{t0}"],
                  bias=shift_of_chunk(c, row))


def phase_norm(G, hnT, l, s):
    P = Phase(G.nc)
    norm_to_hnT(P, G, hnT,
                lambda c, row: G.G1[:, l, s, c, row:row + 1],
                lambda c, row: G.modT[:, l, 3 * s * 16 + c, row:row + 1])
    P.end()


def phase_ffn_up(G, hnT, l, s, which):
    nc = G.nc
    P = Phase(nc)
    FB = 256
    wstg = P.rot(4, [128, NCH, FB])
    wbf = P.rot(4, [128, NCH, FB], BF16)
    ps1 = P.rot(2, [128, 512], psum=True)
    ps3 = P.rot(2, [128, 512], psum=True)
    sil = P.rot(2, [128, 512])
    h1 = P.rot(3, [128, 512], BF16)
    hkeys = [hnT.k + f"_{t0}" for (t0, n) in TG]
    for fb in range(DFF // FB):
        wb = []
        for wsrc in (G.ffn_w1, G.ffn_w3):
            st_ = wstg.next()
            P.dma(st_[:], fm(wsrc[l, which])[:, :, fb * FB:(fb + 1) * FB], [], [st_.k])
            b = wbf.next()
            P.cp(b[:], st_[:], [st_.k], [b.k], eng="pool")
            wb.append(b)
        for gi, (t0, n) in enumerate(TG):
            for fc in range(FB // 128):
                p1 = ps1.next()
                p3 = ps3.next()
                for c in range(NCH):
                    P.mm(p1[:, 0:n], wb[0][:, c, fc * 128:(fc + 1) * 128], hnT[:, c, t0:t0 + n], c == 0, c == NCH - 1,
                         [wb[0].k, hkeys[gi]], [p1.k])
                for c in range(NCH):
                    P.mm(p3[:, 0:n], wb[1][:, c, fc * 128:(fc + 1) * 128], hnT[:, c, t0:t0 + n], c == 0, c == NCH - 1,
                         [wb[1].k, hkeys[gi]], [p3.k])
                sl = sil.next()
                P.act(sl[:, 0:n], p1[:, 0:n], AF.Silu, [p1.k], [sl.k])
                ho = h1.next()
                P.tt(ho[:, 0:n], p3[:, 0:n], sl[:, 0:n], ALU.mult, [p3.k, sl.k], [ho.k])
                f0 = fb * FB + fc * 128
                P.dma(G.h1T_d[f0:f0 + 128, t0:t0 + n], ho[:, 0:n], [ho.k], [], q="pool")
    P.end()


def phase_ffn_down(G, l, s, which):
    nc = G.nc
    P = Phase(nc)
    wstg = P.rot(3, [128, 4, 512])
    w2q = P.sb([128, NFC, 512], BF16)
    h1t = P.rot(2, [128, NFC, 512], BF16)
    hold = P.rot(2, [128, 4, 512])
    hnew = P.rot(2, [128, 4, 512])
    pso = P.rot(4, [128, 512], psum=True)
    for dq in range(4):
        for f4 in range(NFC // 4):
            st_ = wstg.next()
            P.dma(st_[:], fm(G.ffn_w2[l, which])[:, f4 * 4:(f4 + 1) * 4, dq * 512:(dq + 1) * 512], [], [st_.k])
            P.cp(w2q[:, f4 * 4:(f4 + 1) * 4, :], st_[:], [st_.k], [w2q.k + f"_{f4}"], eng="pool")
        wkeys = [w2q.k + f"_{f4}" for f4 in range(NFC // 4)]
        for (t0, n) in TG:
            row = 1 if t0 < LC else 0
            ht = h1t.next()
            for q4 in range(4):
                P.dma(ht[:, q4 * 11:(q4 + 1) * 11, 0:n], fm(G.h1T_d)[:, q4 * 11:(q4 + 1) * 11, t0:t0 + n], [], [ht.k])
            ho = hold.next()
            P.dma(ho[:, :, 0:n], fm(G.hT_d)[:, dq * 4:(dq + 1) * 4, t0:t0 + n], ["hT_d"], [ho.k])
            hn = hnew.next()
            for dc in range(4):
                ps = pso.next()
                for fc in range(NFC):
                    P.mm(ps[:, 0:n], w2q[:, fc, dc * 128:(dc + 1) * 128], ht[:, fc, 0:n], fc == 0, fc == NFC - 1,
                         [wkeys[fc // 4], ht.k], [ps.k])
                P.stt(hn[:, dc, 0:n], ps[:, 0:n], G.gate[:, l, s, dq * 4 + dc, row:row + 1], ho[:, dc, 0:n],
                      ALU.mult, ALU.add, [ps.k, ho.k, "gate"], [hn.k])
            P.dma(fm(G.hT_d)[:, dq * 4:(dq + 1) * 4, t0:t0 + n], hn[:, :, 0:n], [hn.k], ["hT_d"], q="pool")
    P.end()


def in_segments():
    segs = []
    c = 0
    while c < OFF_G:
        n = min(128, OFF_G - c)
        segs.append((c, n, False))
        c += n
    while c < D_IN:
        segs.append((c, 128, True))
        c += 128
    return segs


def phase_inproj(G, hnT, l):
    P = Phase(G.nc)
    segs = in_segments()
    inbT = P.sb([128, len(segs)])
    pbt = P.ps([128, 64])
    braw = P.rot(2, [64, 128])
    r1 = braw.next()
    P.dma(r1[0:51, :], G.in_b[l:l + 1, 0:6528].rearrange("o (j p) -> (o j) p", p=128), [], [r1.k])
    P.tr(pbt[:, 0:51], r1[0:51, :], G.ident[0:51, 0:51], [r1.k, "ident"], [pbt.k])
    P.cp(inbT[:, 0:51], pbt[:, 0:51], [pbt.k], [inbT.k])
    P.dma(inbT[0:48, 51:52], G.in_b[l:l + 1, 6528:6576].rearrange("o p -> p o"), [], [inbT.k])
    r2 = braw.next()
    P.dma(r2[:], G.in_b[l:l + 1, OFF_G:D_IN].rearrange("o (j p) -> (o j) p", p=128), [], [r2.k])
    P.tr(pbt[:, 0:64], r2[:], G.ident[0:64, 0:64], [r2.k, "ident"], [pbt.k])
    P.cp(inbT[:, 52:116], pbt[:, 0:64], [pbt.k], [inbT.k])
    wstg = P.rot(3, [128, NCH, 256])
    wbf = P.rot(3, [128, NCH, 256], BF16)
    pso = P.rot(4, [128, 512], psum=True)
    ot = P.rot(4, [128, 512])
    hkeys = ["hnT" + f"_{t0}" for (t0, n) in TG]
    si = 0
    while si < len(segs):
        blk = [si]
        if si + 1 < len(segs) and segs[si][1] == 128:
            blk.append(si + 1)
        c0 = segs[blk[0]][0]
        w = sum(segs[j][1] for j in blk)
        st_ = wstg.next()
        P.dma(st_[:, :, 0:w], fm(G.in_w[l])[:, :, c0:c0 + w], [], [st_.k])
        b = wbf.next()
        P.cp(b[:, :, 0:w], st_[:, :, 0:w], [st_.k], [b.k], eng="pool")
        for gi, (t0, n) in enumerate(TG):
            for j in blk:
                col, ns, sig = segs[j]
                off = col - c0
                ps = pso.next()
                for c in range(NCH):
                    P.mm(ps[0:ns, 0:n], b[:, c, off:off + ns], hnT[:, c, t0:t0 + n], c == 0, c == NCH - 1,
                         [b.k, hkeys[gi]], [ps.k])
                o = ot.next()
                if sig:
                    P.act(o[0:ns, 0:n], ps[0:ns, 0:n], AF.Sigmoid, [ps.k, inbT.k], [o.k], bias=inbT[0:ns, j:j + 1])
                else:
                    P.ts(o[0:ns, 0:n], ps[0:ns, 0:n], inbT[0:ns, j:j + 1], None, ALU.add, None, [ps.k, inbT.k], [o.k])
                P.dma(G.pT_d[col:col + ns, t0:t0 + n], o[0:ns, 0:n], [o.k], [], q="pool")
        si += len(blk)
    P.end()


def phase_gmlp(G, l):
    P = Phase(G.nc)
    ident = G.ident
    wsT = P.sb([128, 4, 128])
    pw = P.rot(2, [128, 128], psum=True)
    wraw = P.rot(2, [128, 128])
    bsbc = P.sb([128, 4, 128])
    for g in range(4):
        wr = wraw.next()
        P.dma(wr[:], G.b_ws[l, g], [], [wr.k])
        p_ = pw.next()
        P.tr(p_[:], wr[:], ident[:], [wr.k, "ident"], [p_.k])
        P.cp(wsT[:, g, :], p_[:], [p_.k], [wsT.k])
        brow = P.sb([1, 128])
        P.dma(brow[:], G.b_bs[l, g:g + 1, :], [], [brow.k])
        p2 = pw.next()
        P.mm(p2[:], G.ones[0:1, :], brow[:], True, True, [brow.k, "ones"], [p2.k])
        P.cp(bsbc[:, g, :], p2[:], [p2.k], [bsbc.k])
    lng = P.sb([128, 8])
    lraw = P.sb([8, 128])
    P.dma(lraw[0:4, :], G.b_ln_g[l:l + 1, :].rearrange("o (j p) -> (o j) p", p=128), [], [lraw.k])
    P.dma(lraw[4:8, :], G.b_ln_b[l:l + 1, :].rearrange("o (j p) -> (o j) p", p=128), [], [lraw.k])
    p_ = pw.next()
    P.tr(p_[:, 0:8], lraw[:], ident[0:8, 0:8], [lraw.k, "ident"], [p_.k])
    P.cp(lng[:], p_[:, 0:8], [p_.k], [lng.k])
    zin = P.rot(2, [128, 8, 512])
    zz = P.rot(2, [128, 8, 512])
    t1 = P.rot(2, [128, 512])
    t2 = P.rot(2, [128, 512])
    ps1 = P.rot(1, [128, 512], psum=True)
    ps2 = P.rot(1, [128, 512], psum=True)
    mean = P.rot(2, [128, 512])
    rstd = P.rot(2, [128, 512])
    vn = P.rot(2, [128, 4, 512])
    vtok = P.rot(3, [128, 128])
    pt = P.rot(2, [128, 128], psum=True)
    pm = P.rot(2, [128, 128], psum=True)
    sres = P.rot(2, [128, 128])
    hb = P.rot(2, [128, 4, 512], BF16)
    for (t0, n) in TG:
        zi = zin.next()
        P.dma(zi[:, :, 0:n], fm(G.pT_d[OFF_B:OFF_B + 1024, :])[:, :, t0:t0 + n], [], [zi.k])
        z = zz.next()
        for c in range(8):
            a = t1.next()
            P.act(a[:, 0:n], zi[:, c, 0:n], AF.Square, [zi.k], [a.k])
            P.ts(a[:, 0:n], a[:, 0:n], 0.044715, 1.0, ALU.mult, ALU.add, [a.k], [a.k])
            P.tt(a[:, 0:n], a[:, 0:n], zi[:, c, 0:n], ALU.mult, [a.k, zi.k], [a.k])
            P.act(a[:, 0:n], a[:, 0:n], AF.Sigmoid, [a.k], [a.k], scale=1.5957691216057308)
            P.tt(z[:, c, 0:n], a[:, 0:n], zi[:, c, 0:n], ALU.mult, [a.k, zi.k], [z.k], eng="pool")
        p1 = ps1.next()
        p2 = ps2.next()
        for c in range(4):
            P.mm(p1[:, 0:n], G.ones[:], z[:, 4 + c, 0:n], c == 0, c == 3, [z.k, "ones"], [p1.k])
        for c in range(4):
            b = t2.next()
            P.act(b[:, 0:n], z[:, 4 + c, 0:n], AF.Square, [z.k], [b.k])
            P.mm(p2[:, 0:n], G.ones[:], b[:, 0:n], c == 0, c == 3, [b.k, "ones"], [p2.k])
        mu = mean.next()
        P.ts(mu[:, 0:n], p1[:, 0:n], 1.0 / 512, None, ALU.mult, None, [p1.k], [mu.k])
        rs = rstd.next()
        P.tt(rs[:, 0:n], mu[:, 0:n], mu[:, 0:n], ALU.mult, [mu.k], [rs.k])
        P.stt(rs[:, 0:n], p2[:, 0:n], 1.0 / 512, rs[:, 0:n], ALU.mult, ALU.subtract, [p2.k, rs.k], [rs.k])
        P.act(rs[:, 0:n], rs[:, 0:n], AF.Sqrt, [rs.k], [rs.k], bias=G.eps5[:, 0:1])
        P.recip(rs[:, 0:n], rs[:, 0:n], [rs.k], [rs.k])
        v = vn.next()
        for c in range(4):
            P.tt(v[:, c, 0:n], z[:, 4 + c, 0:n], mu[:, 0:n], ALU.subtract, [z.k, mu.k], [v.k])
            P.tt(v[:, c, 0:n], v[:, c, 0:n], rs[:, 0:n], ALU.mult, [v.k, rs.k], [v.k])
            P.ts(v[:, c, 0:n], v[:, c, 0:n], lng[:, c:c + 1], lng[:, 4 + c:5 + c], ALU.mult, ALU.add, [v.k, lng.k], [v.k])
        h = hb.next()
        for j in range(n // 128):
            for g in range(4):
                p_ = pt.next()
                P.tr(p_[:], v[:, g, j * 128:(j + 1) * 128], ident[:], [v.k, "ident"], [p_.k])
                vt = vtok.next()
                P.cp(vt[:], p_[:], [p_.k], [vt.k], eng="act_copy")
                pq = pm.next()
                P.mm(pq[:], vt[:], wsT[:, g, :], True, True, [vt.k, wsT.k], [pq.k])
                sr = sres.next()
                P.tt(sr[:], pq[:], bsbc[:, g, :], ALU.add, [pq.k, bsbc.k], [sr.k])
                P.tt(h[:, g, j * 128:(j + 1) * 128], sr[:], z[:, g, j * 128:(j + 1) * 128], ALU.mult, [sr.k, z.k], [h.k], eng="pool")
        P.dma(fm(G.hsT_d[1])[:, :, t0:t0 + n], h[:, :, 0:n], [h.k], [], q="pool")
    P.end()


import os
STOP = int(os.environ.get("KSTOP", "0"))
NIT = int(os.environ.get("KNIT", "7"))
ORDER = {0: list(range(NT)), 1: [1, 0] + list(range(NT - 1, 1, -1))}


def load_cols(P, dst, src_row_ap, nrows, pw, ident):
    raw = P.sb([nrows, 128])
    P.dma(raw[:], src_row_ap.rearrange("o (j p) -> (o j) p", p=128), [], [raw.k])
    p_ = pw.next()
    P.tr(p_[:, 0:nrows], raw[:], ident[0:nrows, 0:nrows], [raw.k, "ident"], [p_.k])
    P.cp(dst, p_[:, 0:nrows], [p_.k], ["cols"])


def head_norm_store(P, G, hsum, n_heads_chunks, eps_col, gcols, bcols, gate_rows, gate_func, dst, sub_mean, pw):
    sq = P.rot(2, [128, 512])
    ps1 = P.rot(1, [128, 512], psum=True)
    ps2 = P.rot(1, [128, 512], psum=True)
    mu = P.rot(2, [128, 512])
    rs = P.rot(2, [128, 512])
    xc = P.rot(2, [128, 512])
    gin = P.rot(2, [128, 512])
    ob = P.rot(2, [128, 512], BF16)
    for (t0, n) in TG:
        for h in range(4):
            x = hsum[:, h, t0:t0 + n]
            xk = hsum.k
            c = xc.next()
            r = rs.next()
            if sub_mean:
                p1 = ps1.next()
                P.mm(p1[:, 0:n], G.ones[:], x, True, True, [xk, "ones"], [p1.k])
                m = mu.next()
                P.ts(m[:, 0:n], p1[:, 0:n], 1.0 / 128, None, ALU.mult, None, [p1.k], [m.k])
                P.tt(c[:, 0:n], x, m[:, 0:n], ALU.subtract, [xk, m.k], [c.k])
            else:
                P.cp(c[:, 0:n], x, [xk], [c.k], eng="pool")
            q = sq.next()
            P.act(q[:, 0:n], c[:, 0:n], AF.Square, [c.k], [q.k])
            p2 = ps2.next()
            P.mm(p2[:, 0:n], G.ones[:], q[:, 0:n], True, True, [q.k, "ones"], [p2.k])
            P.act(r[:, 0:n], p2[:, 0:n], AF.Sqrt, [p2.k], [r.k], bias=eps_col, scale=1.0 / 128)
            P.recip(r[:, 0:n], r[:, 0:n], [r.k], [r.k])
            P.tt(c[:, 0:n], c[:, 0:n], r[:, 0:n], ALU.mult, [c.k, r.k], [c.k])
            if bcols is not None:
                P.ts(c[:, 0:n], c[:, 0:n], gcols[:, h:h + 1], bcols[:, h:h + 1], ALU.mult, ALU.add, [c.k, "cols"], [c.k])
            else:
                P.ts(c[:, 0:n], c[:, 0:n], gcols[:, h:h + 1], None, ALU.mult, None, [c.k, "cols"], [c.k])
            if STOP == 4:
                continue
            g = gin.next()
            P.dma(g[:, 0:n], G.pT_d[gate_rows + h * 128:gate_rows + (h + 1) * 128, t0:t0 + n], [], [g.k])
            P.act(g[:, 0:n], g[:, 0:n], gate_func, [g.k], [g.k])
            if STOP == 6:
                continue
            o = ob.next()
            P.tt(o[:, 0:n], c[:, 0:n], g[:, 0:n], ALU.mult, [c.k, g.k], [o.k], eng="pool")
            if STOP == 5:
                continue
            P.dma(dst[h * 128:(h + 1) * 128, t0:t0 + n], o[:, 0:n], [o.k], [], q="sp")


def phase_rwkv(G, l):
    P = Phase(G.nc)
    ident = G.ident
    bones = G.bones
    q8 = P.psq(8)
    pw = Rot(q8.tiles[0:2])
    mu = P.sb([128, 15])
    load_cols(P, mu[:], G.a_mu[l:l + 1, :], 15, pw, ident)
    mu1 = P.sb([128, 15])
    P.ts(mu1[:], mu[:], -1.0, 1.0, ALU.mult, ALU.add, ["cols"], [mu1.k])
    mum = P.sb([128, 6, 15])
    for j in range(6):
        P.ts(mum[:, j, :], mu[:], G.pmask[:, j:j + 1], None, ALU.mult, None, ["cols", "pmask"], [mum.k])
    cols = P.sb([128, 12, 4])
    for i_, src in enumerate((G.a_w0[l, 0:1, :], G.a_w0[l, 1:2, :], G.a_a0[l, 0:1, :], G.a_a0[l, 1:2, :],
                              G.a_kk[l:l + 1, :], G.a_ka[l:l + 1, :], G.a_rk[l:l + 1, :], G.a_ln_g[l:l + 1, :],
                              G.a_ln_b[l:l + 1, :])):
        load_cols(P, cols[:, i_, :], src, 4, pw, ident)
    P.ts(cols[:, 9, :], cols[:, 5, :], -1.0, 1.0, ALU.mult, ALU.add, ["cols"], ["cols"])
    wup = P.sb([128, 512])
    aup = P.sb([128, 512])
    gup = P.sb([128, 512])
    for d in range(2):
        P.dma(wup[d * 64:(d + 1) * 64, :], G.a_wup[l, d], [], [wup.k])
        P.dma(aup[d * 64:(d + 1) * 64, :], G.a_aup[l, d], [], [aup.k])
    P.dma(gup[:], G.a_gup[l], [], [gup.k])

    def mix(dst, chunk):
        x = xin.next()
        P.dma(x[:], G.pT_d[chunk * 128:(chunk + 1) * 128, :], [], [x.k])
        P.ts(dst[:], x[:], mu1[:, chunk:chunk + 1], None, ALU.mult, None, [x.k, mu1.k], [dst.k])
        m = lambda j: mum[:, j, chunk:chunk + 1]
        rk_ = [x.k, dst.k, mum.k]
        P.stt(dst[:, 1:LC], x[:, 0:LC - 1], m(4), dst[:, 1:LC], ALU.mult, ALU.add, rk_, [dst.k])
        P.stt(dst[:, 0:LC - 1], x[:, 1:LC], m(5), dst[:, 0:LC - 1], ALU.mult, ALU.add, rk_, [dst.k])
        xg = x[:, LC:T].rearrange("p (r w) -> p r w", w=64)
        dg = dst[:, LC:T].rearrange("p (r w) -> p r w", w=64)
        P.stt(dg[:, :, 1:64], xg[:, :, 0:63], m(0), dg[:, :, 1:64], ALU.mult, ALU.add, rk_, [dst.k])
        P.stt(dg[:, :, 0:63], xg[:, :, 1:64], m(1), dg[:, :, 0:63], ALU.mult, ALU.add, rk_, [dst.k])
        P.stt(dst[:, LC + 64:T], x[:, LC:T - 64], m(2), dst[:, LC + 64:T], ALU.mult, ALU.add, rk_, [dst.k])
        P.stt(dst[:, LC:T - 64], x[:, LC + 64:T], m(3), dst[:, LC:T - 64], ALU.mult, ALU.add, rk_, [dst.k])

    xin = P.rot(1, [128, T])
    tw = P.sb([128, T])
    adz = P.sb([128, T])
    sg = P.sb([128, T])
    mix(tw, 12)
    P.act(tw[:], tw[:], AF.Tanh, [tw.k], [tw.k])
    mix(adz, 13)
    mix(sg, 14)
    P.act(sg[:], sg[:], AF.Sigmoid, [sg.k], [sg.k])
    if STOP == 21:
        P.end()
        return
    rT = P.sb([128, T])
    kT = P.sb([128, T])
    vT = P.sb([128, T])
    kkT = P.sb([128, T])
    lw = P.sb([128, T])
    aT = P.sb([128, T])
    ktT = P.sb([128, T])
    bT = P.sb([128, T])
    ysum = P.sb([128, T])
    Vboth = P.sb([128, NT, 128])
    Vpad = P.sb([128, NT, 2, 128])
    Upad = P.sb([128, 2, 128])
    STbd = P.sb([128, 128])
    pbig = P.rot(2, [128, 512], psum=True)
    tmp = P.rot(4, [128, 512])
    sm = P.rot(3, [128, 128])
    pq = Rot([P.ps([128, 128]) for _ in range(4)] + [Tile(q8.tiles[4].t, q8.tiles[4].k)])
    E = P.rot(3, [128, 3, 128])
    ops4 = P.rot(2, [128, 4, 128])
    tokT = P.rot(2, [128, 3, 128])
    Ysb = P.rot(4, [128, 128])
    YTsb = P.rot(4, [128, 128])
    Rsb = P.rot(8, [128, 128])
    msk = P.rot(4, [128, 3, 128])
    Wsb = P.rot(2, [128, 64])
    ob = P.rot(2, [128, 512], BF16)
    P.memset(Upad[:], 0.0, [], [Upad.k + "0", Upad.k + "1"])
    for c in range(4):
        mix(rT, c)
        mix(kT, 4 + c)
        mix(vT, 8 + c)
        if STOP == 25:
            P.end()
            return
        for (t0, n) in TG:
            a = tmp.next()
            P.ts(kkT[:, t0:t0 + n], kT[:, t0:t0 + n], cols[:, 4, c:c + 1], None, ALU.mult, None, [kT.k, "cols"], [kkT.k])
            P.act(a[:, 0:n], kkT[:, t0:t0 + n], AF.Square, [kkT.k], [a.k])
            pb_ = pbig.next()
            P.mm(pb_[:, 0:n], bones[:], a[:, 0:n], True, True, [a.k, "bones"], [pb_.k])
            P.ts(a[:, 0:n], pb_[:, 0:n], 1e-24, None, ALU.max, None, [pb_.k], [a.k])
            P.act(a[:, 0:n], a[:, 0:n], AF.Sqrt, [a.k], [a.k])
            P.recip(a[:, 0:n], a[:, 0:n], [a.k], [a.k])
            P.tt(kkT[:, t0:t0 + n], kkT[:, t0:t0 + n], a[:, 0:n], ALU.mult, [a.k, kkT.k], [kkT.k])
        if STOP == 26:
            P.end()
            return
        P.memset(Vpad[:], 0.0, [], [Vpad.k])
        if STOP == 27:
            P.end()
            return
        for t4 in range(0, NT, 4):
            nn = min(4, NT - t4)
            p_ = pbig.next()
            for j in range(nn):
                P.tr(p_[:, j * 128:(j + 1) * 128], vT[:, (t4 + j) * 128:(t4 + j + 1) * 128], ident[:], [vT.k, "ident"], [p_.k])
            pv = p_[:, 0:nn * 128].rearrange("p (j t) -> p j t", j=nn)
            P.cp(Vboth[:, t4:t4 + nn, :], pv, [p_.k], [Vboth.k], eng="act_copy")
            for j in range(2):
                P.cp(Vpad[:, t4:t4 + nn, j, j * 64:(j + 1) * 64], pv[:, :, j * 64:(j + 1) * 64], [p_.k], [Vpad.k], eng="act_copy")
        if STOP == 22:
            P.end()
            return
        for d in range(2):
            tri = G.tri[:, d, :]
            stri = G.tri[:, 2 + d, :]
            striT = G.tri[:, 3 - d, :]
            last = 127 if d == 0 else 0
            hp_d = slice(d * 64, (d + 1) * 64)
            for (t0, n) in TG:
                pb_ = pbig.next()
                P.mm(pb_[:, 0:n], wup[hp_d, c * 128:(c + 1) * 128], tw[hp_d, t0:t0 + n], True, True, [wup.k, tw.k], [pb_.k])
                P.act(lw[:, t0:t0 + n], pb_[:, 0:n], AF.Sigmoid, [pb_.k, "cols"], [lw.k], bias=cols[:, d, c:c + 1])
                P.ts(lw[:, t0:t0 + n], lw[:, t0:t0 + n], -0.6065306597126334, None, ALU.mult, None, [lw.k], [lw.k])
                pb2 = pbig.next()
                P.mm(pb2[:, 0:n], aup[hp_d, c * 128:(c + 1) * 128], adz[hp_d, t0:t0 + n], True, True, [aup.k, adz.k], [pb2.k])
                P.act(aT[:, t0:t0 + n], pb2[:, 0:n], AF.Sigmoid, [pb2.k, "cols"], [aT.k], bias=cols[:, 2 + d, c:c + 1])
                a = tmp.next()
                P.ts(a[:, 0:n], aT[:, t0:t0 + n], cols[:, 5, c:c + 1], cols[:, 9, c:c + 1], ALU.mult, ALU.add, [aT.k, "cols"], [a.k])
                P.tt(ktT[:, t0:t0 + n], kT[:, t0:t0 + n], a[:, 0:n], ALU.mult, [kT.k, a.k], [ktT.k])
                P.tt(bT[:, t0:t0 + n], kkT[:, t0:t0 + n], aT[:, t0:t0 + n], ALU.mult, [kkT.k, aT.k], [bT.k], eng="pool")
            if STOP == 23:
                P.end()
                return
            P.memset(STbd[:], 0.0, [], [STbd.k])
            def prep(tt, inter):
                tsl = slice(tt * 128, (tt + 1) * 128)
                tk = tokT.next()
                p_ = pq.next()
                P.tr(p_[:], lw[:, tsl], ident[:], [lw.k, "ident"], [p_.k])
                P.cp(tk[:, 0, :], p_[:], [p_.k], [tk.k + "a"], eng="act_copy")
                pci = pq.next()
                pce = pq.next()
                P.mm(pci[:], tk[:, 0, :], tri, True, True, [tk.k + "a", "tri"], [pci.k])
                P.mm(pce[:], tk[:, 0, :], stri, True, True, [tk.k + "a", "tri"], [pce.k])
                e = E.next()
                P.act(e[:, 0, :], pci[:], AF.Exp, [pci.k], [e.k])
                P.act(e[:, 1, :], pce[:], AF.Exp, [pce.k], [e.k])
                P.act(e[:, 2, :], pci[:], AF.Exp, [pci.k], [e.k], scale=-1.0)
                o4 = ops4.next()
                P.tt(o4[:, 0, :], kkT[:, tsl], e[:, 1, :], ALU.mult, [kkT.k, e.k], [o4.k])
                P.tt(o4[:, 1, :], bT[:, tsl], e[:, 2, :], ALU.mult, [bT.k, e.k], [o4.k], eng="pool")
                P.tt(o4[:, 2, :], ktT[:, tsl], e[:, 2, :], ALU.mult, [ktT.k, e.k], [o4.k])
                P.tt(o4[:, 3, :], rT[:, tsl], e[:, 0, :], ALU.mult, [rT.k, e.k], [o4.k], eng="pool")
                p_ = pq.next()
                P.tr(p_[:], o4[:, 1, :], ident[:], [o4.k, "ident"], [p_.k])
                P.ts(tk[:, 1, :], p_[:], -1.0, None, ALU.mult, None, [p_.k], [tk.k + "b"])
                p_ = pq.next()
                P.tr(p_[:], o4[:, 2, :], ident[:], [o4.k, "ident"], [p_.k])
                P.cp(tk[:, 2, :], p_[:], [p_.k], [tk.k + "c"], eng="act_copy")
                mks = []
                Ys, YTs, Rs = [None, None], [None, None], [None, None]
                for j in range(2):
                    hp = slice(j * 64, (j + 1) * 64)
                    kkg, bh, kth, rt = o4[hp, 0, :], o4[hp, 1, :], o4[hp, 2, :], o4[hp, 3, :]
                    p1 = pq.next()
                    P.mm(p1[:], bh, kkg, True, True, [o4.k], [p1.k])
                    Y = Ysb.next()
                    P.stt(Y[:], p1[:], -1.0, stri, ALU.mult, ALU.mult, [p1.k, "tri"], [Y.k])
                    p2 = pq.next()
                    P.mm(p2[:], kkg, bh, True, True, [o4.k], [p2.k])
                    YT = YTsb.next()
                    P.stt(YT[:], p2[:], -1.0, striT, ALU.mult, ALU.mult, [p2.k, "tri"], [YT.k])
                    R = Rsb.next()
                    P.tt(R[:], Y[:], ident[:], ALU.add, [Y.k, "ident"], [R.k], eng="pool")
                    Ys[j], YTs[j], Rs[j] = Y, YT, R
                for j in range(2):
                    hp = slice(j * 64, (j + 1) * 64)
                    kkg, bh, kth, rt = o4[hp, 0, :], o4[hp, 1, :], o4[hp, 2, :], o4[hp, 3, :]
                    mk = msk.next()
                    p3 = pq.next()
                    P.mm(p3[:], kth, kkg, True, True, [o4.k], [p3.k])
                    P.tt(mk[:, 0, :], p3[:], stri, ALU.mult, [p3.k, "tri"], [mk.k + "0"])
                    p4 = pq.next()
                    P.mm(p4[:], bh, rt, True, True, [o4.k], [p4.k])
                    P.stt(mk[:, 1, :], p4[:], -1.0, tri, ALU.mult, ALU.mult, [p4.k, "tri"], [mk.k + "1"])
                    p5 = pq.next()
                    P.mm(p5[:], kth, rt, True, True, [o4.k], [p5.k])
                    P.tt(mk[:, 2, :], p5[:], tri, ALU.mult, [p5.k, "tri"], [mk.k + "2"])
                    mks.append(mk)
                for it in range(1, 7):
                    for j in range(2):
                        Y, YT, R = Ys[j], YTs[j], Rs[j]
                        pyT = pq.next()
                        P.mm(pyT[:], Y[:], YT[:], True, True, [Y.k, YT.k], [pyT.k])
                        if it < 6:
                            py = pq.next()
                            P.mm(py[:], YT[:], Y[:], True, True, [Y.k, YT.k], [py.k])
                            Y2 = Ysb.next()
                            P.cp(Y2[:], py[:], [py.k], [Y2.k], eng="act_copy")
                        YT2 = YTsb.next()
                        P.tt(YT2[:], pyT[:], G.ones[:], ALU.mult, [pyT.k, "ones"], [YT2.k])
                        pr = pq.next()
                        P.mm(pr[:], YT2[:], R[:], True, True, [YT2.k, R.k], [pr.k])
                        R2 = Rsb.next()
                        P.tt(R2[:], pr[:], R[:], ALU.add, [pr.k, R.k], [R2.k])
                        Rs[j] = R2
                        YTs[j] = YT2
                        if it < 6:
                            Ys[j] = Y2
                    if inter:
                        inter.pop(0)()
                while inter:
                    inter.pop(0)()
                return dict(tt=tt, o4=o4, tk=tk, e=e, mks=mks, Rs=Rs)
            def finish_stages(cx):
                tt, o4, tk, e, mks, Rs = cx["tt"], cx["o4"], cx["tk"], cx["e"], cx["mks"], cx["Rs"]
                tsl = slice(tt * 128, (tt + 1) * 128)
                Ws = [None, None]

                def fW():
                    for j in range(2):
                        hp = slice(j * 64, (j + 1) * 64)
                        kkg = o4[hp, 0, :]
                        mk = mks[j]
                        pW = pq.next()
                        P.mm(pW[:, 0:64], kkg, STbd[hp, j * 64:(j + 1) * 64], True, False, [o4.k, STbd.k], [pW.k])
                        P.mm(pW[:, 0:64], mk[:, 0, :], Vboth[:, tt, j * 64:(j + 1) * 64], False, True, [mk.k + "0", Vboth.k], [pW.k])
                        W_ = Wsb.next()
                        P.cp(W_[:], pW[:, 0:64], [pW.k], [W_.k], eng="act_copy")
                        Ws[j] = W_

                def fU():
                    for j in range(2):
                        R = Rs[j]
                        W_ = Ws[j]
                        pU = pq.next()
                        P.mm(pU[:, 0:64], R[:], W_[:], True, True, [R.k, W_.k], [pU.k])
                        P.cp(Upad[:, j, j * 64:(j + 1) * 64], pU[:, 0:64], [pU.k], [Upad.k + str(j)], eng="act_copy")

                def fY():
                    pY = pq.next()
                    P.mm(pY[:], STbd[:], o4[:, 3, :], True, False, [STbd.k, o4.k], [pY.k])
                    for j in range(2):
                        P.mm(pY[:], Upad[:, j, :], mks[j][:, 1, :], False, False, [Upad.k + str(j), mks[j].k + "1"], [pY.k])
                        P.mm(pY[:], Vpad[:, tt, j, :], mks[j][:, 2, :], False, j == 1, [Vpad.k, mks[j].k + "2"], [pY.k])
                    if d == 0:
                        P.cp(ysum[:, tsl], pY[:], [pY.k], [ysum.k], eng="act_copy")
                    else:
                        yt_ = sm.next()
                        P.cp(yt_[:], pY[:], [pY.k], [yt_.k], eng="act_copy")
                        P.tt(ysum[:, tsl], ysum[:, tsl], yt_[:], ALU.add, [yt_.k, ysum.k], [ysum.k])

                def fS():
                    pS = pq.next()
                    P.mm(pS[:], tk[:, 2, :], Vboth[:, tt, :], True, False, [tk.k + "c", Vboth.k], [pS.k])
                    P.mm(pS[:], tk[:, 1, :], Upad[:, 0, :], False, False, [tk.k + "b", Upad.k + "0"], [pS.k])
                    P.mm(pS[:], tk[:, 1, :], Upad[:, 1, :], False, True, [tk.k + "b", Upad.k + "1"], [pS.k])
                    s_ = sm.next()
                    P.cp(s_[:], pS[:], [pS.k], [s_.k], eng="act_copy")
                    P.tt(s_[:], s_[:], bones[:], ALU.mult, [s_.k, "bones"], [s_.k])
                    P.tt(STbd[:], STbd[:], s_[:], ALU.add, [s_.k, STbd.k], [STbd.k], eng="pool")
                    P.ts(STbd[:], STbd[:], e[:, 0, last:last + 1], None, ALU.mult, None, [e.k, STbd.k], [STbd.k])
                    if STOP == 24:
                        P.end()
                        return

                return [fW, fU, fY, fS]

            prev = None
            for tt in ORDER[d]:
                stages = finish_stages(prev) if prev is not None else []
                prev = prep(tt, stages)
            for f_ in finish_stages(prev):
                f_()
        if G.debug == "brtest" and c == 0:
            for i_, t_ in enumerate((ysum, kkT, lw, aT, ktT, bT, rT, vT)):
                P.dma(G.dbg_a[i_], t_[:], [t_.k], [], q="sp")
        for (t0, n) in TG:
            pm_ = pbig.next()
            P.mm(pm_[:, 0:n], bones[:], ysum[:, t0:t0 + n], True, True, [ysum.k, "bones"], [pm_.k])
            xc = tmp.next()
            P.stt(xc[:, 0:n], pm_[:, 0:n], -1.0 / 64, ysum[:, t0:t0 + n], ALU.mult, ALU.add, [pm_.k, ysum.k], [xc.k])
            sq_ = tmp.next()
            P.act(sq_[:, 0:n], xc[:, 0:n], AF.Square, [xc.k], [sq_.k])
            pv_ = pbig.next()
            P.mm(pv_[:, 0:n], bones[:], sq_[:, 0:n], True, True, [sq_.k, "bones"], [pv_.k])
            P.act(sq_[:, 0:n], pv_[:, 0:n], AF.Sqrt, [pv_.k], [sq_.k], bias=G.epsa[:, 0:1], scale=1.0 / 64)
            P.recip(sq_[:, 0:n], sq_[:, 0:n], [sq_.k], [sq_.k])
            P.tt(xc[:, 0:n], xc[:, 0:n], sq_[:, 0:n], ALU.mult, [xc.k, sq_.k], [xc.k])
            P.ts(xc[:, 0:n], xc[:, 0:n], cols[:, 7, c:c + 1], cols[:, 8, c:c + 1], ALU.mult, ALU.add, [xc.k, "cols"], [xc.k])
            rk_ = tmp.next()
            P.stt(rk_[:, 0:n], rT[:, t0:t0 + n], cols[:, 6, c:c + 1], kT[:, t0:t0 + n], ALU.mult, ALU.mult, [rT.k, kT.k, "cols"], [rk_.k])
            pb_ = pbig.next()
            P.mm(pb_[:, 0:n], bones[:], rk_[:, 0:n], True, True, [rk_.k, "bones"], [pb_.k])
            P.tt(rk_[:, 0:n], pb_[:, 0:n], vT[:, t0:t0 + n], ALU.mult, [pb_.k, vT.k], [rk_.k])
            P.tt(xc[:, 0:n], xc[:, 0:n], rk_[:, 0:n], ALU.add, [xc.k, rk_.k], [xc.k], eng="pool")
            pg_ = pbig.next()
            P.mm(pg_[:, 0:n], gup[:, c * 128:(c + 1) * 128], sg[:, t0:t0 + n], True, True, [gup.k, sg.k], [pg_.k])
            o = ob.next()
            P.tt(o[:, 0:n], xc[:, 0:n], pg_[:, 0:n], ALU.mult, [xc.k, pg_.k], [o.k])
            P.dma(G.hsT_d[0][c * 128:(c + 1) * 128, t0:t0 + n], o[:, 0:n], [o.k], [], q="sp")
    P.end()


def phase_mlstm(G, l):
    P = Phase(G.nc)
    ident = G.ident
    pw = P.psq(2)
    cw = P.sb([128, 8, 4])
    for j in range(3):
        load_cols(P, cw[:, :, j], G.c_conv_w[l, j:j + 1, :], 8, pw, ident)
    load_cols(P, cw[:, :, 3], G.c_conv_b[l:l + 1, :], 8, pw, ident)
    lng = P.sb([128, 4])
    load_cols(P, lng[:], G.c_ln_g[l:l + 1, :], 4, pw, ident)
    gb = P.sb([16, 1])
    P.dma(gb[:], G.c_gate_b[l].rearrange("a b (c o) -> (a b c) o", o=1), [], [gb.k])
    qT = P.sb([128, 4, T], BF16)
    kT = P.sb([128, 4, T], BF16)
    kTok = P.sb([128, NT, 4, 128], BF16)
    vTok = P.sb([128, NT, 4, 128], BF16)
    hsum = P.sb([128, 4, T])
    xin = P.rot(1, [128, T])
    yv = P.rot(1, [128, T])
    ptr = P.rot(1, [128, 512], psum=True)
    SEG = [(0, LC), (LC, T - LC)]
    for c in range(8):
        x = xin.next()
        P.dma(x[:], G.pT_d[OFF_C + c * 128:OFF_C + (c + 1) * 128, :], [], [x.k])
        y = yv.next()
        P.ts(y[:], x[:], cw[:, c, 1:2], cw[:, c, 3:4], ALU.mult, ALU.add, [x.k, "cols"], [y.k])
        for (a, n) in SEG:
            P.stt(y[:, a + 1:a + n], x[:, a:a + n - 1], cw[:, c, 0:1], y[:, a + 1:a + n], ALU.mult, ALU.add, [x.k, y.k, "cols"], [y.k])
            P.stt(y[:, a:a + n - 1], x[:, a + 1:a + n], cw[:, c, 2:3], y[:, a:a + n - 1], ALU.mult, ALU.add, [x.k, y.k, "cols"], [y.k])
        if c < 4:
            P.act(qT[:, c, :], y[:], AF.Silu, [y.k], [qT.k])
        else:
            P.act(y[:], y[:], AF.Silu, [y.k], [y.k])
            P.ts(y[:], y[:], 128 ** -0.5, None, ALU.mult, None, [y.k], [y.k])
            P.cp(kT[:, c - 4, :], y[:], [y.k], [kT.k], eng="pool")
            for t4 in range(0, NT, 4):
                nn = min(4, NT - t4)
                p_ = ptr.next()
                for j in range(nn):
                    P.tr(p_[:, j * 128:(j + 1) * 128], y[:, (t4 + j) * 128:(t4 + j + 1) * 128], ident[:], [y.k, "ident"], [p_.k])
                P.cp(kTok[:, t4:t4 + nn, c - 4, :], p_[:, 0:nn * 128].rearrange("p (j t) -> p j t", j=nn), [p_.k], [kTok.k], eng="act_copy")
    for h in range(4):
        x = xin.next()
        P.dma(x[:], G.pT_d[OFF_C + 1024 + h * 128:OFF_C + 1024 + (h + 1) * 128, :], [], [x.k])
        for t4 in range(0, NT, 4):
            nn = min(4, NT - t4)
            p_ = ptr.next()
            for j in range(nn):
                P.tr(p_[:, j * 128:(j + 1) * 128], x[:, (t4 + j) * 128:(t4 + j + 1) * 128], ident[:], [x.k, "ident"], [p_.k])
            P.cp(vTok[:, t4:t4 + nn, h, :], p_[:, 0:nn * 128].rearrange("p (j t) -> p j t", j=nn), [p_.k], [vTok.k], eng="act_copy")
    if STOP == 1:
        P.end()
        return
    gT = Tile(xin.tiles[0].t[0:16, :], xin.tiles[0].k)
    lfT = Tile(yv.tiles[0].t[0:16, :], yv.tiles[0].k)
    P.dma(gT[:], G.pT_d[OFF_C + 2048:OFF_C + 2064, :], [], [gT.k])
    P.ts(gT[:], gT[:], gb[:, 0:1], None, ALU.add, None, [gT.k, gb.k], [gT.k])
    if STOP == 11:
        P.end()
        return
    P.act(lfT[:], gT[:], AF.Exp, [gT.k], [lfT.k], scale=-1.0)
    if STOP == 12:
        P.end()
        return
    P.act(lfT[:], lfT[:], AF.Ln, [lfT.k], [lfT.k], bias=G.onec[0:16, 0:1])
    if STOP == 13:
        P.end()
        return
    gTok = P.sb([128, NT, 16])
    lfTok = P.sb([128, NT, 16])
    for tt in range(NT):
        p_ = pw.next()
        P.mm(p_[:, 0:16], gT[:, tt * 128:(tt + 1) * 128], ident[0:16, 0:16], True, True, [gT.k, "ident"], [p_.k])
        P.mm(p_[:, 16:32], lfT[:, tt * 128:(tt + 1) * 128], ident[0:16, 0:16], True, True, [lfT.k, "ident"], [p_.k])
        P.cp(gTok[:, tt, :], p_[:, 0:16], [p_.k], [gTok.k])
        P.ts(lfTok[:, tt, :], p_[:, 16:32], -1.0, None, ALU.mult, None, [p_.k], [lfTok.k])
    if STOP == 2:
        P.end()
        return
    CT = P.sb([128, 4, 128])
    CTb = P.sb([128, 4, 128], BF16)
    NR = P.sb([128, 4, 128])
    NRb = P.sb([128, 4, 128], BF16)
    q8 = P.psq(8)
    pb4 = Rot([pw.tiles[0]])
    pbb = Rot([Tile(ptr.tiles[0].t[:, 0:128], ptr.tiles[0].k), P.ps([128, 128])])
    pss = Rot([q8.tiles[3]])
    psn = Rot([q8.tiles[4]])
    psd = Rot([q8.tiles[5]])
    pcb = P.ps([128, 512])
    psc = Rot([Tile(pcb.t[:, 0:256], pcb.k)])
    bcol = P.rot(2, [128, 4])
    ib = P.rot(2, [128, 4])
    lmask = P.rot(3, [128, 128])
    DT = P.rot(3, [128, 128])
    EB = P.rot(3, [128, 128])
    PT = P.rot(3, [128, 128], BF16)
    qs = P.rot(3, [128, 128], BF16)
    den = P.rot(3, [128, 128])
    hd = P.rot(3, [128, 128])
    bl = P.rot(3, [128, 2])
    sw = P.rot(3, [128, 1])
    ksw = P.rot(3, [128, 128], BF16)
    swbc = P.rot(3, [128, 128], BF16)
    for d in range(2):
        tri = G.tri[:, d, :]
        last = 127 if d == 0 else 0
        for h in range(4):
            P.memset(CT[:, h, :], 0.0, [], [CT.k + str(h)])
            P.memset(CTb[:, h, :], 0.0, [], [CTb.k + str(h)])
            P.memset(NR[:, h, :], 0.0, [], [NR.k + str(h)])
            P.memset(NRb[:, h, :], 0.0, [], [NRb.k + str(h)])
        for tt in ORDER[d]:
            tsl = slice(tt * 128, (tt + 1) * 128)
            p4 = pb4.next()
            P.mm(p4[:, 0:4], tri, lfTok[:, tt, d * 8 + 4:d * 8 + 8], True, True, ["tri", lfTok.k], [p4.k])
            bc = bcol.next()
            P.cp(bc[:], p4[:, 0:4], [p4.k], [bc.k])
            ibt = ib.next()
            P.tt(ibt[:], gTok[:, tt, d * 8:d * 8 + 4], bc[:], ALU.subtract, [gTok.k, bc.k], [ibt.k])
            for h in range(4):
                lm = lmask.next()
                P.ts(lm[:], tri, lfTok[:, tt, d * 8 + 4 + h:d * 8 + 5 + h], None, ALU.mult, None, ["tri", lfTok.k], [lm.k])
                pB = pbb.next()
                P.mm(pB[:], G.ones[:], lm[:], True, True, [lm.k, "ones"], [pB.k])
                dt_ = DT.next()
                P.act(dt_[:], pB[:], AF.Exp, [pB.k, ibt.k], [dt_.k], bias=ibt[:, h:h + 1])
                eb = EB.next()
                P.act(eb[:], pB[:], AF.Exp, [pB.k], [eb.k])
                b_ = bl.next()
                P.cp(b_[:, 0:1], pB[:, last:last + 1], [pB.k], [b_.k])
                P.tt(dt_[:], dt_[:], tri, ALU.mult, [dt_.k, "tri"], [dt_.k], eng="pool")
                ps_s = pss.next()
                P.mm(ps_s[:], kT[:, h, tsl], qT[:, h, tsl], True, True, [kT.k, qT.k], [ps_s.k])
                pt_ = PT.next()
                P.tt(pt_[:], ps_s[:], dt_[:], ALU.mult, [ps_s.k, dt_.k], [pt_.k])
                q_ = qs.next()
                P.tt(q_[:], qT[:, h, tsl], eb[:], ALU.mult, [qT.k, eb.k], [q_.k], eng="pool")
                pn = psn.next()
                P.mm(pn[:], vTok[:, tt, h, :], pt_[:], True, False, [vTok.k, pt_.k], [pn.k])
                P.mm(pn[:], CTb[:, h, :], q_[:], False, True, [CTb.k + str(h), q_.k], [pn.k])
                pd = psd.next()
                P.mm(pd[:], G.onesb[:], pt_[:], True, False, [pt_.k, "onesb"], [pd.k])
                P.mm(pd[:], NRb[:, h, :], q_[:], False, True, [NRb.k + str(h), q_.k], [pd.k])
                de = den.next()
                P.act(de[:], pd[:], AF.Abs, [pd.k], [de.k])
                P.ts(de[:], de[:], 1.0, None, ALU.max, None, [de.k], [de.k])
                P.recip(de[:], de[:], [de.k], [de.k])
                if d == 0:
                    P.tt(hsum[:, h, tsl], pn[:], de[:], ALU.mult, [pn.k, de.k], [hsum.k])
                else:
                    hh = hd.next()
                    P.tt(hh[:], pn[:], de[:], ALU.mult, [pn.k, de.k], [hh.k])
                    P.tt(hsum[:, h, tsl], hsum[:, h, tsl], hh[:], ALU.add, [hh.k, hsum.k], [hsum.k], eng="pool")
                s_ = sw.next()
                P.act(s_[:], ibt[:, h:h + 1], AF.Exp, [ibt.k, b_.k], [s_.k], bias=b_[:, 0:1])
                P.act(b_[:, 1:2], b_[:, 0:1], AF.Exp, [b_.k], [b_.k])
                ks = ksw.next()
                P.ts(ks[:], kTok[:, tt, h, :], s_[:, 0:1], None, ALU.mult, None, [kTok.k, s_.k], [ks.k])
                sb_ = swbc.next()
                P.ts(sb_[:], G.ones[:], s_[:, 0:1], None, ALU.mult, None, [s_.k, "ones"], [sb_.k])
                pc_ = psc.next()
                P.mm(pc_[:, 0:128], ks[:], vTok[:, tt, h, :], True, True, [ks.k, vTok.k], [pc_.k])
                P.mm(pc_[:, 128:256], kTok[:, tt, h, :], sb_[:], True, True, [kTok.k, sb_.k], [pc_.k])
                P.stt(CT[:, h, :], CT[:, h, :], b_[:, 1:2], pc_[:, 0:128], ALU.mult, ALU.add, [pc_.k, b_.k, CT.k + str(h)], [CT.k + str(h)])
                P.stt(NR[:, h, :], NR[:, h, :], b_[:, 1:2], pc_[:, 128:256], ALU.mult, ALU.add, [pc_.k, b_.k, NR.k + str(h)], [NR.k + str(h)])
                P.cp(CTb[:, h, :], CT[:, h, :], [CT.k + str(h)], [CTb.k + str(h)], eng="act_copy")
                P.cp(NRb[:, h, :], NR[:, h, :], [NR.k + str(h)], [NRb.k + str(h)], eng="act_copy")
    if STOP == 3:
        P.end()
        return
    head_norm_store(P, G, hsum, 4, G.eps5[:, 0:1], lng, None, OFF_C + 1536, AF.Sigmoid, G.hsT_d[2], True, pw)
    P.end()


def phase_gla(G, l):
    P = Phase(G.nc)
    ident = G.ident
    q8 = P.psq(8)
    pw = Rot(q8.tiles[0:2])
    lng = P.sb([128, 4])
    load_cols(P, lng[:], G.d_ln_g[l:l + 1, :], 4, pw, ident)
    aup = P.sb([17, 2, 256])
    for d in range(2):
        P.dma(aup[0:16, d, :], G.d_aup[l, d], [], [aup.k])
        P.dma(aup[16:17, d, :], G.d_ab[l, d:d + 1, :], [], [aup.k])
    adT = P.sb([17, 2, T])
    P.memset(adT[:], 1.0, [], [adT.k])
    for d in range(2):
        P.dma(adT[0:16, d, :], G.pT_d[OFF_D + 1536 + d * 16:OFF_D + 1552 + d * 16, :], [adT.k], [adT.k])
    qT = P.sb([128, 2, T])
    kT = P.sb([128, 2, T])
    for c in range(2):
        P.dma(qT[:, c, :], G.pT_d[OFF_D + c * 128:OFF_D + (c + 1) * 128, :], [], [qT.k])
        P.dma(kT[:, c, :], G.pT_d[OFF_D + 256 + c * 128:OFF_D + 256 + (c + 1) * 128, :], [], [kT.k])
    vTok = P.sb([128, NT, 4, 128], BF16)
    xin = P.rot(2, [128, T])
    ptr = P.rot(1, [128, 512], psum=True)
    for h in range(4):
        x = xin.next()
        P.dma(x[:], G.pT_d[OFF_D + 512 + h * 128:OFF_D + 512 + (h + 1) * 128, :], [], [x.k])
        for t4 in range(0, NT, 4):
            nn = min(4, NT - t4)
            p_ = ptr.next()
            for j in range(nn):
                P.tr(p_[:, j * 128:(j + 1) * 128], x[:, (t4 + j) * 128:(t4 + j + 1) * 128], ident[:], [x.k, "ident"], [p_.k])
            P.cp(vTok[:, t4:t4 + nn, h, :], p_[:, 0:nn * 128].rearrange("p (j t) -> p j t", j=nn), [p_.k], [vTok.k], eng="act_copy")
    osum = P.sb([128, 4, T])
    S2 = P.sb([128, 2, 128])
    S2b = P.sb([128, 2, 128], BF16)
    plb = P.ps([128, 512])
    pla = Rot([Tile(plb.t[:, 0:256], plb.k)])
    pS = Rot([Tile(plb.t[:, 256:512], plb.k)])
    laTok = P.rot(2, [128, 256])
    pbc = Rot(q8.tiles[2:4])
    EQ = P.rot(3, [128, 128])
    EK = P.rot(3, [128, 128])
    qt = P.rot(3, [128, 128], BF16)
    kt = P.rot(3, [128, 128], BF16)
    pkt = P.rot(1, [128, 128], psum=True, dt=BF16)
    ktTok = P.rot(2, [128, 128], BF16)
    pA = Rot(q8.tiles[4:6])
    ATm = P.rot(3, [128, 128], BF16)
    po = Rot([Tile(ptr.tiles[0].t[:, 0:128], ptr.tiles[0].k), P.ps([128, 128])])
    od = P.rot(3, [128, 128])
    for d in range(2):
        tri = G.tri[:, d, :]
        last = 127 if d == 0 else 0
        for c in range(2):
            P.memset(S2[:, c, :], 0.0, [], [S2.k + str(c)])
            P.memset(S2b[:, c, :], 0.0, [], [S2b.k + str(c)])
        for tt in ORDER[d]:
            tsl = slice(tt * 128, (tt + 1) * 128)
            pl = pla.next()
            P.mm(pl[:], adT[:, d, tsl], aup[:, d, :], True, True, [adT.k, aup.k], [pl.k])
            la = laTok.next()
            P.act(la[:], pl[:], AF.Exp, [pl.k], [la.k], scale=-1.0)
            P.act(la[:], la[:], AF.Ln, [la.k], [la.k], bias=G.onec[:, 0:1])
            for c in range(2):
                pb_ = pbc.next()
                P.mm(pb_[:], la[:, c * 128:(c + 1) * 128], tri, True, True, [la.k, "tri"], [pb_.k])
                eq = EQ.next()
                ek = EK.next()
                P.act(eq[:], pb_[:], AF.Exp, [pb_.k], [eq.k], scale=-1.0 / 16)
                P.act(ek[:], pb_[:], AF.Exp, [pb_.k], [ek.k], scale=1.0 / 16)
                q_ = qt.next()
                k_ = kt.next()
                P.stt(q_[:], qT[:, c, tsl], 0.125, eq[:], ALU.mult, ALU.mult, [qT.k, eq.k], [q_.k])
                P.tt(k_[:], kT[:, c, tsl], ek[:], ALU.mult, [kT.k, ek.k], [k_.k], eng="pool")
                pk = pkt.next()
                P.tr(pk[:], k_[:], G.identb[:], [k_.k, "identb"], [pk.k])
                kk = ktTok.next()
                P.cp(kk[:], pk[:], [pk.k], [kk.k], eng="act_copy")
                for j in range(2):
                    h = c * 2 + j
                    hp = slice(j * 64, (j + 1) * 64)
                    pa = pA.next()
                    P.mm(pa[:], k_[hp, :], q_[hp, :], True, True, [k_.k, q_.k], [pa.k])
                    am = ATm.next()
                    P.tt(am[:], pa[:], tri, ALU.mult, [pa.k, "tri"], [am.k])
                    p_o = po.next()
                    P.mm(p_o[:], vTok[:, tt, h, :], am[:], True, False, [vTok.k, am.k], [p_o.k])
                    P.mm(p_o[:], S2b[hp, c, :], q_[hp, :], False, True, [S2b.k + str(c), q_.k], [p_o.k])
                    if d == 0:
                        P.cp(osum[:, h, tsl], p_o[:], [p_o.k], [osum.k], eng="act_copy")
                    else:
                        P.tt(osum[:, h, tsl], osum[:, h, tsl], p_o[:], ALU.add, [p_o.k, osum.k], [osum.k])
                ps_ = pS.next()
                P.mm(ps_[:], kk[:], vTok[:, tt, c * 2:c * 2 + 2, :].rearrange("p a b -> p (a b)"), True, True, [kk.k, vTok.k], [ps_.k])
                for j in range(2):
                    hp = slice(j * 64, (j + 1) * 64)
                    P.tt(S2[hp, c, :], S2[hp, c, :], ps_[hp, j * 128:(j + 1) * 128], ALU.add, [ps_.k, S2.k + str(c)], [S2.k + str(c)])
                P.ts(S2[:, c, :], S2[:, c, :], eq[:, last:last + 1], None, ALU.mult, None, [eq.k, S2.k + str(c)], [S2.k + str(c)])
                P.cp(S2b[:, c, :], S2[:, c, :], [S2.k + str(c)], [S2b.k + str(c)], eng="act_copy")
    head_norm_store(P, G, osum, 4, G.epsc[:, 0:1], lng, None, OFF_D + 1024, AF.Silu, G.hsT_d[3], False, pw)
    P.end()


def phase_down(G, wsrc, nk, gate_of, rhs_res=None, rhs_dram=None):
    P = Phase(G.nc)
    wstg = P.rot(3, [128, 4, 512])
    w2q = P.sb([128, nk, 512], BF16)
    h1t = P.rot(2, [128, nk, 512], BF16) if rhs_dram is not None else None
    hold = P.rot(2, [128, 4, 512])
    hnew = P.rot(2, [128, 4, 512])
    pso = P.rot(4, [128, 512], psum=True)
    for dq in range(4):
        for f4 in range(nk // 4):
            st_ = wstg.next()
            P.dma(st_[:], fm(wsrc)[:, f4 * 4:(f4 + 1) * 4, dq * 512:(dq + 1) * 512], [], [st_.k])
            P.cp(w2q[:, f4 * 4:(f4 + 1) * 4, :], st_[:], [st_.k], [w2q.k + f"_{f4}"], eng="pool")
        wkeys = [w2q.k + f"_{f4}" for f4 in range(nk // 4)]
        for (t0, n) in TG:
            row = 1 if t0 < LC else 0
            if rhs_dram is not None:
                ht = h1t.next()
                nq = nk // 4
                for q4 in range(4):
                    P.dma(ht[:, q4 * nq:(q4 + 1) * nq, 0:n], fm(rhs_dram)[:, q4 * nq:(q4 + 1) * nq, t0:t0 + n], [], [ht.k])
                rhs = lambda fc: ht[:, fc, 0:n]
                rk = ht.k
            else:
                rhs = lambda fc: rhs_res[:, fc, t0:t0 + n]
                rk = rhs_res.k
            ho = hold.next()
            P.dma(ho[:, :, 0:n], fm(G.hT_d)[:, dq * 4:(dq + 1) * 4, t0:t0 + n], ["hT_d"], [ho.k])
            hn = hnew.next()
            for dc in range(4):
                ps = pso.next()
                for fc in range(nk):
                    P.mm(ps[:, 0:n], w2q[:, fc, dc * 128:(dc + 1) * 128], rhs(fc), fc == 0, fc == nk - 1,
                         [wkeys[fc // 4], rk], [ps.k])
                P.stt(hn[:, dc, 0:n], ps[:, 0:n], gate_of(dq * 4 + dc, row), ho[:, dc, 0:n],
                      ALU.mult, ALU.add, [ps.k, ho.k, "gate"], [hn.k])
            P.dma(fm(G.hT_d)[:, dq * 4:(dq + 1) * 4, t0:t0 + n], hn[:, :, 0:n], [hn.k], ["hT_d"], q="pool")
    P.end()


def phase_merge(G, yT, l):
    P = Phase(G.nc)
    hs = P.sb([128, 16, T], BF16)
    for n4 in range(4):
        for (t0, n) in TG:
            P.dma(hs[:, n4 * 4:(n4 + 1) * 4, t0:t0 + n], fm(G.hsT_d[n4])[:, :, t0:t0 + n], [], [hs.k])
    wstg = P.rot(2, [128, 16, 128])
    wbf = P.rot(2, [128, 16, 128], BF16)
    gt = P.rot(4, [128, 512])
    pso = P.rot(4, [128, 512], psum=True)
    acc = P.rot(2, [128, 512])
    for dc in range(NCH):
        st_ = wstg.next()
        P.dma(st_[:], G.br_w[l].rearrange("n (k p) d -> p (n k) d", p=128)[:, :, dc * 128:(dc + 1) * 128], [], [st_.k])
        b = wbf.next()
        P.cp(b[:], st_[:], [st_.k], [b.k], eng="pool")
        for (t0, n) in TG:
            a = acc.next()
            for n4 in range(4):
                g = gt.next()
                r0 = OFF_G + n4 * D + dc * 128
                P.dma(g[:, 0:n], G.pT_d[r0:r0 + 128, t0:t0 + n], [], [g.k])
                ps = pso.next()
                for k in range(4):
                    P.mm(ps[:, 0:n], b[:, n4 * 4 + k, :], hs[:, n4 * 4 + k, t0:t0 + n], k == 0, k == 3, [b.k, hs.k], [ps.k])
                if n4 == 0:
                    P.tt(a[:, 0:n], ps[:, 0:n], g[:, 0:n], ALU.mult, [ps.k, g.k], [a.k])
                else:
                    P.tt(g[:, 0:n], ps[:, 0:n], g[:, 0:n], ALU.mult, [ps.k, g.k], [g.k])
                    if n4 < 3:
                        P.tt(a[:, 0:n], a[:, 0:n], g[:, 0:n], ALU.add, [a.k, g.k], [a.k], eng="pool")
                    else:
                        P.tt(yT[:, dc, t0:t0 + n], a[:, 0:n], g[:, 0:n], ALU.add, [a.k, g.k], [yT.k], eng="pool")
    P.end()


def sub_mixer(G, l):
    Phase.count += 1
    with G.nc.sbuf_tensor(f"hnT_{Phase.count}", [128, NCH, T], BF16) as hn_t:
        hnT = Tile(hn_t, "hnT")
        phase_norm(G, hnT, l, 1)
        phase_inproj(G, hnT, l)
    if G.debug == "inproj":
        return
    for br in G.branches:
        {"a": lambda G, l: phase_rwkv(G, l), "b": phase_gmlp, "c": phase_mlstm, "d": phase_gla}[br](G, l)
    if G.debug == "branches":
        return
    Phase.count += 1
    with G.nc.sbuf_tensor(f"yT_{Phase.count}", [128, NCH, T], BF16) as y_t:
        yT = Tile(y_t, "yT")
        phase_merge(G, yT, l)
        phase_down(G, G.out_w[l], NCH, lambda ch, row: G.gate[:, l, 1, ch, row:row + 1], rhs_res=yT)


def sub_ffn(G, l, s, which):
    Phase.count += 1
    with G.nc.sbuf_tensor(f"hnT_{Phase.count}", [128, NCH, T], BF16) as hn_t:
        hnT = Tile(hn_t, "hnT")
        phase_norm(G, hnT, l, s)
        phase_ffn_up(G, hnT, l, s, which)
    phase_down(G, G.ffn_w2[l, which], NFC, lambda ch, row: G.gate[:, l, s, ch, row:row + 1], rhs_dram=G.h1T_d)


def phase_final(G):
    nc = G.nc
    P = Phase(nc)
    hnT = P.sb([128, NCH, T], F32) if False else None
    hin = P.rot(2, [128, NCH, 512])
    sq = P.rot(2, [128, 512])
    pss = P.rot(2, [128, 512], psum=True)
    rstd = P.rot(2, [128, 512])
    hn = P.rot(2, [128, NCH, 512])
    pst = P.rot(2, [128, 512], psum=True)
    ot = P.rot(2, [128, D])
    for (t0, n) in TG[1:]:
        hi = hin.next()
        P.dma(hi[:, :, 0:n], fm(G.hT_d)[:, :, t0:t0 + n], ["hT_d"], [hi.k])
        ps = pss.next()
        for c in range(NCH):
            s = sq.next()
            P.act(s[:, 0:n], hi[:, c, 0:n], AF.Square, [hi.k], [s.k])
            P.mm(ps[:, 0:n], G.ones[:], s[:, 0:n], c == 0, c == NCH - 1, [s.k, "ones"], [ps.k])
        rs = rstd.next()
        P.act(rs[:, 0:n], ps[:, 0:n], AF.Sqrt, [ps.k], [rs.k], bias=G.epsc[:, 0:1], scale=1.0 / D)
        P.recip(rs[:, 0:n], rs[:, 0:n], [rs.k], [rs.k])
        h2 = hn.next()
        for c in range(NCH):
            P.stt(h2[:, c, 0:n], hi[:, c, 0:n], G.fgT[:, c:c + 1], rs[:, 0:n], ALU.mult, ALU.mult,
                  [hi.k, rs.k, "fgT"], [h2.k])
        for tt in range(n // 128):
            o = ot.next()
            for c4 in range(4):
                pt = pst.next()
                for j in range(4):
                    c = c4 * 4 + j
                    P.tr(pt[:, j * 128:(j + 1) * 128], h2[:, c, tt * 128:(tt + 1) * 128], G.ident[:], [h2.k, "ident"], [pt.k])
                P.cp(o[:, c4 * 512:(c4 + 1) * 512], pt[:], [pt.k], [o.k], eng="dve" if c4 % 2 == 0 else "act_copy")
            tok = t0 - LC + tt * 128
            P.dma(G.out[tok:tok + 128, :], o[:], [o.k], [], q="pool")
    P.end()


_orig_cp = Phase.cp


def _cp(self, out, in_, r, w, eng="dve"):
    if eng == "act_copy":
        self.S.op("act", lambda e: e.copy(out=out, in_=in_), r, w)
    elif eng == "dve":
        self.S.op("dve", lambda e: e.tensor_scalar(out=out, in0=in_, scalar1=1.0, scalar2=None, op0=ALU.mult), r, w)
    else:
        _orig_cp(self, out, in_, r, w, eng)


Phase.cp = _cp


def build(debug=None, depth=DEPTH, branches="abcd"):
    nc = bass.Bass("TRN2", target_bir_lowering=False)
    G = Ctx()
    G.nc = nc
    BIG = ("ada_w", "ffn_w1", "ffn_w3", "ffn_w2", "in_w", "br_w", "out_w", "x_b")
    inp = lambda name, shape: None if (debug == "brtest" and name in BIG) else nc.dram_tensor(name, list(shape), F32, kind="ExternalInput").ap()
    G.x_b = inp("x_b", [2048, D])
    G.ctx_b = inp("ctx_b", [LC, D])
    G.c_b = inp("c_b", [1, D])
    G.c_ctx = inp("c_ctx", [1, D])
    G.ada_w = inp("ada_w", [DEPTH, D, 9 * D])
    G.ada_b = inp("ada_b", [DEPTH, 9 * D])
    G.norm_g = inp("norm_g", [DEPTH, 3, D])
    G.ffn_w1 = inp("ffn_w1", [DEPTH, 2, D, DFF])
    G.ffn_w3 = inp("ffn_w3", [DEPTH, 2, D, DFF])
    G.ffn_w2 = inp("ffn_w2", [DEPTH, 2, DFF, D])
    G.final_g = inp("final_g", [1, D])
    G.identd = inp("identd", [128, 128])
    G.trid = inp("trid", [128, 4, 128])
    G.bonesd = inp("bonesd", [128, 128])
    G.pmaskd = inp("pmaskd", [128, 6])
    for nm, shp in (("in_w", [DEPTH, D, D_IN]), ("in_b", [DEPTH, D_IN]), ("a_mu", [DEPTH, A_COLS]),
                    ("a_w0", [DEPTH, 2, 512]), ("a_wup", [DEPTH, 2, 64, 512]), ("a_a0", [DEPTH, 2, 512]),
                    ("a_aup", [DEPTH, 2, 64, 512]), ("a_gup", [DEPTH, 128, 512]), ("a_kk", [DEPTH, 512]),
                    ("a_ka", [DEPTH, 512]), ("a_rk", [DEPTH, 512]), ("a_ln_g", [DEPTH, 512]), ("a_ln_b", [DEPTH, 512]),
                    ("b_ws", [DEPTH, 4, 128, 128]), ("b_bs", [DEPTH, 4, 128]), ("b_ln_g", [DEPTH, 512]),
                    ("b_ln_b", [DEPTH, 512]), ("c_conv_w", [DEPTH, 3, 1024]), ("c_conv_b", [DEPTH, 1024]),
                    ("c_gate_b", [DEPTH, 2, 2, 4]), ("c_ln_g", [DEPTH, 512]), ("d_aup", [DEPTH, 2, 16, 256]),
                    ("d_ab", [DEPTH, 2, 256]), ("d_ln_g", [DEPTH, 512]), ("br_w", [DEPTH, 4, 512, D]),
                    ("out_w", [DEPTH, D, D])):
        setattr(G, nm, inp(nm, shp))
    G.debug = debug
    G.branches = branches
    G.out = None if debug == "brtest" else nc.dram_tensor("out", [2048, D], F32, kind="ExternalOutput").ap()
    G.hT_d = nc.dram_tensor("hT_d", [D, T], F32, kind="Internal").ap()
    G.h1T_d = nc.dram_tensor("h1T_d", [DFF, T], BF16, kind="Internal").ap()
    G.pT_d = nc.dram_tensor("pT_d", [D_IN, T], F32, kind="ExternalOutput" if debug == "inproj" else ("ExternalInput" if debug == "brtest" else "Internal")).ap()
    G.hsT_d = nc.dram_tensor("hsT_d", [4, 512, T], BF16, kind="ExternalOutput" if debug in ("branches", "brtest") else "Internal").ap()
    if debug == "brtest":
        G.dbg_a = nc.dram_tensor("dbg_a", [8, 128, T], F32, kind="ExternalOutput").ap()
    if debug:
        G.dbg_h = nc.dram_tensor("dbg_h", [D, T], F32, kind="ExternalOutput").ap()
    with contextlib.ExitStack() as st:
        sbt = lambda name, shape, dt=F32: st.enter_context(nc.sbuf_tensor(name, list(shape), dt))
        G.ident = sbt("ident", [128, 128])
        G.ones = sbt("ones", [128, 128])
        G.epsc = sbt("epsc", [128, 1])
        G.modT = sbt("modT", [128, DEPTH, 144, 2])
        G.G1 = sbt("G1", [128, DEPTH, 3, NCH, 2])
        G.gate = sbt("gate", [128, DEPTH, 3, NCH, 2])
        G.fgT = sbt("fgT", [128, NCH])
        G.eps5 = sbt("eps5", [128, 1])
        G.onec = sbt("onec", [128, 1])
        G.tri = sbt("tri", [128, 4, 128])
        G.onesb = sbt("onesb", [128, 128], BF16)
        G.bones = sbt("bones", [128, 128])
        G.pmask = sbt("pmask", [128, 6])
        G.epsa = sbt("epsa", [128, 1])
        G.identb = sbt("identb", [128, 128], BF16)
        P = Phase(nc)
        P.dma(G.ident[:], G.identd[:, :], [], ["ident"])
        P.dma(G.tri[:], G.trid[:, :, :], [], ["tri"])
        P.dma(G.bones[:], G.bonesd[:, :], [], ["bones"])
        P.dma(G.pmask[:], G.pmaskd[:, :], [], ["pmask"])
        P.memset(G.epsa[:], 64e-5, [], ["epsa"])
        P.memset(G.ones[:], 1.0, [], ["ones"])
        P.memset(G.onesb[:], 1.0, [], ["onesb"])
        P.memset(G.epsc[:], 1e-6, [], ["epsc"])
        P.memset(G.eps5[:], 1e-5, [], ["eps5"])
        P.memset(G.onec[:], 1.0, [], ["onec"])
        P.cp(G.identb[:], G.ident[:], ["ident"], ["identb"])
        P.end()
        if debug == "ffntest":
            sub_ffn(G, 0, 0, 0)
            return nc
        if debug == "brtest":
            for br in branches:
                {"a": phase_rwkv, "b": phase_gmlp, "c": phase_mlstm, "d": phase_gla}[br](G, 0)
            return nc
        phase_init(G)
        for l in range(depth):
            if debug == "init":
                break
            if debug == "norm":
                Phase.count += 1
                with G.nc.sbuf_tensor(f"hnT_{Phase.count}", [128, NCH, T], BF16) as hn_t:
                    phase_norm(G, Tile(hn_t, "hnT"), l, 0)
                break
            if debug == "up":
                Phase.count += 1
                with G.nc.sbuf_tensor(f"hnT_{Phase.count}", [128, NCH, T], BF16) as hn_t:
                    phase_norm(G, Tile(hn_t, "hnT"), l, 0)
                    phase_ffn_up(G, Tile(hn_t, "hnT"), l, 0, 0)
                break
            sub_ffn(G, l, 0, 0)
            if debug == "ffn1":
                break
            sub_mixer(G, l)
            if debug in ("inproj", "branches", "mix"):
                break
            sub_ffn(G, l, 2, 1)
            if debug == "ffn2":
                break
        if debug:
            P = Phase(nc)
            t = P.rot(2, [128, NCH, 512])
            for (t0, n) in TG:
                b = t.next()
                P.dma(b[:, :, 0:n], fm(G.hT_d)[:, :, t0:t0 + n], [], [b.k])
                P.dma(fm(G.dbg_h)[:, :, t0:t0 + n], b[:, :, 0:n], [b.k], [], q="pool")
            P.end()
        if debug not in ("init", "norm", "up", "inproj", "branches"):
            phase_final(G)
    return nc


def make_in_maps(inputs):
    f = lambda a: np.ascontiguousarray(a, dtype=np.float32)
    ident = np.eye(128, dtype=np.float32)
    ii = np.arange(128)
    trid = np.stack([(ii[:, None] <= ii[None, :]), (ii[:, None] >= ii[None, :]),
                     (ii[:, None] < ii[None, :]), (ii[:, None] > ii[None, :])], axis=1).astype(np.float32)
    bonesd = ((ii[:, None] // 64) == (ii[None, :] // 64)).astype(np.float32)
    pmaskd = np.stack([(ii % 4 == 0), (ii % 4 == 1), (ii % 4 == 2), (ii % 4 == 3), (ii % 2 == 0), (ii % 2 == 1)],
                      axis=1).astype(np.float32)
    shared = {
        "c_ctx": f(inputs["c_ctx"]).reshape(1, D),
        "ada_w": f(inputs["ada_w"]), "ada_b": f(inputs["ada_b"]), "norm_g": f(inputs["norm_g"]),
        "ffn_w1": f(inputs["ffn_w1"]), "ffn_w3": f(inputs["ffn_w3"]), "ffn_w2": f(inputs["ffn_w2"]),
        "final_g": f(inputs["final_g"]).reshape(1, D), "identd": ident, "trid": trid, "bonesd": bonesd, "pmaskd": pmaskd,
    }
    for nm in ("in_w", "in_b", "a_mu", "a_w0", "a_wup", "a_a0", "a_aup", "a_gup", "a_kk", "a_ka", "a_ln_g", "a_ln_b",
               "b_ws", "b_bs", "b_ln_g", "b_ln_b", "c_conv_w", "c_conv_b", "c_gate_b", "c_ln_g", "d_aup", "d_ab",
               "d_ln_g", "br_w", "out_w"):
        shared[nm] = f(inputs[nm])
    shared["a_rk"] = f(inputs["a_rk"]).reshape(DEPTH, 512)
    maps = []
    for core in range(8):
        b = core % 4
        m = dict(shared)
        m["x_b"] = f(inputs["x"][b])
        m["ctx_b"] = f(inputs["ctx"][b])
        m["c_b"] = f(inputs["c"][b]).reshape(1, D)
        maps.append(m)
    return maps


def kernel(**inputs):
    nc = build()
    maps = make_in_maps(inputs)[:4]
    res = run_bass_kernel_spmd(nc, maps, core_ids=list(range(4)))
    return np.stack([np.asarray(res.results[b]["out"], dtype=np.float32) for b in range(4)], axis=0)
```

```python
import contextlib
import numpy as np
import concourse.bass as bass
import concourse.mybir as mybir
from concourse.bass_utils import run_bass_kernel_spmd

F32 = mybir.dt.float32
BF16 = mybir.dt.bfloat16
AF = mybir.ActivationFunctionType
ALU = mybir.AluOpType
AX = mybir.AxisListType

ENGS = ("pe", "dve", "act", "pool", "sp")
NDMASEM = 24

T = 2304
LC = 256
D = 2048
NCH = 16
DFF = 5632
NFC = 44
DEPTH = 4
A_COLS, B_COLS, C_COLS, D_COLS = 1920, 1024, 2064, 1568
OFF_A, OFF_B, OFF_C, OFF_D, OFF_G = 0, 1920, 2944, 5008, 6576
D_IN = 14768
TG = [(0, 256), (256, 512), (768, 512), (1280, 512), (1792, 512)]
NT = 18


class Sched:
    uid = 0

    def __init__(self, nc):
        self.nc = nc
        self.ops = {e: [] for e in ENGS}
        self.cnt = {e: 0 for e in ENGS}
        self.dcnt = {e: 0 for e in ENGS}
        self.last_w = {}
        self.readers = {}
        self.seen = {e: {} for e in ENGS}

    def _deps(self, reads, writes):
        deps = set()
        for k in reads:
            if k in self.last_w:
                deps.add(self.last_w[k])
        for k in writes:
            if k in self.last_w:
                deps.add(self.last_w[k])
            for r in self.readers.get(k, ()):
                deps.add(r)
        return deps

    def _update(self, tok, reads, writes):
        for k in writes:
            self.last_w[k] = tok
            self.readers[k] = []
        for k in reads:
            self.readers.setdefault(k, []).append(tok)

    def _waits(self, eng, deps):
        waits = []
        seen = self.seen[eng]
        best = {}
        for d in deps:
            if d[0] == "c":
                _, e, idx = d
                if e == eng and e == "pe":
                    continue
                key = ("c", e)
                if best.get(key, 0) < idx:
                    best[key] = idx
            else:
                _, q, i = d
                slot = i % NDMASEM
                val = 16 * (i // NDMASEM + 1)
                key = ("d", q, slot)
                if best.get(key, 0) < val:
                    best[key] = val
        for key, val in best.items():
            if seen.get(key, 0) >= val:
                continue
            seen[key] = val
            waits.append((key, val))
        return waits

    def op(self, eng, fn, reads=(), writes=()):
        deps = self._deps(reads, writes)
        waits = self._waits(eng, deps)
        self.cnt[eng] += 1
        tok = ("c", eng, self.cnt[eng])
        self.ops[eng].append((waits, fn, "c", None))
        self._update(tok, reads, writes)
        return tok

    def dma(self, q, fn, reads=(), writes=()):
        deps = self._deps(reads, writes)
        i = self.dcnt[q]
        if i >= NDMASEM:
            deps.add(("d", q, i - NDMASEM))
        waits = self._waits(q, deps)
        self.dcnt[q] += 1
        tok = ("d", q, i)
        self.ops[q].append((waits, fn, "d", i % NDMASEM))
        self._update(tok, reads, writes)
        return tok

    def emit(self):
        nc = self.nc
        Sched.uid += 1
        u = Sched.uid
        csem = {e: nc.alloc_semaphore(f"c{u}_{e}") for e in ENGS}
        dsem = {}
        for q in ENGS:
            for s in range(min(NDMASEM, self.dcnt[q])):
                dsem[(q, s)] = nc.alloc_semaphore(f"d{u}_{q}_{s}")
        with nc.Block() as block:

            def run(engname, eng):
                for waits, fn, kind, slot in self.ops[engname]:
                    for key, val in waits:
                        if key[0] == "c":
                            eng.wait_ge(csem[key[1]], val)
                        else:
                            eng.wait_ge(dsem[(key[1], key[2])], val)
                    ins = fn(eng)
                    if kind == "c":
                        ins.then_inc(csem[engname], 1)
                    else:
                        ins.then_inc(dsem[(engname, slot)], 16)
                n = self.dcnt[engname]
                for s in range(min(NDMASEM, n)):
                    c = (n - 1 - s) // NDMASEM + 1
                    eng.wait_ge(dsem[(engname, s)], 16 * c)

            if self.ops["sp"]:
                @block.sync
                def _(e):
                    run("sp", e)
            if self.ops["pool"]:
                @block.gpsimd
                def _(e):
                    run("pool", e)
            if self.ops["act"]:
                @block.scalar
                def _(e):
                    run("act", e)
            if self.ops["dve"]:
                @block.vector
                def _(e):
                    run("dve", e)
            if self.ops["pe"]:
                @block.tensor
                def _(e):
                    run("pe", e)
        nc.all_engine_barrier()
        nc.clear_and_free_semaphores(list(csem.values()) + list(dsem.values()))
        nc.all_engine_barrier()


class Tile:
    def __init__(self, t, key):
        self.t = t
        self.k = key

    def __getitem__(self, idx):
        return self.t[idx]


class Rot:
    def __init__(self, tiles):
        self.tiles = tiles
        self.i = 0

    def next(self):
        t = self.tiles[self.i % len(self.tiles)]
        self.i += 1
        return t


class Phase:
    count = 0

    def __init__(self, nc):
        Phase.count += 1
        self.nc = nc
        self.id = Phase.count
        self.n = 0
        self.st = contextlib.ExitStack()
        self.S = Sched(nc)

    def sb(self, shape, dt=F32):
        self.n += 1
        name = f"p{self.id}_{self.n}"
        return Tile(self.st.enter_context(self.nc.sbuf_tensor(name, list(shape), dt)), name)

    def ps(self, shape, dt=F32):
        self.n += 1
        name = f"q{self.id}_{self.n}"
        return Tile(self.st.enter_context(self.nc.psum_tensor(name, list(shape), dt)), name)

    def psq(self, n, width=128, dt=F32):
        per = (512 if dt == F32 else 1024) // width
        tiles = []
        bank = None
        for i in range(n):
            if i % per == 0:
                bank = self.ps([128, 512 if dt == F32 else 1024], dt)
            j = i % per
            tiles.append(Tile(bank.t[:, j * width:(j + 1) * width], bank.k))
        return Rot(tiles)

    def rot(self, n, shape, dt=F32, psum=False):
        return Rot([(self.ps if psum else self.sb)(shape, dt) for _ in range(n)])

    def end(self):
        self.S.emit()
        self.st.close()
        self.nc.all_engine_barrier()

    def mm(self, out, lhsT, rhs, start, stop, r, w):
        self.S.op("pe", lambda e: e.matmul(out, lhsT=lhsT, rhs=rhs, start=start, stop=stop), r, w)

    def tr(self, out, in_, ident, r, w):
        self.S.op("pe", lambda e: e.transpose(out, in_, ident), r, w)

    def act(self, out, in_, func, r, w, bias=None, scale=None, accum=None):
        kw = {}
        if bias is not None:
            kw["bias"] = bias
        if scale is not None:
            kw["scale"] = scale
        if accum is not None:
            kw["accum_out"] = accum
        self.S.op("act", lambda e: e.activation(out=out, in_=in_, func=func, **kw), r, w)

    def tt(self, out, a, b, op, r, w, eng="dve"):
        self.S.op(eng, lambda e: e.tensor_tensor(out=out, in0=a, in1=b, op=op), r, w)

    def ts(self, out, a, s1, s2, op0, op1, r, w, eng="dve"):
        if s2 is None:
            self.S.op(eng, lambda e: e.tensor_scalar(out=out, in0=a, scalar1=s1, scalar2=None, op0=op0), r, w)
        else:
            self.S.op(eng, lambda e: e.tensor_scalar(out=out, in0=a, scalar1=s1, scalar2=s2, op0=op0, op1=op1), r, w)

    def stt(self, out, a, s, b, op0, op1, r, w, eng="dve"):
        self.S.op(eng, lambda e: e.scalar_tensor_tensor(out=out, in0=a, scalar=s, in1=b, op0=op0, op1=op1), r, w)

    def cp(self, out, in_, r, w, eng="dve"):
        self.S.op(eng, lambda e: e.tensor_copy(out=out, in_=in_), r, w)

    def recip(self, out, in_, r, w):
        self.S.op("dve", lambda e: e.reciprocal(out=out, in_=in_), r, w)

    def memset(self, out, val, r, w, eng="dve"):
        self.S.op(eng, lambda e: e.memset(out, val), r, w)

    def dma(self, out, in_, r, w, q="sp"):
        self.S.dma(q, lambda e: e.dma_start(out=out, in_=in_), r, w)


class Ctx:
    pass


def fm(ap2d):
    return ap2d.rearrange("(c p) t -> p c t", p=128)


def phase_init(G):
    nc = G.nc
    P = Phase(nc)
    ident = G.ident
    xin = P.rot(2, [128, D])
    stg = P.rot(2, [128, NCH, 128])
    pst = P.rot(2, [128, 512], psum=True)
    for tt in range(NT):
        xi = xin.next()
        src = G.ctx_b[tt * 128:(tt + 1) * 128, :] if tt < 2 else G.x_b[(tt - 2) * 128:(tt - 1) * 128, :]
        P.dma(xi[:], src, [], [xi.k])
        so = stg.next()
        for c4 in range(4):
            pt = pst.next()
            for j in range(4):
                c = c4 * 4 + j
                P.tr(pt[:, j * 128:(j + 1) * 128], xi[:, c * 128:(c + 1) * 128], ident[:], [xi.k, "ident"], [pt.k])
            P.cp(so[:, c4 * 4:(c4 + 1) * 4, :], pt[:].rearrange("p (j t) -> p j t", j=4), [pt.k], [so.k],
                 eng="dve" if c4 % 2 == 0 else "act_copy")
        P.dma(fm(G.hT_d)[:, :, tt * 128:(tt + 1) * 128], so[:], [so.k], ["hT_d"], q="pool")
    craw = P.sb([32, 128])
    P.dma(craw[0:16, :], G.c_b.rearrange("o (c p) -> (o c) p", p=128), [], [craw.k])
    P.dma(craw[16:32, :], G.c_ctx.rearrange("o (c p) -> (o c) p", p=128), [], [craw.k])
    pc = P.ps([128, 32])
    P.tr(pc[:], craw[:], ident[0:32, 0:32], [craw.k, "ident"], [pc.k])
    condT = P.sb([128, NCH, 2])
    for r in range(2):
        P.act(condT[:, :, r], pc[:, r * 16:(r + 1) * 16], AF.Silu, [pc.k], [condT.k])
    wst = P.rot(3, [128, D])
    pm = P.rot(2, [128, NCH, 2], psum=True)
    maccs = P.rot(2, [128, NCH, 2])
    braw = P.rot(2, [72, 128])
    pb = P.rot(1, [128, 72], psum=True)
    badT = P.sb([128, DEPTH, 144])
    for l in range(DEPTH):
        for hlf in range(2):
            br = braw.next()
            P.dma(br[:], G.ada_b[l:l + 1, hlf * 9216:(hlf + 1) * 9216].rearrange("o (j p) -> (o j) p", p=128), [], [br.k])
            pbt = pb.next()
            P.tr(pbt[:], br[:], ident[0:72, 0:72], [br.k, "ident"], [pbt.k])
            P.cp(badT[:, l, hlf * 72:(hlf + 1) * 72], pbt[:], [pbt.k], [badT.k])
        for n in range(9):
            macc = maccs.next()
            for c in range(NCH):
                w = wst.next()
                P.dma(w[:], G.ada_w[l, c * 128:(c + 1) * 128, n * D:(n + 1) * D], [], [w.k])
                pmt = pm.next()
                for cc in range(NCH):
                    P.mm(pmt[:, cc, :], w[:, cc * 128:(cc + 1) * 128], condT[:, c, :], True, True,
                         [w.k, condT.k], [pmt.k])
                if c == 0:
                    P.cp(macc[:], pmt[:], [pmt.k], [macc.k])
                else:
                    P.tt(macc[:], macc[:], pmt[:], ALU.add, [pmt.k, macc.k], [macc.k])
            for r in range(2):
                P.tt(G.modT[:, l, n * 16:(n + 1) * 16, r], macc[:, :, r], badT[:, l, n * 16:(n + 1) * 16], ALU.add,
                     [macc.k, badT.k], ["modT"])
    png = P.ps([128, DEPTH * 3 * NCH])
    ngv = G.norm_g.rearrange("l s (c p) -> (l s c) p", p=128)
    for half in range(2):
        ngraw = P.sb([96, 128])
        P.dma(ngraw[:], ngv[half * 96:(half + 1) * 96, :], [], [ngraw.k])
        P.tr(png[:, half * 96:(half + 1) * 96], ngraw[:], ident[0:96, 0:96], [ngraw.k, "ident"], [png.k])
    for l in range(DEPTH):
        for s in range(3):
            for r in range(2):
                P.stt(G.G1[:, l, s, :, r], G.modT[:, l, (3 * s + 1) * 16:(3 * s + 2) * 16, r], 1.0,
                      png[:, (l * 3 + s) * 16:(l * 3 + s + 1) * 16], ALU.add, ALU.mult, ["modT", png.k], ["G1"])
                P.ts(G.gate[:, l, s, :, r], G.modT[:, l, (3 * s + 2) * 16:(3 * s + 3) * 16, r],
                     0.5 if s != 1 else 1.0, None, ALU.mult, None, ["modT"], ["gate"])
    fgraw = P.sb([16, 128])
    P.dma(fgraw[:], G.final_g.rearrange("o (c p) -> (o c) p", p=128), [], [fgraw.k])
    P.tr(pc[:, 0:16], fgraw[:], ident[0:16, 0:16], [fgraw.k, "ident"], [pc.k])
    P.cp(G.fgT[:], pc[:, 0:16], [pc.k], ["fgT"])
    P.end()


def norm_to_hnT(P, G, hnT, g1_of_chunk, shift_of_chunk):
    hin = P.rot(2, [128, NCH, 512])
    sq = P.rot(2, [128, 512])
    pss = P.rot(2, [128, 512], psum=True)
    rstd = P.rot(2, [128, 512])
    tmp = P.rot(3, [128, 512])
    for (t0, n) in TG:
        row = 1 if t0 < LC else 0
        hi = hin.next()
        P.dma(hi[:, :, 0:n], fm(G.hT_d)[:, :, t0:t0 + n], ["hT_d"], [hi.k])
        ps = pss.next()
        for c in range(NCH):
            s = sq.next()
            P.act(s[:, 0:n], hi[:, c, 0:n], AF.Square, [hi.k], [s.k])
            P.mm(ps[:, 0:n], G.ones[:], s[:, 0:n], c == 0, c == NCH - 1, [s.k, "ones"], [ps.k])
        rs = rstd.next()
        P.act(rs[:, 0:n], ps[:, 0:n], AF.Sqrt, [ps.k], [rs.k], bias=G.epsc[:, 0:1], scale=1.0 / D)
        P.recip(rs[:, 0:n], rs[:, 0:n], [rs.k], [rs.k])
        for c in range(NCH):
            tm = tmp.next()
            P.stt(tm[:, 0:n], hi[:, c, 0:n], g1_of_chunk(c, row), rs[:, 0:n], ALU.mult, ALU.mult,
                  [hi.k, rs.k, "G1"], [tm.k])
            P.act(hnT[:, c, t0:t0 + n], tm[:, 0:n], AF.Identity, [tm.k, "modT"], [hnT.k + f"_{t0}"],
                  bias=shift_of_chunk(c, row))


def phase_norm(G, hnT, l, s):
    P = Phase(G.nc)
    norm_to_hnT(P, G, hnT,
                lambda c, row: G.G1[:, l, s, c, row:row + 1],
                lambda c, row: G.modT[:, l, 3 * s * 16 + c, row:row + 1])
    P.end()


def phase_ffn_up(G, hnT, l, s, which):
    nc = G.nc
    P = Phase(nc)
    FB = 256
    wstg = P.rot(4, [128, NCH, FB])
    wbf = P.rot(4, [128, NCH, FB], BF16)
    ps1 = P.rot(2, [128, 512], psum=True)
    ps3 = P.rot(2, [128, 512], psum=True)
    sil = P.rot(2, [128, 512])
    h1 = P.rot(3, [128, 512], BF16)
    hkeys = [hnT.k + f"_{t0}" for (t0, n) in TG]
    def load_block(fb):
        wb = []
        for wsrc in (G.ffn_w1, G.ffn_w3):
            st_ = wstg.next()
            P.dma(st_[:], fm(wsrc[l, which])[:, :, fb * FB:(fb + 1) * FB], [], [st_.k])
            b = wbf.next()
            P.cp(b[:], st_[:], [st_.k], [b.k], eng="pool")
            wb.append(b)
        return wb

    nxt = load_block(0)
    for fb in range(DFF // FB):
        wb = nxt
        if fb + 1 < DFF // FB:
            nxt = load_block(fb + 1)
        for gi, (t0, n) in enumerate(TG):
            for fc in range(FB // 128):
                p1 = ps1.next()
                p3 = ps3.next()
                for c in range(NCH):
                    P.mm(p1[:, 0:n], wb[0][:, c, fc * 128:(fc + 1) * 128], hnT[:, c, t0:t0 + n], c == 0, c == NCH - 1,
                         [wb[0].k, hkeys[gi]], [p1.k])
                for c in range(NCH):
                    P.mm(p3[:, 0:n], wb[1][:, c, fc * 128:(fc + 1) * 128], hnT[:, c, t0:t0 + n], c == 0, c == NCH - 1,
                         [wb[1].k, hkeys[gi]], [p3.k])
                sl = sil.next()
                P.act(sl[:, 0:n], p1[:, 0:n], AF.Silu, [p1.k], [sl.k])
                ho = h1.next()
                P.tt(ho[:, 0:n], p3[:, 0:n], sl[:, 0:n], ALU.mult, [p3.k, sl.k], [ho.k])
                f0 = fb * FB + fc * 128
                P.dma(G.h1T_d[f0:f0 + 128, t0:t0 + n], ho[:, 0:n], [ho.k], [], q="pool")
    P.end()


def phase_ffn_down(G, l, s, which):
    nc = G.nc
    P = Phase(nc)
    wstg = P.rot(3, [128, 4, 512])
    w2q = P.sb([128, NFC, 512], BF16)
    h1t = P.rot(2, [128, NFC, 512], BF16)
    hold = P.rot(2, [128, 4, 512])
    hnew = P.rot(2, [128, 4, 512])
    pso = P.rot(4, [128, 512], psum=True)
    for dq in range(4):
        for f4 in range(NFC // 4):
            st_ = wstg.next()
            P.dma(st_[:], fm(G.ffn_w2[l, which])[:, f4 * 4:(f4 + 1) * 4, dq * 512:(dq + 1) * 512], [], [st_.k])
            P.cp(w2q[:, f4 * 4:(f4 + 1) * 4, :], st_[:], [st_.k], [w2q.k + f"_{f4}"], eng="pool")
        wkeys = [w2q.k + f"_{f4}" for f4 in range(NFC // 4)]
        for (t0, n) in TG:
            row = 1 if t0 < LC else 0
            ht = h1t.next()
            for q4 in range(4):
                P.dma(ht[:, q4 * 11:(q4 + 1) * 11, 0:n], fm(G.h1T_d)[:, q4 * 11:(q4 + 1) * 11, t0:t0 + n], [], [ht.k])
            ho = hold.next()
            P.dma(ho[:, :, 0:n], fm(G.hT_d)[:, dq * 4:(dq + 1) * 4, t0:t0 + n], ["hT_d"], [ho.k])
            hn = hnew.next()
            for dc in range(4):
                ps = pso.next()
                for fc in range(NFC):
                    P.mm(ps[:, 0:n], w2q[:, fc, dc * 128:(dc + 1) * 128], ht[:, fc, 0:n], fc == 0, fc == NFC - 1,
                         [wkeys[fc // 4], ht.k], [ps.k])
                P.stt(hn[:, dc, 0:n], ps[:, 0:n], G.gate[:, l, s, dq * 4 + dc, row:row + 1], ho[:, dc, 0:n],
                      ALU.mult, ALU.add, [ps.k, ho.k, "gate"], [hn.k])
            P.dma(fm(G.hT_d)[:, dq * 4:(dq + 1) * 4, t0:t0 + n], hn[:, :, 0:n], [hn.k], ["hT_d"], q="pool")
    P.end()


def in_segments():
    segs = []
    c = 0
    while c < OFF_G:
        n = min(128, OFF_G - c)
        segs.append((c, n, False))
        c += n
    while c < D_IN:
        segs.append((c, 128, True))
        c += 128
    return segs


def phase_inproj(G, hnT, l):
    P = Phase(G.nc)
    segs = in_segments()
    inbT = P.sb([128, len(segs)])
    pbt = P.ps([128, 64])
    braw = P.rot(2, [64, 128])
    r1 = braw.next()
    P.dma(r1[0:51, :], G.in_b[l:l + 1, 0:6528].rearrange("o (j p) -> (o j) p", p=128), [], [r1.k])
    P.tr(pbt[:, 0:51], r1[0:51, :], G.ident[0:51, 0:51], [r1.k, "ident"], [pbt.k])
    P.cp(inbT[:, 0:51], pbt[:, 0:51], [pbt.k], [inbT.k])
    P.dma(inbT[0:48, 51:52], G.in_b[l:l + 1, 6528:6576].rearrange("o p -> p o"), [], [inbT.k])
    r2 = braw.next()
    P.dma(r2[:], G.in_b[l:l + 1, OFF_G:D_IN].rearrange("o (j p) -> (o j) p", p=128), [], [r2.k])
    P.tr(pbt[:, 0:64], r2[:], G.ident[0:64, 0:64], [r2.k, "ident"], [pbt.k])
    P.cp(inbT[:, 52:116], pbt[:, 0:64], [pbt.k], [inbT.k])
    wstg = P.rot(3, [128, NCH, 256])
    wbf = P.rot(3, [128, NCH, 256], BF16)
    pso = P.rot(4, [128, 512], psum=True)
    ot = P.rot(4, [128, 512])
    hkeys = ["hnT" + f"_{t0}" for (t0, n) in TG]
    blocks = []
    si = 0
    while si < len(segs):
        blk = [si]
        if si + 1 < len(segs) and segs[si][1] == 128:
            blk.append(si + 1)
        blocks.append(blk)
        si += len(blk)

    def load_block(blk):
        c0 = segs[blk[0]][0]
        w = sum(segs[j][1] for j in blk)
        st_ = wstg.next()
        P.dma(st_[:, :, 0:w], fm(G.in_w[l])[:, :, c0:c0 + w], [], [st_.k])
        b = wbf.next()
        P.cp(b[:, :, 0:w], st_[:, :, 0:w], [st_.k], [b.k], eng="pool")
        return b

    nxt = load_block(blocks[0])
    for bi, blk in enumerate(blocks):
        b = nxt
        c0 = segs[blk[0]][0]
        if bi + 1 < len(blocks):
            nxt = load_block(blocks[bi + 1])
        for gi, (t0, n) in enumerate(TG):
            for j in blk:
                col, ns, sig = segs[j]
                off = col - c0
                ps = pso.next()
                for c in range(NCH):
                    P.mm(ps[0:ns, 0:n], b[:, c, off:off + ns], hnT[:, c, t0:t0 + n], c == 0, c == NCH - 1,
                         [b.k, hkeys[gi]], [ps.k])
                o = ot.next()
                if sig:
                    P.act(o[0:ns, 0:n], ps[0:ns, 0:n], AF.Sigmoid, [ps.k, inbT.k], [o.k], bias=inbT[0:ns, j:j + 1])
                else:
                    P.ts(o[0:ns, 0:n], ps[0:ns, 0:n], inbT[0:ns, j:j + 1], None, ALU.add, None, [ps.k, inbT.k], [o.k])
                P.dma(G.pT_d[col:col + ns, t0:t0 + n], o[0:ns, 0:n], [o.k], [], q="pool")
    P.end()


def phase_gmlp(G, l):
    P = Phase(G.nc)
    ident = G.ident
    wsT = P.sb([128, 4, 128])
    pw = P.rot(2, [128, 128], psum=True)
    wraw = P.rot(2, [128, 128])
    bsbc = P.sb([128, 4, 128])
    for g in range(4):
        wr = wraw.next()
        P.dma(wr[:], G.b_ws[l, g], [], [wr.k])
        p_ = pw.next()
        P.tr(p_[:], wr[:], ident[:], [wr.k, "ident"], [p_.k])
        P.cp(wsT[:, g, :], p_[:], [p_.k], [wsT.k])
        brow = P.sb([1, 128])
        P.dma(brow[:], G.b_bs[l, g:g + 1, :], [], [brow.k])
        p2 = pw.next()
        P.mm(p2[:], G.ones[0:1, :], brow[:], True, True, [brow.k, "ones"], [p2.k])
        P.cp(bsbc[:, g, :], p2[:], [p2.k], [bsbc.k])
    lng = P.sb([128, 8])
    lraw = P.sb([8, 128])
    P.dma(lraw[0:4, :], G.b_ln_g[l:l + 1, :].rearrange("o (j p) -> (o j) p", p=128), [], [lraw.k])
    P.dma(lraw[4:8, :], G.b_ln_b[l:l + 1, :].rearrange("o (j p) -> (o j) p", p=128), [], [lraw.k])
    p_ = pw.next()
    P.tr(p_[:, 0:8], lraw[:], ident[0:8, 0:8], [lraw.k, "ident"], [p_.k])
    P.cp(lng[:], p_[:, 0:8], [p_.k], [lng.k])
    zin = P.rot(2, [128, 8, 512])
    zz = P.rot(2, [128, 8, 512])
    t1 = P.rot(2, [128, 512])
    t2 = P.rot(2, [128, 512])
    ps1 = P.rot(1, [128, 512], psum=True)
    ps2 = P.rot(1, [128, 512], psum=True)
    mean = P.rot(2, [128, 512])
    rstd = P.rot(2, [128, 512])
    vn = P.rot(2, [128, 4, 512])
    vtok = P.rot(3, [128, 128])
    pt = P.rot(2, [128, 128], psum=True)
    pm = P.rot(2, [128, 128], psum=True)
    sres = P.rot(2, [128, 128])
    hb = P.rot(2, [128, 4, 512], BF16)
    for (t0, n) in TG:
        zi = zin.next()
        P.dma(zi[:, :, 0:n], fm(G.pT_d[OFF_B:OFF_B + 1024, :])[:, :, t0:t0 + n], [], [zi.k])
        z = zz.next()
        for c in range(8):
            a = t1.next()
            P.act(a[:, 0:n], zi[:, c, 0:n], AF.Square, [zi.k], [a.k])
            P.ts(a[:, 0:n], a[:, 0:n], 0.044715, 1.0, ALU.mult, ALU.add, [a.k], [a.k])
            P.tt(a[:, 0:n], a[:, 0:n], zi[:, c, 0:n], ALU.mult, [a.k, zi.k], [a.k])
            P.act(a[:, 0:n], a[:, 0:n], AF.Sigmoid, [a.k], [a.k], scale=1.5957691216057308)
            P.tt(z[:, c, 0:n], a[:, 0:n], zi[:, c, 0:n], ALU.mult, [a.k, zi.k], [z.k], eng="pool")
        p1 = ps1.next()
        p2 = ps2.next()
        for c in range(4):
            P.mm(p1[:, 0:n], G.ones[:], z[:, 4 + c, 0:n], c == 0, c == 3, [z.k, "ones"], [p1.k])
        for c in range(4):
            b = t2.next()
            P.act(b[:, 0:n], z[:, 4 + c, 0:n], AF.Square, [z.k], [b.k])
            P.mm(p2[:, 0:n], G.ones[:], b[:, 0:n], c == 0, c == 3, [b.k, "ones"], [p2.k])
        mu = mean.next()
        P.ts(mu[:, 0:n], p1[:, 0:n], 1.0 / 512, None, ALU.mult, None, [p1.k], [mu.k])
        rs = rstd.next()
        P.tt(rs[:, 0:n], mu[:, 0:n], mu[:, 0:n], ALU.mult, [mu.k], [rs.k])
        P.stt(rs[:, 0:n], p2[:, 0:n], 1.0 / 512, rs[:, 0:n], ALU.mult, ALU.subtract, [p2.k, rs.k], [rs.k])
        P.act(rs[:, 0:n], rs[:, 0:n], AF.Sqrt, [rs.k], [rs.k], bias=G.eps5[:, 0:1])
        P.recip(rs[:, 0:n], rs[:, 0:n], [rs.k], [rs.k])
        v = vn.next()
        for c in range(4):
            P.tt(v[:, c, 0:n], z[:, 4 + c, 0:n], mu[:, 0:n], ALU.subtract, [z.k, mu.k], [v.k])
            P.tt(v[:, c, 0:n], v[:, c, 0:n], rs[:, 0:n], ALU.mult, [v.k, rs.k], [v.k])
            P.ts(v[:, c, 0:n], v[:, c, 0:n], lng[:, c:c + 1], lng[:, 4 + c:5 + c], ALU.mult, ALU.add, [v.k, lng.k], [v.k])
        h = hb.next()
        for j in range(n // 128):
            for g in range(4):
                p_ = pt.next()
                P.tr(p_[:], v[:, g, j * 128:(j + 1) * 128], ident[:], [v.k, "ident"], [p_.k])
                vt = vtok.next()
                P.cp(vt[:], p_[:], [p_.k], [vt.k], eng="act_copy")
                pq = pm.next()
                P.mm(pq[:], vt[:], wsT[:, g, :], True, True, [vt.k, wsT.k], [pq.k])
                sr = sres.next()
                P.tt(sr[:], pq[:], bsbc[:, g, :], ALU.add, [pq.k, bsbc.k], [sr.k])
                P.tt(h[:, g, j * 128:(j + 1) * 128], sr[:], z[:, g, j * 128:(j + 1) * 128], ALU.mult, [sr.k, z.k], [h.k], eng="pool")
        P.dma(fm(G.hsT_d[1])[:, :, t0:t0 + n], h[:, :, 0:n], [h.k], [], q="pool")
    P.end()


import os
STOP = int(os.environ.get("KSTOP", "0"))
NIT = int(os.environ.get("KNIT", "7"))
ORDER = {0: list(range(NT)), 1: [1, 0] + list(range(NT - 1, 1, -1))}


def load_cols(P, dst, src_row_ap, nrows, pw, ident):
    raw = P.sb([nrows, 128])
    P.dma(raw[:], src_row_ap.rearrange("o (j p) -> (o j) p", p=128), [], [raw.k])
    p_ = pw.next()
    P.tr(p_[:, 0:nrows], raw[:], ident[0:nrows, 0:nrows], [raw.k, "ident"], [p_.k])
    P.cp(dst, p_[:, 0:nrows], [p_.k], ["cols"])


def head_norm_store(P, G, hsum, n_heads_chunks, eps_col, gcols, bcols, gate_rows, gate_func, dst, sub_mean, pw):
    sq = P.rot(2, [128, 512])
    ps1 = P.rot(1, [128, 512], psum=True)
    ps2 = P.rot(1, [128, 512], psum=True)
    mu = P.rot(2, [128, 512])
    rs = P.rot(2, [128, 512])
    xc = P.rot(2, [128, 512])
    gin = P.rot(2, [128, 512])
    ob = P.rot(2, [128, 512], BF16)
    for (t0, n) in TG:
        for h in range(4):
            x = hsum[:, h, t0:t0 + n]
            xk = hsum.k
            c = xc.next()
            r = rs.next()
            if sub_mean:
                p1 = ps1.next()
                P.mm(p1[:, 0:n], G.ones[:], x, True, True, [xk, "ones"], [p1.k])
                m = mu.next()
                P.ts(m[:, 0:n], p1[:, 0:n], 1.0 / 128, None, ALU.mult, None, [p1.k], [m.k])
                P.tt(c[:, 0:n], x, m[:, 0:n], ALU.subtract, [xk, m.k], [c.k])
            else:
                P.cp(c[:, 0:n], x, [xk], [c.k], eng="pool")
            q = sq.next()
            P.act(q[:, 0:n], c[:, 0:n], AF.Square, [c.k], [q.k])
            p2 = ps2.next()
            P.mm(p2[:, 0:n], G.ones[:], q[:, 0:n], True, True, [q.k, "ones"], [p2.k])
            P.act(r[:, 0:n], p2[:, 0:n], AF.Sqrt, [p2.k], [r.k], bias=eps_col, scale=1.0 / 128)
            P.recip(r[:, 0:n], r[:, 0:n], [r.k], [r.k])
            P.tt(c[:, 0:n], c[:, 0:n], r[:, 0:n], ALU.mult, [c.k, r.k], [c.k])
            if bcols is not None:
                P.ts(c[:, 0:n], c[:, 0:n], gcols[:, h:h + 1], bcols[:, h:h + 1], ALU.mult, ALU.add, [c.k, "cols"], [c.k])
            else:
                P.ts(c[:, 0:n], c[:, 0:n], gcols[:, h:h + 1], None, ALU.mult, None, [c.k, "cols"], [c.k])
            if STOP == 4:
                continue
            g = gin.next()
            P.dma(g[:, 0:n], G.pT_d[gate_rows + h * 128:gate_rows + (h + 1) * 128, t0:t0 + n], [], [g.k])
            P.act(g[:, 0:n], g[:, 0:n], gate_func, [g.k], [g.k])
            if STOP == 6:
                continue
            o = ob.next()
            P.tt(o[:, 0:n], c[:, 0:n], g[:, 0:n], ALU.mult, [c.k, g.k], [o.k], eng="pool")
            if STOP == 5:
                continue
            P.dma(dst[h * 128:(h + 1) * 128, t0:t0 + n], o[:, 0:n], [o.k], [], q="sp")


def phase_rwkv(G, l):
    P = Phase(G.nc)
    ident = G.ident
    bones = G.bones
    q8 = P.psq(8)
    pw = Rot(q8.tiles[0:2])
    mu = P.sb([128, 15])
    load_cols(P, mu[:], G.a_mu[l:l + 1, :], 15, pw, ident)
    mu1 = P.sb([128, 15])
    P.ts(mu1[:], mu[:], -1.0, 1.0, ALU.mult, ALU.add, ["cols"], [mu1.k])
    mum = P.sb([128, 6, 15])
    for j in range(6):
        P.ts(mum[:, j, :], mu[:], G.pmask[:, j:j + 1], None, ALU.mult, None, ["cols", "pmask"], [mum.k])
    cols = P.sb([128, 12, 4])
    for i_, src in enumerate((G.a_w0[l, 0:1, :], G.a_w0[l, 1:2, :], G.a_a0[l, 0:1, :], G.a_a0[l, 1:2, :],
                              G.a_kk[l:l + 1, :], G.a_ka[l:l + 1, :], G.a_rk[l:l + 1, :], G.a_ln_g[l:l + 1, :],
                              G.a_ln_b[l:l + 1, :])):
        load_cols(P, cols[:, i_, :], src, 4, pw, ident)
    P.ts(cols[:, 9, :], cols[:, 5, :], -1.0, 1.0, ALU.mult, ALU.add, ["cols"], ["cols"])
    wup = P.sb([128, 512])
    aup = P.sb([128, 512])
    gup = P.sb([128, 512])
    for d in range(2):
        P.dma(wup[d * 64:(d + 1) * 64, :], G.a_wup[l, d], [], [wup.k])
        P.dma(aup[d * 64:(d + 1) * 64, :], G.a_aup[l, d], [], [aup.k])
    P.dma(gup[:], G.a_gup[l], [], [gup.k])

    def mix(dst, chunk):
        x = xin.next()
        P.dma(x[:], G.pT_d[chunk * 128:(chunk + 1) * 128, :], [], [x.k])
        P.ts(dst[:], x[:], mu1[:, chunk:chunk + 1], None, ALU.mult, None, [x.k, mu1.k], [dst.k])
        m = lambda j: mum[:, j, chunk:chunk + 1]
        rk_ = [x.k, dst.k, mum.k]
        P.stt(dst[:, 1:LC], x[:, 0:LC - 1], m(4), dst[:, 1:LC], ALU.mult, ALU.add, rk_, [dst.k])
        P.stt(dst[:, 0:LC - 1], x[:, 1:LC], m(5), dst[:, 0:LC - 1], ALU.mult, ALU.add, rk_, [dst.k])
        xg = x[:, LC:T].rearrange("p (r w) -> p r w", w=64)
        dg = dst[:, LC:T].rearrange("p (r w) -> p r w", w=64)
        P.stt(dg[:, :, 1:64], xg[:, :, 0:63], m(0), dg[:, :, 1:64], ALU.mult, ALU.add, rk_, [dst.k])
        P.stt(dg[:, :, 0:63], xg[:, :, 1:64], m(1), dg[:, :, 0:63], ALU.mult, ALU.add, rk_, [dst.k])
        P.stt(dst[:, LC + 64:T], x[:, LC:T - 64], m(2), dst[:, LC + 64:T], ALU.mult, ALU.add, rk_, [dst.k])
        P.stt(dst[:, LC:T - 64], x[:, LC + 64:T], m(3), dst[:, LC:T - 64], ALU.mult, ALU.add, rk_, [dst.k])

    xin = P.rot(1, [128, T])
    tw = P.sb([128, T])
    adz = P.sb([128, T])
    sg = P.sb([128, T])
    mix(tw, 12)
    P.act(tw[:], tw[:], AF.Tanh, [tw.k], [tw.k])
    mix(adz, 13)
    mix(sg, 14)
    P.act(sg[:], sg[:], AF.Sigmoid, [sg.k], [sg.k])
    if STOP == 21:
        P.end()
        return
    rT = P.sb([128, T])
    kT = P.sb([128, T])
    vT = P.sb([128, T])
    kkT = P.sb([128, T])
    lw = P.sb([128, T])
    aT = P.sb([128, T])
    ktT = P.sb([128, T])
    bT = P.sb([128, T])
    ysum = P.sb([128, T])
    Vboth = P.sb([128, NT, 128])
    Vpad = P.sb([128, NT, 2, 128])
    Upad = P.sb([128, 2, 128])
    STbd = P.sb([128, 128])
    pbig = P.rot(2, [128, 512], psum=True)
    tmp = P.rot(4, [128, 512])
    sm = P.rot(6, [128, 128])
    pq = Rot([P.ps([128, 128]) for _ in range(4)] + [Tile(q8.tiles[4].t, q8.tiles[4].k)])
    E = P.rot(3, [128, 3, 128])
    ops4 = P.rot(2, [128, 4, 128])
    tokT = P.rot(2, [128, 3, 128])
    Ysb = P.rot(4, [128, 128])
    YTsb = P.rot(4, [128, 128])
    Rsb = P.rot(4, [128, 128])
    msk = P.rot(4, [128, 3, 128])
    Wsb = P.rot(2, [128, 64])
    ob = P.rot(2, [128, 512], BF16)
    P.memset(Upad[:], 0.0, [], [Upad.k + "0", Upad.k + "1"])
    for c in range(4):
        mix(rT, c)
        mix(kT, 4 + c)
        mix(vT, 8 + c)
        if STOP == 25:
            P.end()
            return
        for (t0, n) in TG:
            a = tmp.next()
            P.ts(kkT[:, t0:t0 + n], kT[:, t0:t0 + n], cols[:, 4, c:c + 1], None, ALU.mult, None, [kT.k, "cols"], [kkT.k])
            P.act(a[:, 0:n], kkT[:, t0:t0 + n], AF.Square, [kkT.k], [a.k])
            pb_ = pbig.next()
            P.mm(pb_[:, 0:n], bones[:], a[:, 0:n], True, True, [a.k, "bones"], [pb_.k])
            P.ts(a[:, 0:n], pb_[:, 0:n], 1e-24, None, ALU.max, None, [pb_.k], [a.k])
            P.act(a[:, 0:n], a[:, 0:n], AF.Sqrt, [a.k], [a.k])
            P.recip(a[:, 0:n], a[:, 0:n], [a.k], [a.k])
            P.tt(kkT[:, t0:t0 + n], kkT[:, t0:t0 + n], a[:, 0:n], ALU.mult, [a.k, kkT.k], [kkT.k])
        if STOP == 26:
            P.end()
            return
        P.memset(Vpad[:], 0.0, [], [Vpad.k])
        if STOP == 27:
            P.end()
            return
        for t4 in range(0, NT, 4):
            nn = min(4, NT - t4)
            p_ = pbig.next()
            for j in range(nn):
                P.tr(p_[:, j * 128:(j + 1) * 128], vT[:, (t4 + j) * 128:(t4 + j + 1) * 128], ident[:], [vT.k, "ident"], [p_.k])
            pv = p_[:, 0:nn * 128].rearrange("p (j t) -> p j t", j=nn)
            P.cp(Vboth[:, t4:t4 + nn, :], pv, [p_.k], [Vboth.k], eng="act_copy")
            for j in range(2):
                P.cp(Vpad[:, t4:t4 + nn, j, j * 64:(j + 1) * 64], pv[:, :, j * 64:(j + 1) * 64], [p_.k], [Vpad.k], eng="act_copy")
        if STOP == 22:
            P.end()
            return
        for d in range(2):
            tri = G.tri[:, d, :]
            stri = G.tri[:, 2 + d, :]
            striT = G.tri[:, 3 - d, :]
            last = 127 if d == 0 else 0
            hp_d = slice(d * 64, (d + 1) * 64)
            for (t0, n) in TG:
                pb_ = pbig.next()
                P.mm(pb_[:, 0:n], wup[hp_d, c * 128:(c + 1) * 128], tw[hp_d, t0:t0 + n], True, True, [wup.k, tw.k], [pb_.k])
                P.act(lw[:, t0:t0 + n], pb_[:, 0:n], AF.Sigmoid, [pb_.k, "cols"], [lw.k], bias=cols[:, d, c:c + 1])
                P.ts(lw[:, t0:t0 + n], lw[:, t0:t0 + n], -0.6065306597126334, None, ALU.mult, None, [lw.k], [lw.k])
                pb2 = pbig.next()
                P.mm(pb2[:, 0:n], aup[hp_d, c * 128:(c + 1) * 128], adz[hp_d, t0:t0 + n], True, True, [aup.k, adz.k], [pb2.k])
                P.act(aT[:, t0:t0 + n], pb2[:, 0:n], AF.Sigmoid, [pb2.k, "cols"], [aT.k], bias=cols[:, 2 + d, c:c + 1])
                a = tmp.next()
                P.ts(a[:, 0:n], aT[:, t0:t0 + n], cols[:, 5, c:c + 1], cols[:, 9, c:c + 1], ALU.mult, ALU.add, [aT.k, "cols"], [a.k])
                P.tt(ktT[:, t0:t0 + n], kT[:, t0:t0 + n], a[:, 0:n], ALU.mult, [kT.k, a.k], [ktT.k])
                P.tt(bT[:, t0:t0 + n], kkT[:, t0:t0 + n], aT[:, t0:t0 + n], ALU.mult, [kkT.k, aT.k], [bT.k], eng="pool")
            if STOP == 23:
                P.end()
                return
            P.memset(STbd[:], 0.0, [], [STbd.k])
            for tt in ORDER[d]:
                tsl = slice(tt * 128, (tt + 1) * 128)
                tk = tokT.next()
                p_ = pq.next()
                P.tr(p_[:], lw[:, tsl], ident[:], [lw.k, "ident"], [p_.k])
                P.cp(tk[:, 0, :], p_[:], [p_.k], [tk.k + "a"], eng="act_copy")
                pci = pq.next()
                pce = pq.next()
                P.mm(pci[:], tk[:, 0, :], tri, True, True, [tk.k + "a", "tri"], [pci.k])
                P.mm(pce[:], tk[:, 0, :], stri, True, True, [tk.k + "a", "tri"], [pce.k])
                e = E.next()
                P.act(e[:, 0, :], pci[:], AF.Exp, [pci.k], [e.k])
                P.act(e[:, 1, :], pce[:], AF.Exp, [pce.k], [e.k])
                P.act(e[:, 2, :], pci[:], AF.Exp, [pci.k], [e.k], scale=-1.0)
                o4 = ops4.next()
                P.tt(o4[:, 0, :], kkT[:, tsl], e[:, 1, :], ALU.mult, [kkT.k, e.k], [o4.k])
                P.tt(o4[:, 1, :], bT[:, tsl], e[:, 2, :], ALU.mult, [bT.k, e.k], [o4.k], eng="pool")
                P.tt(o4[:, 2, :], ktT[:, tsl], e[:, 2, :], ALU.mult, [ktT.k, e.k], [o4.k])
                P.tt(o4[:, 3, :], rT[:, tsl], e[:, 0, :], ALU.mult, [rT.k, e.k], [o4.k], eng="pool")
                p_ = pq.next()
                P.tr(p_[:], o4[:, 1, :], ident[:], [o4.k, "ident"], [p_.k])
                P.ts(tk[:, 1, :], p_[:], -1.0, None, ALU.mult, None, [p_.k], [tk.k + "b"])
                p_ = pq.next()
                P.tr(p_[:], o4[:, 2, :], ident[:], [o4.k, "ident"], [p_.k])
                P.cp(tk[:, 2, :], p_[:], [p_.k], [tk.k + "c"], eng="act_copy")
                mks = []
                Ys, YTs, Rs = [None, None], [None, None], [None, None]
                for j in range(2):
                    hp = slice(j * 64, (j + 1) * 64)
                    kkg, bh, kth, rt = o4[hp, 0, :], o4[hp, 1, :], o4[hp, 2, :], o4[hp, 3, :]
                    p1 = pq.next()
                    P.mm(p1[:], bh, kkg, True, True, [o4.k], [p1.k])
                    Y = Ysb.next()
                    P.stt(Y[:], p1[:], -1.0, stri, ALU.mult, ALU.mult, [p1.k, "tri"], [Y.k])
                    p2 = pq.next()
                    P.mm(p2[:], kkg, bh, True, True, [o4.k], [p2.k])
                    YT = YTsb.next()
                    P.stt(YT[:], p2[:], -1.0, striT, ALU.mult, ALU.mult, [p2.k, "tri"], [YT.k])
                    R = Rsb.next()
                    P.tt(R[:], Y[:], ident[:], ALU.add, [Y.k, "ident"], [R.k], eng="pool")
                    Ys[j], YTs[j], Rs[j] = Y, YT, R
                for j in range(2):
                    hp = slice(j * 64, (j + 1) * 64)
                    kkg, bh, kth, rt = o4[hp, 0, :], o4[hp, 1, :], o4[hp, 2, :], o4[hp, 3, :]
                    mk = msk.next()
                    p3 = pq.next()
                    P.mm(p3[:], kth, kkg, True, True, [o4.k], [p3.k])
                    P.tt(mk[:, 0, :], p3[:], stri, ALU.mult, [p3.k, "tri"], [mk.k + "0"])
                    p4 = pq.next()
                    P.mm(p4[:], bh, rt, True, True, [o4.k], [p4.k])
                    P.stt(mk[:, 1, :], p4[:], -1.0, tri, ALU.mult, ALU.mult, [p4.k, "tri"], [mk.k + "1"])
                    p5 = pq.next()
                    P.mm(p5[:], kth, rt, True, True, [o4.k], [p5.k])
                    P.tt(mk[:, 2, :], p5[:], tri, ALU.mult, [p5.k, "tri"], [mk.k + "2"])
                    mks.append(mk)
                for it in range(1, 7):
                    for j in range(2):
                        Y, YT, R = Ys[j], YTs[j], Rs[j]
                        pyT = pq.next()
                        P.mm(pyT[:], Y[:], YT[:], True, True, [Y.k, YT.k], [pyT.k])
                        if it < 6:
                            py = pq.next()
                            P.mm(py[:], YT[:], Y[:], True, True, [Y.k, YT.k], [py.k])
                            Y2 = Ysb.next()
                            P.cp(Y2[:], py[:], [py.k], [Y2.k], eng="act_copy")
                        YT2 = YTsb.next()
                        P.tt(YT2[:], pyT[:], G.ones[:], ALU.mult, [pyT.k, "ones"], [YT2.k])
                        pr = pq.next()
                        P.mm(pr[:], YT2[:], R[:], True, True, [YT2.k, R.k], [pr.k])
                        R2 = Rsb.next()
                        P.tt(R2[:], pr[:], R[:], ALU.add, [pr.k, R.k], [R2.k])
                        Rs[j] = R2
                        YTs[j] = YT2
                        if it < 6:
                            Ys[j] = Y2
                for j in range(2):
                    hp = slice(j * 64, (j + 1) * 64)
                    kkg = o4[hp, 0, :]
                    mk = mks[j]
                    R = Rs[j]
                    pW = pq.next()
                    P.mm(pW[:, 0:64], kkg, STbd[hp, j * 64:(j + 1) * 64], True, False, [o4.k, STbd.k], [pW.k])
                    P.mm(pW[:, 0:64], mk[:, 0, :], Vboth[:, tt, j * 64:(j + 1) * 64], False, True, [mk.k + "0", Vboth.k], [pW.k])
                    W_ = Wsb.next()
                    P.cp(W_[:], pW[:, 0:64], [pW.k], [W_.k], eng="act_copy")
                    pU = pq.next()
                    P.mm(pU[:, 0:64], R[:], W_[:], True, True, [R.k, W_.k], [pU.k])
                    P.cp(Upad[:, j, j * 64:(j + 1) * 64], pU[:, 0:64], [pU.k], [Upad.k + str(j)], eng="act_copy")
                pY = pq.next()
                P.mm(pY[:], STbd[:], o4[:, 3, :], True, False, [STbd.k, o4.k], [pY.k])
                for j in range(2):
                    P.mm(pY[:], Upad[:, j, :], mks[j][:, 1, :], False, False, [Upad.k + str(j), mks[j].k + "1"], [pY.k])
                    P.mm(pY[:], Vpad[:, tt, j, :], mks[j][:, 2, :], False, j == 1, [Vpad.k, mks[j].k + "2"], [pY.k])
                if d == 0:
                    P.cp(ysum[:, tsl], pY[:], [pY.k], [ysum.k], eng="act_copy")
                else:
                    yt_ = sm.next()
                    P.cp(yt_[:], pY[:], [pY.k], [yt_.k], eng="act_copy")
                    P.tt(ysum[:, tsl], ysum[:, tsl], yt_[:], ALU.add, [yt_.k, ysum.k], [ysum.k])
                pS = pq.next()
                P.mm(pS[:], tk[:, 2, :], Vboth[:, tt, :], True, False, [tk.k + "c", Vboth.k], [pS.k])
                P.mm(pS[:], tk[:, 1, :], Upad[:, 0, :], False, False, [tk.k + "b", Upad.k + "0"], [pS.k])
                P.mm(pS[:], tk[:, 1, :], Upad[:, 1, :], False, True, [tk.k + "b", Upad.k + "1"], [pS.k])
                s_ = sm.next()
                P.cp(s_[:], pS[:], [pS.k], [s_.k], eng="act_copy")
                P.tt(s_[:], s_[:], bones[:], ALU.mult, [s_.k, "bones"], [s_.k])
                P.tt(STbd[:], STbd[:], s_[:], ALU.add, [s_.k, STbd.k], [STbd.k], eng="pool")
                P.ts(STbd[:], STbd[:], e[:, 0, last:last + 1], None, ALU.mult, None, [e.k, STbd.k], [STbd.k])
                if STOP == 24:
                    P.end()
                    return
        if G.debug == "brtest" and c == 0:
            for i_, t_ in enumerate((ysum, kkT, lw, aT, ktT, bT, rT, vT)):
                P.dma(G.dbg_a[i_], t_[:], [t_.k], [], q="sp")
        for (t0, n) in TG:
            pm_ = pbig.next()
            P.mm(pm_[:, 0:n], bones[:], ysum[:, t0:t0 + n], True, True, [ysum.k, "bones"], [pm_.k])
            xc = tmp.next()
            P.stt(xc[:, 0:n], pm_[:, 0:n], -1.0 / 64, ysum[:, t0:t0 + n], ALU.mult, ALU.add, [pm_.k, ysum.k], [xc.k])
            sq_ = tmp.next()
            P.act(sq_[:, 0:n], xc[:, 0:n], AF.Square, [xc.k], [sq_.k])
            pv_ = pbig.next()
            P.mm(pv_[:, 0:n], bones[:], sq_[:, 0:n], True, True, [sq_.k, "bones"], [pv_.k])
            P.act(sq_[:, 0:n], pv_[:, 0:n], AF.Sqrt, [pv_.k], [sq_.k], bias=G.epsa[:, 0:1], scale=1.0 / 64)
            P.recip(sq_[:, 0:n], sq_[:, 0:n], [sq_.k], [sq_.k])
            P.tt(xc[:, 0:n], xc[:, 0:n], sq_[:, 0:n], ALU.mult, [xc.k, sq_.k], [xc.k])
            P.ts(xc[:, 0:n], xc[:, 0:n], cols[:, 7, c:c + 1], cols[:, 8, c:c + 1], ALU.mult, ALU.add, [xc.k, "cols"], [xc.k])
            rk_ = tmp.next()
            P.stt(rk_[:, 0:n], rT[:, t0:t0 + n], cols[:, 6, c:c + 1], kT[:, t0:t0 + n], ALU.mult, ALU.mult, [rT.k, kT.k, "cols"], [rk_.k])
            pb_ = pbig.next()
            P.mm(pb_[:, 0:n], bones[:], rk_[:, 0:n], True, True, [rk_.k, "bones"], [pb_.k])
            P.tt(rk_[:, 0:n], pb_[:, 0:n], vT[:, t0:t0 + n], ALU.mult, [pb_.k, vT.k], [rk_.k])
            P.tt(xc[:, 0:n], xc[:, 0:n], rk_[:, 0:n], ALU.add, [xc.k, rk_.k], [xc.k], eng="pool")
            pg_ = pbig.next()
            P.mm(pg_[:, 0:n], gup[:, c * 128:(c + 1) * 128], sg[:, t0:t0 + n], True, True, [gup.k, sg.k], [pg_.k])
            o = ob.next()
            P.tt(o[:, 0:n], xc[:, 0:n], pg_[:, 0:n], ALU.mult, [xc.k, pg_.k], [o.k])
            P.dma(G.hsT_d[0][c * 128:(c + 1) * 128, t0:t0 + n], o[:, 0:n], [o.k], [], q="sp")
    P.end()


def phase_mlstm(G, l):
    P = Phase(G.nc)
    ident = G.ident
    pw = P.psq(2)
    cw = P.sb([128, 8, 4])
    for j in range(3):
        load_cols(P, cw[:, :, j], G.c_conv_w[l, j:j + 1, :], 8, pw, ident)
    load_cols(P, cw[:, :, 3], G.c_conv_b[l:l + 1, :], 8, pw, ident)
    lng = P.sb([128, 4])
    load_cols(P, lng[:], G.c_ln_g[l:l + 1, :], 4, pw, ident)
    gb = P.sb([16, 1])
    P.dma(gb[:], G.c_gate_b[l].rearrange("a b (c o) -> (a b c) o", o=1), [], [gb.k])
    qT = P.sb([128, 4, T], BF16)
    kT = P.sb([128, 4, T], BF16)
    kTok = P.sb([128, NT, 4, 128], BF16)
    vTok = P.sb([128, NT, 4, 128], BF16)
    hsum = P.sb([128, 4, T])
    xin = P.rot(1, [128, T])
    yv = P.rot(1, [128, T])
    ptr = P.rot(1, [128, 512], psum=True)
    SEG = [(0, LC), (LC, T - LC)]
    for c in range(8):
        x = xin.next()
        P.dma(x[:], G.pT_d[OFF_C + c * 128:OFF_C + (c + 1) * 128, :], [], [x.k])
        y = yv.next()
        P.ts(y[:], x[:], cw[:, c, 1:2], cw[:, c, 3:4], ALU.mult, ALU.add, [x.k, "cols"], [y.k])
        for (a, n) in SEG:
            P.stt(y[:, a + 1:a + n], x[:, a:a + n - 1], cw[:, c, 0:1], y[:, a + 1:a + n], ALU.mult, ALU.add, [x.k, y.k, "cols"], [y.k])
            P.stt(y[:, a:a + n - 1], x[:, a + 1:a + n], cw[:, c, 2:3], y[:, a:a + n - 1], ALU.mult, ALU.add, [x.k, y.k, "cols"], [y.k])
        if c < 4:
            P.act(qT[:, c, :], y[:], AF.Silu, [y.k], [qT.k])
        else:
            P.act(y[:], y[:], AF.Silu, [y.k], [y.k])
            P.ts(y[:], y[:], 128 ** -0.5, None, ALU.mult, None, [y.k], [y.k])
            P.cp(kT[:, c - 4, :], y[:], [y.k], [kT.k], eng="pool")
            for t4 in range(0, NT, 4):
                nn = min(4, NT - t4)
                p_ = ptr.next()
                for j in range(nn):
                    P.tr(p_[:, j * 128:(j + 1) * 128], y[:, (t4 + j) * 128:(t4 + j + 1) * 128], ident[:], [y.k, "ident"], [p_.k])
                P.cp(kTok[:, t4:t4 + nn, c - 4, :], p_[:, 0:nn * 128].rearrange("p (j t) -> p j t", j=nn), [p_.k], [kTok.k], eng="act_copy")
    for h in range(4):
        x = xin.next()
        P.dma(x[:], G.pT_d[OFF_C + 1024 + h * 128:OFF_C + 1024 + (h + 1) * 128, :], [], [x.k])
        for t4 in range(0, NT, 4):
            nn = min(4, NT - t4)
            p_ = ptr.next()
            for j in range(nn):
                P.tr(p_[:, j * 128:(j + 1) * 128], x[:, (t4 + j) * 128:(t4 + j + 1) * 128], ident[:], [x.k, "ident"], [p_.k])
            P.cp(vTok[:, t4:t4 + nn, h, :], p_[:, 0:nn * 128].rearrange("p (j t) -> p j t", j=nn), [p_.k], [vTok.k], eng="act_copy")
    if STOP == 1:
        P.end()
        return
    gT = Tile(xin.tiles[0].t[0:16, :], xin.tiles[0].k)
    lfT = Tile(yv.tiles[0].t[0:16, :], yv.tiles[0].k)
    P.dma(gT[:], G.pT_d[OFF_C + 2048:OFF_C + 2064, :], [], [gT.k])
    P.ts(gT[:], gT[:], gb[:, 0:1], None, ALU.add, None, [gT.k, gb.k], [gT.k])
    if STOP == 11:
        P.end()
        return
    P.act(lfT[:], gT[:], AF.Exp, [gT.k], [lfT.k], scale=-1.0)
    if STOP == 12:
        P.end()
        return
    P.act(lfT[:], lfT[:], AF.Ln, [lfT.k], [lfT.k], bias=G.onec[0:16, 0:1])
    if STOP == 13:
        P.end()
        return
    gTok = P.sb([128, NT, 16])
    lfTok = P.sb([128, NT, 16])
    for tt in range(NT):
        p_ = pw.next()
        P.mm(p_[:, 0:16], gT[:, tt * 128:(tt + 1) * 128], ident[0:16, 0:16], True, True, [gT.k, "ident"], [p_.k])
        P.mm(p_[:, 16:32], lfT[:, tt * 128:(tt + 1) * 128], ident[0:16, 0:16], True, True, [lfT.k, "ident"], [p_.k])
        P.cp(gTok[:, tt, :], p_[:, 0:16], [p_.k], [gTok.k])
        P.ts(lfTok[:, tt, :], p_[:, 16:32], -1.0, None, ALU.mult, None, [p_.k], [lfTok.k])
    if STOP == 2:
        P.end()
        return
    CT = P.sb([128, 4, 128])
    CTb = P.sb([128, 4, 128], BF16)
    NR = P.sb([128, 4, 128])
    NRb = P.sb([128, 4, 128], BF16)
    q8 = P.psq(8)
    pb4 = Rot([pw.tiles[0]])
    pbb = Rot([Tile(ptr.tiles[0].t[:, 0:128], ptr.tiles[0].k), P.ps([128, 128])])
    pss = Rot([q8.tiles[3]])
    psn = Rot([q8.tiles[4]])
    psd = Rot([q8.tiles[5]])
    pcb = P.ps([128, 512])
    psc = Rot([Tile(pcb.t[:, 0:256], pcb.k)])
    bcol = P.rot(2, [128, 4])
    ib = P.rot(2, [128, 4])
    lmask = P.rot(3, [128, 128])
    DT = P.rot(3, [128, 128])
    EB = P.rot(3, [128, 128])
    PT = P.rot(3, [128, 128], BF16)
    qs = P.rot(3, [128, 128], BF16)
    den = P.rot(3, [128, 128])
    hd = P.rot(3, [128, 128])
    bl = P.rot(3, [128, 2])
    sw = P.rot(3, [128, 1])
    ksw = P.rot(3, [128, 128], BF16)
    swbc = P.rot(3, [128, 128], BF16)
    for d in range(2):
        tri = G.tri[:, d, :]
        last = 127 if d == 0 else 0
        for h in range(4):
            P.memset(CT[:, h, :], 0.0, [], [CT.k + str(h)])
            P.memset(CTb[:, h, :], 0.0, [], [CTb.k + str(h)])
            P.memset(NR[:, h, :], 0.0, [], [NR.k + str(h)])
            P.memset(NRb[:, h, :], 0.0, [], [NRb.k + str(h)])
        for tt in ORDER[d]:
            tsl = slice(tt * 128, (tt + 1) * 128)
            p4 = pb4.next()
            P.mm(p4[:, 0:4], tri, lfTok[:, tt, d * 8 + 4:d * 8 + 8], True, True, ["tri", lfTok.k], [p4.k])
            bc = bcol.next()
            P.cp(bc[:], p4[:, 0:4], [p4.k], [bc.k])
            ibt = ib.next()
            P.tt(ibt[:], gTok[:, tt, d * 8:d * 8 + 4], bc[:], ALU.subtract, [gTok.k, bc.k], [ibt.k])
            for h in range(4):
                lm = lmask.next()
                P.ts(lm[:], tri, lfTok[:, tt, d * 8 + 4 + h:d * 8 + 5 + h], None, ALU.mult, None, ["tri", lfTok.k], [lm.k])
                pB = pbb.next()
                P.mm(pB[:], G.ones[:], lm[:], True, True, [lm.k, "ones"], [pB.k])
                dt_ = DT.next()
                P.act(dt_[:], pB[:], AF.Exp, [pB.k, ibt.k], [dt_.k], bias=ibt[:, h:h + 1])
                eb = EB.next()
                P.act(eb[:], pB[:], AF.Exp, [pB.k], [eb.k])
                b_ = bl.next()
                P.cp(b_[:, 0:1], pB[:, last:last + 1], [pB.k], [b_.k])
                P.tt(dt_[:], dt_[:], tri, ALU.mult, [dt_.k, "tri"], [dt_.k], eng="pool")
                ps_s = pss.next()
                P.mm(ps_s[:], kT[:, h, tsl], qT[:, h, tsl], True, True, [kT.k, qT.k], [ps_s.k])
                pt_ = PT.next()
                P.tt(pt_[:], ps_s[:], dt_[:], ALU.mult, [ps_s.k, dt_.k], [pt_.k])
                q_ = qs.next()
                P.tt(q_[:], qT[:, h, tsl], eb[:], ALU.mult, [qT.k, eb.k], [q_.k], eng="pool")
                pn = psn.next()
                P.mm(pn[:], vTok[:, tt, h, :], pt_[:], True, False, [vTok.k, pt_.k], [pn.k])
                P.mm(pn[:], CTb[:, h, :], q_[:], False, True, [CTb.k + str(h), q_.k], [pn.k])
                pd = psd.next()
                P.mm(pd[:], G.onesb[:], pt_[:], True, False, [pt_.k, "onesb"], [pd.k])
                P.mm(pd[:], NRb[:, h, :], q_[:], False, True, [NRb.k + str(h), q_.k], [pd.k])
                de = den.next()
                P.act(de[:], pd[:], AF.Abs, [pd.k], [de.k])
                P.ts(de[:], de[:], 1.0, None, ALU.max, None, [de.k], [de.k])
                P.recip(de[:], de[:], [de.k], [de.k])
                if d == 0:
                    P.tt(hsum[:, h, tsl], pn[:], de[:], ALU.mult, [pn.k, de.k], [hsum.k])
                else:
                    hh = hd.next()
                    P.tt(hh[:], pn[:], de[:], ALU.mult, [pn.k, de.k], [hh.k])
                    P.tt(hsum[:, h, tsl], hsum[:, h, tsl], hh[:], ALU.add, [hh.k, hsum.k], [hsum.k], eng="pool")
                s_ = sw.next()
                P.act(s_[:], ibt[:, h:h + 1], AF.Exp, [ibt.k, b_.k], [s_.k], bias=b_[:, 0:1])
                P.act(b_[:, 1:2], b_[:, 0:1], AF.Exp, [b_.k], [b_.k])
                ks = ksw.next()
                P.ts(ks[:], kTok[:, tt, h, :], s_[:, 0:1], None, ALU.mult, None, [kTok.k, s_.k], [ks.k])
                sb_ = swbc.next()
                P.ts(sb_[:], G.ones[:], s_[:, 0:1], None, ALU.mult, None, [s_.k, "ones"], [sb_.k])
                pc_ = psc.next()
                P.mm(pc_[:, 0:128], ks[:], vTok[:, tt, h, :], True, True, [ks.k, vTok.k], [pc_.k])
                P.mm(pc_[:, 128:256], kTok[:, tt, h, :], sb_[:], True, True, [kTok.k, sb_.k], [pc_.k])
                P.stt(CT[:, h, :], CT[:, h, :], b_[:, 1:2], pc_[:, 0:128], ALU.mult, ALU.add, [pc_.k, b_.k, CT.k + str(h)], [CT.k + str(h)])
                P.stt(NR[:, h, :], NR[:, h, :], b_[:, 1:2], pc_[:, 128:256], ALU.mult, ALU.add, [pc_.k, b_.k, NR.k + str(h)], [NR.k + str(h)])
                P.cp(CTb[:, h, :], CT[:, h, :], [CT.k + str(h)], [CTb.k + str(h)], eng="act_copy")
                P.cp(NRb[:, h, :], NR[:, h, :], [NR.k + str(h)], [NRb.k + str(h)], eng="act_copy")
    if STOP == 3:
        P.end()
        return
    head_norm_store(P, G, hsum, 4, G.eps5[:, 0:1], lng, None, OFF_C + 1536, AF.Sigmoid, G.hsT_d[2], True, pw)
    P.end()


def phase_gla(G, l):
    P = Phase(G.nc)
    ident = G.ident
    q8 = P.psq(8)
    pw = Rot(q8.tiles[0:2])
    lng = P.sb([128, 4])
    load_cols(P, lng[:], G.d_ln_g[l:l + 1, :], 4, pw, ident)
    aup = P.sb([17, 2, 256])
    for d in range(2):
        P.dma(aup[0:16, d, :], G.d_aup[l, d], [], [aup.k])
        P.dma(aup[16:17, d, :], G.d_ab[l, d:d + 1, :], [], [aup.k])
    adT = P.sb([17, 2, T])
    P.memset(adT[:], 1.0, [], [adT.k])
    for d in range(2):
        P.dma(adT[0:16, d, :], G.pT_d[OFF_D + 1536 + d * 16:OFF_D + 1552 + d * 16, :], [adT.k], [adT.k])
    qT = P.sb([128, 2, T])
    kT = P.sb([128, 2, T])
    for c in range(2):
        P.dma(qT[:, c, :], G.pT_d[OFF_D + c * 128:OFF_D + (c + 1) * 128, :], [], [qT.k])
        P.dma(kT[:, c, :], G.pT_d[OFF_D + 256 + c * 128:OFF_D + 256 + (c + 1) * 128, :], [], [kT.k])
    vTok = P.sb([128, NT, 4, 128], BF16)
    xin = P.rot(2, [128, T])
    ptr = P.rot(1, [128, 512], psum=True)
    for h in range(4):
        x = xin.next()
        P.dma(x[:], G.pT_d[OFF_D + 512 + h * 128:OFF_D + 512 + (h + 1) * 128, :], [], [x.k])
        for t4 in range(0, NT, 4):
            nn = min(4, NT - t4)
            p_ = ptr.next()
            for j in range(nn):
                P.tr(p_[:, j * 128:(j + 1) * 128], x[:, (t4 + j) * 128:(t4 + j + 1) * 128], ident[:], [x.k, "ident"], [p_.k])
            P.cp(vTok[:, t4:t4 + nn, h, :], p_[:, 0:nn * 128].rearrange("p (j t) -> p j t", j=nn), [p_.k], [vTok.k], eng="act_copy")
    osum = P.sb([128, 4, T])
    S2 = P.sb([128, 2, 128])
    S2b = P.sb([128, 2, 128], BF16)
    plb = P.ps([128, 512])
    pla = Rot([Tile(plb.t[:, 0:256], plb.k)])
    pS = Rot([Tile(plb.t[:, 256:512], plb.k)])
    laTok = P.rot(2, [128, 256])
    pbc = Rot(q8.tiles[2:4])
    EQ = P.rot(3, [128, 128])
    EK = P.rot(3, [128, 128])
    qt = P.rot(3, [128, 128], BF16)
    kt = P.rot(3, [128, 128], BF16)
    pkt = P.rot(1, [128, 128], psum=True, dt=BF16)
    ktTok = P.rot(2, [128, 128], BF16)
    pA = Rot(q8.tiles[4:6])
    ATm = P.rot(3, [128, 128], BF16)
    po = Rot([Tile(ptr.tiles[0].t[:, 0:128], ptr.tiles[0].k), P.ps([128, 128])])
    od = P.rot(3, [128, 128])
    for d in range(2):
        tri = G.tri[:, d, :]
        last = 127 if d == 0 else 0
        for c in range(2):
            P.memset(S2[:, c, :], 0.0, [], [S2.k + str(c)])
            P.memset(S2b[:, c, :], 0.0, [], [S2b.k + str(c)])
        for tt in ORDER[d]:
            tsl = slice(tt * 128, (tt + 1) * 128)
            pl = pla.next()
            P.mm(pl[:], adT[:, d, tsl], aup[:, d, :], True, True, [adT.k, aup.k], [pl.k])
            la = laTok.next()
            P.act(la[:], pl[:], AF.Exp, [pl.k], [la.k], scale=-1.0)
            P.act(la[:], la[:], AF.Ln, [la.k], [la.k], bias=G.onec[:, 0:1])
            for c in range(2):
                pb_ = pbc.next()
                P.mm(pb_[:], la[:, c * 128:(c + 1) * 128], tri, True, True, [la.k, "tri"], [pb_.k])
                eq = EQ.next()
                ek = EK.next()
                P.act(eq[:], pb_[:], AF.Exp, [pb_.k], [eq.k], scale=-1.0 / 16)
                P.act(ek[:], pb_[:], AF.Exp, [pb_.k], [ek.k], scale=1.0 / 16)
                q_ = qt.next()
                k_ = kt.next()
                P.stt(q_[:], qT[:, c, tsl], 0.125, eq[:], ALU.mult, ALU.mult, [qT.k, eq.k], [q_.k])
                P.tt(k_[:], kT[:, c, tsl], ek[:], ALU.mult, [kT.k, ek.k], [k_.k], eng="pool")
                pk = pkt.next()
                P.tr(pk[:], k_[:], G.identb[:], [k_.k, "identb"], [pk.k])
                kk = ktTok.next()
                P.cp(kk[:], pk[:], [pk.k], [kk.k], eng="act_copy")
                for j in range(2):
                    h = c * 2 + j
                    hp = slice(j * 64, (j + 1) * 64)
                    pa = pA.next()
                    P.mm(pa[:], k_[hp, :], q_[hp, :], True, True, [k_.k, q_.k], [pa.k])
                    am = ATm.next()
                    P.tt(am[:], pa[:], tri, ALU.mult, [pa.k, "tri"], [am.k])
                    p_o = po.next()
                    P.mm(p_o[:], vTok[:, tt, h, :], am[:], True, False, [vTok.k, am.k], [p_o.k])
                    P.mm(p_o[:], S2b[hp, c, :], q_[hp, :], False, True, [S2b.k + str(c), q_.k], [p_o.k])
                    if d == 0:
                        P.cp(osum[:, h, tsl], p_o[:], [p_o.k], [osum.k], eng="act_copy")
                    else:
                        P.tt(osum[:, h, tsl], osum[:, h, tsl], p_o[:], ALU.add, [p_o.k, osum.k], [osum.k])
                ps_ = pS.next()
                P.mm(ps_[:], kk[:], vTok[:, tt, c * 2:c * 2 + 2, :].rearrange("p a b -> p (a b)"), True, True, [kk.k, vTok.k], [ps_.k])
                for j in range(2):
                    hp = slice(j * 64, (j + 1) * 64)
                    P.tt(S2[hp, c, :], S2[hp, c, :], ps_[hp, j * 128:(j + 1) * 128], ALU.add, [ps_.k, S2.k + str(c)], [S2.k + str(c)])
                P.ts(S2[:, c, :], S2[:, c, :], eq[:, last:last + 1], None, ALU.mult, None, [eq.k, S2.k + str(c)], [S2.k + str(c)])
                P.cp(S2b[:, c, :], S2[:, c, :], [S2.k + str(c)], [S2b.k + str(c)], eng="act_copy")
    head_norm_store(P, G, osum, 4, G.epsc[:, 0:1], lng, None, OFF_D + 1024, AF.Silu, G.hsT_d[3], False, pw)
    P.end()


def phase_down(G, wsrc, nk, gate_of, rhs_res=None, rhs_dram=None):
    P = Phase(G.nc)
    wstg = P.rot(3, [128, 4, 512])
    w2q = P.sb([128, nk, 512], BF16)
    h1t = P.rot(2, [128, nk, 512], BF16) if rhs_dram is not None else None
    hold = P.rot(2, [128, 4, 512])
    hnew = P.rot(2, [128, 4, 512])
    pso = P.rot(4, [128, 512], psum=True)
    for dq in range(4):
        for f4 in range(nk // 4):
            st_ = wstg.next()
            P.dma(st_[:], fm(wsrc)[:, f4 * 4:(f4 + 1) * 4, dq * 512:(dq + 1) * 512], [], [st_.k])
            P.cp(w2q[:, f4 * 4:(f4 + 1) * 4, :], st_[:], [st_.k], [w2q.k + f"_{f4}"], eng="pool")
        wkeys = [w2q.k + f"_{f4}" for f4 in range(nk // 4)]
        for (t0, n) in TG:
            row = 1 if t0 < LC else 0
            if rhs_dram is not None:
                ht = h1t.next()
                nq = nk // 4
                for q4 in range(4):
                    P.dma(ht[:, q4 * nq:(q4 + 1) * nq, 0:n], fm(rhs_dram)[:, q4 * nq:(q4 + 1) * nq, t0:t0 + n], [], [ht.k])
                rhs = lambda fc: ht[:, fc, 0:n]
                rk = ht.k
            else:
                rhs = lambda fc: rhs_res[:, fc, t0:t0 + n]
                rk = rhs_res.k
            ho = hold.next()
            P.dma(ho[:, :, 0:n], fm(G.hT_d)[:, dq * 4:(dq + 1) * 4, t0:t0 + n], ["hT_d"], [ho.k])
            hn = hnew.next()
            for dc in range(4):
                ps = pso.next()
                for fc in range(nk):
                    P.mm(ps[:, 0:n], w2q[:, fc, dc * 128:(dc + 1) * 128], rhs(fc), fc == 0, fc == nk - 1,
                         [wkeys[fc // 4], rk], [ps.k])
                P.stt(hn[:, dc, 0:n], ps[:, 0:n], gate_of(dq * 4 + dc, row), ho[:, dc, 0:n],
                      ALU.mult, ALU.add, [ps.k, ho.k, "gate"], [hn.k])
            P.dma(fm(G.hT_d)[:, dq * 4:(dq + 1) * 4, t0:t0 + n], hn[:, :, 0:n], [hn.k], ["hT_d"], q="pool")
    P.end()


def phase_merge(G, yT, l):
    P = Phase(G.nc)
    hs = P.sb([128, 16, T], BF16)
    for n4 in range(4):
        for (t0, n) in TG:
            P.dma(hs[:, n4 * 4:(n4 + 1) * 4, t0:t0 + n], fm(G.hsT_d[n4])[:, :, t0:t0 + n], [], [hs.k])
    wstg = P.rot(2, [128, 16, 128])
    wbf = P.rot(2, [128, 16, 128], BF16)
    gt = P.rot(4, [128, 512])
    pso = P.rot(4, [128, 512], psum=True)
    acc = P.rot(2, [128, 512])
    for dc in range(NCH):
        st_ = wstg.next()
        P.dma(st_[:], G.br_w[l].rearrange("n (k p) d -> p (n k) d", p=128)[:, :, dc * 128:(dc + 1) * 128], [], [st_.k])
        b = wbf.next()
        P.cp(b[:], st_[:], [st_.k], [b.k], eng="pool")
        for (t0, n) in TG:
            a = acc.next()
            for n4 in range(4):
                g = gt.next()
                r0 = OFF_G + n4 * D + dc * 128
                P.dma(g[:, 0:n], G.pT_d[r0:r0 + 128, t0:t0 + n], [], [g.k])
                ps = pso.next()
                for k in range(4):
                    P.mm(ps[:, 0:n], b[:, n4 * 4 + k, :], hs[:, n4 * 4 + k, t0:t0 + n], k == 0, k == 3, [b.k, hs.k], [ps.k])
                if n4 == 0:
                    P.tt(a[:, 0:n], ps[:, 0:n], g[:, 0:n], ALU.mult, [ps.k, g.k], [a.k])
                else:
                    P.tt(g[:, 0:n], ps[:, 0:n], g[:, 0:n], ALU.mult, [ps.k, g.k], [g.k])
                    if n4 < 3:
                        P.tt(a[:, 0:n], a[:, 0:n], g[:, 0:n], ALU.add, [a.k, g.k], [a.k], eng="pool")
                    else:
                        P.tt(yT[:, dc, t0:t0 + n], a[:, 0:n], g[:, 0:n], ALU.add, [a.k, g.k], [yT.k], eng="pool")
    P.end()


def sub_mixer(G, l):
    Phase.count += 1
    with G.nc.sbuf_tensor(f"hnT_{Phase.count}", [128, NCH, T], BF16) as hn_t:
        hnT = Tile(hn_t, "hnT")
        phase_norm(G, hnT, l, 1)
        phase_inproj(G, hnT, l)
    if G.debug == "inproj":
        return
    for br in G.branches:
        {"a": lambda G, l: phase_rwkv(G, l), "b": phase_gmlp, "c": phase_mlstm, "d": phase_gla}[br](G, l)
    if G.debug == "branches":
        return
    Phase.count += 1
    with G.nc.sbuf_tensor(f"yT_{Phase.count}", [128, NCH, T], BF16) as y_t:
        yT = Tile(y_t, "yT")
        phase_merge(G, yT, l)
        phase_down(G, G.out_w[l], NCH, lambda ch, row: G.gate[:, l, 1, ch, row:row + 1], rhs_res=yT)


def sub_ffn(G, l, s, which):
    Phase.count += 1
    with G.nc.sbuf_tensor(f"hnT_{Phase.count}", [128, NCH, T], BF16) as hn_t:
        hnT = Tile(hn_t, "hnT")
        phase_norm(G, hnT, l, s)
        phase_ffn_up(G, hnT, l, s, which)
    phase_down(G, G.ffn_w2[l, which], NFC, lambda ch, row: G.gate[:, l, s, ch, row:row + 1], rhs_dram=G.h1T_d)


def phase_final(G):
    nc = G.nc
    P = Phase(nc)
    hnT = P.sb([128, NCH, T], F32) if False else None
    hin = P.rot(2, [128, NCH, 512])
    sq = P.rot(2, [128, 512])
    pss = P.rot(2, [128, 512], psum=True)
    rstd = P.rot(2, [128, 512])
    hn = P.rot(2, [128, NCH, 512])
    pst = P.rot(2, [128, 512], psum=True)
    ot = P.rot(2, [128, D])
    for (t0, n) in TG[1:]:
        hi = hin.next()
        P.dma(hi[:, :, 0:n], fm(G.hT_d)[:, :, t0:t0 + n], ["hT_d"], [hi.k])
        ps = pss.next()
        for c in range(NCH):
            s = sq.next()
            P.act(s[:, 0:n], hi[:, c, 0:n], AF.Square, [hi.k], [s.k])
            P.mm(ps[:, 0:n], G.ones[:], s[:, 0:n], c == 0, c == NCH - 1, [s.k, "ones"], [ps.k])
        rs = rstd.next()
        P.act(rs[:, 0:n], ps[:, 0:n], AF.Sqrt, [ps.k], [rs.k], bias=G.epsc[:, 0:1], scale=1.0 / D)
        P.recip(rs[:, 0:n], rs[:, 0:n], [rs.k], [rs.k])
        h2 = hn.next()
        for c in range(NCH):
            P.stt(h2[:, c, 0:n], hi[:, c, 0:n], G.fgT[:, c:c + 1], rs[:, 0:n], ALU.mult, ALU.mult,
                  [hi.k, rs.k, "fgT"], [h2.k])
        for tt in range(n // 128):
            o = ot.next()
            for c4 in range(4):
                pt = pst.next()
                for j in range(4):
                    c = c4 * 4 + j
                    P.tr(pt[:, j * 128:(j + 1) * 128], h2[:, c, tt * 128:(tt + 1) * 128], G.ident[:], [h2.k, "ident"], [pt.k])
                P.cp(o[:, c4 * 512:(c4 + 1) * 512], pt[:], [pt.k], [o.k], eng="dve" if c4 % 2 == 0 else "act_copy")
            tok = t0 - LC + tt * 128
            P.dma(G.out[tok:tok + 128, :], o[:], [o.k], [], q="pool")
    P.end()


_orig_cp = Phase.cp


def _cp(self, out, in_, r, w, eng="dve"):
    if eng == "act_copy":
        self.S.op("act", lambda e: e.copy(out=out, in_=in_), r, w)
    elif eng == "dve":
        self.S.op("dve", lambda e: e.tensor_scalar(out=out, in0=in_, scalar1=1.0, scalar2=None, op0=ALU.mult), r, w)
    else:
        _orig_cp(self, out, in_, r, w, eng)


Phase.cp = _cp


def build(debug=None, depth=DEPTH, branches="abcd"):
    nc = bass.Bass("TRN2", target_bir_lowering=False)
    G = Ctx()
    G.nc = nc
    BIG = ("ada_w", "ffn_w1", "ffn_w3", "ffn_w2", "in_w", "br_w", "out_w", "x_b")
    inp = lambda name, shape: None if (debug == "brtest" and name in BIG) else nc.dram_tensor(name, list(shape), F32, kind="ExternalInput").ap()
    G.x_b = inp("x_b", [2048, D])
    G.ctx_b = inp("ctx_b", [LC, D])
    G.c_b = inp("c_b", [1, D])
    G.c_ctx = inp("c_ctx", [1, D])
    G.ada_w = inp("ada_w", [DEPTH, D, 9 * D])
    G.ada_b = inp("ada_b", [DEPTH, 9 * D])
    G.norm_g = inp("norm_g", [DEPTH, 3, D])
    G.ffn_w1 = inp("ffn_w1", [DEPTH, 2, D, DFF])
    G.ffn_w3 = inp("ffn_w3", [DEPTH, 2, D, DFF])
    G.ffn_w2 = inp("ffn_w2", [DEPTH, 2, DFF, D])
    G.final_g = inp("final_g", [1, D])
    G.identd = inp("identd", [128, 128])
    G.trid = inp("trid", [128, 4, 128])
    G.bonesd = inp("bonesd", [128, 128])
    G.pmaskd = inp("pmaskd", [128, 6])
    for nm, shp in (("in_w", [DEPTH, D, D_IN]), ("in_b", [DEPTH, D_IN]), ("a_mu", [DEPTH, A_COLS]),
                    ("a_w0", [DEPTH, 2, 512]), ("a_wup", [DEPTH, 2, 64, 512]), ("a_a0", [DEPTH, 2, 512]),
                    ("a_aup", [DEPTH, 2, 64, 512]), ("a_gup", [DEPTH, 128, 512]), ("a_kk", [DEPTH, 512]),
                    ("a_ka", [DEPTH, 512]), ("a_rk", [DEPTH, 512]), ("a_ln_g", [DEPTH, 512]), ("a_ln_b", [DEPTH, 512]),
                    ("b_ws", [DEPTH, 4, 128, 128]), ("b_bs", [DEPTH, 4, 128]), ("b_ln_g", [DEPTH, 512]),
                    ("b_ln_b", [DEPTH, 512]), ("c_conv_w", [DEPTH, 3, 1024]), ("c_conv_b", [DEPTH, 1024]),
                    ("c_gate_b", [DEPTH, 2, 2, 4]), ("c_ln_g", [DEPTH, 512]), ("d_aup", [DEPTH, 2, 16, 256]),
                    ("d_ab", [DEPTH, 2, 256]), ("d_ln_g", [DEPTH, 512]), ("br_w", [DEPTH, 4, 512, D]),
                    ("out_w", [DEPTH, D, D])):
        setattr(G, nm, inp(nm, shp))
    G.debug = debug
    G.branches = branches
    G.out = None if debug == "brtest" else nc.dram_tensor("out", [2048, D], F32, kind="ExternalOutput").ap()
    G.hT_d = nc.dram_tensor("hT_d", [D, T], F32, kind="Internal").ap()
    G.h1T_d = nc.dram_tensor("h1T_d", [DFF, T], BF16, kind="Internal").ap()
    G.pT_d = nc.dram_tensor("pT_d", [D_IN, T], F32, kind="ExternalOutput" if debug == "inproj" else ("ExternalInput" if debug == "brtest" else "Internal")).ap()
    G.hsT_d = nc.dram_tensor("hsT_d", [4, 512, T], BF16, kind="ExternalOutput" if debug in ("branches", "brtest") else "Internal").ap()
    if debug == "brtest":
        G.dbg_a = nc.dram_tensor("dbg_a", [8, 128, T], F32, kind="ExternalOutput").ap()
    if debug:
        G.dbg_h = nc.dram_tensor("dbg_h", [D, T], F32, kind="ExternalOutput").ap()
    with contextlib.ExitStack() as st:
        sbt = lambda name, shape, dt=F32: st.enter_context(nc.sbuf_tensor(name, list(shape), dt))
        G.ident = sbt("ident", [128, 128])
        G.ones = sbt("ones", [128, 128])
        G.epsc = sbt("epsc", [128, 1])
        G.modT = sbt("modT", [128, DEPTH, 144, 2])
        G.G1 = sbt("G1", [128, DEPTH, 3, NCH, 2])
        G.gate = sbt("gate", [128, DEPTH, 3, NCH, 2])
        G.fgT = sbt("fgT", [128, NCH])
        G.eps5 = sbt("eps5", [128, 1])
        G.onec = sbt("onec", [128, 1])
        G.tri = sbt("tri", [128, 4, 128])
        G.onesb = sbt("onesb", [128, 128], BF16)
        G.bones = sbt("bones", [128, 128])
        G.pmask = sbt("pmask", [128, 6])
        G.epsa = sbt("epsa", [128, 1])
        G.identb = sbt("identb", [128, 128], BF16)
        P = Phase(nc)
        P.dma(G.ident[:], G.identd[:, :], [], ["ident"])
        P.dma(G.tri[:], G.trid[:, :, :], [], ["tri"])
        P.dma(G.bones[:], G.bonesd[:, :], [], ["bones"])
        P.dma(G.pmask[:], G.pmaskd[:, :], [], ["pmask"])
        P.memset(G.epsa[:], 64e-5, [], ["epsa"])
        P.memset(G.ones[:], 1.0, [], ["ones"])
        P.memset(G.onesb[:], 1.0, [], ["onesb"])
        P.memset(G.epsc[:], 1e-6, [], ["epsc"])
        P.memset(G.eps5[:], 1e-5, [], ["eps5"])
        P.memset(G.onec[:], 1.0, [], ["onec"])
        P.cp(G.identb[:], G.ident[:], ["ident"], ["identb"])
        P.end()
        if debug == "ffntest":
            sub_ffn(G, 0, 0, 0)
            return nc
        if debug == "brtest":
            for br in branches:
                {"a": phase_rwkv, "b": phase_gmlp, "c": phase_mlstm, "d": phase_gla}[br](G, 0)
            return nc
        phase_init(G)
        for l in range(depth):
            if debug == "init":
                break
            if debug == "norm":
                Phase.count += 1
                with G.nc.sbuf_tensor(f"hnT_{Phase.count}", [128, NCH, T], BF16) as hn_t:
                    phase_norm(G, Tile(hn_t, "hnT"), l, 0)
                break
            if debug == "up":
                Phase.count += 1
                with G.nc.sbuf_tensor(f"hnT_{Phase.count}", [128, NCH, T], BF16) as hn_t:
                    phase_norm(G, Tile(hn_t, "hnT"), l, 0)
                    phase_ffn_up(G, Tile(hn_t, "hnT"), l, 0, 0)
                break
            sub_ffn(G, l, 0, 0)
            if debug == "ffn1":
                break
            sub_mixer(G, l)
            if debug in ("inproj", "branches", "mix"):
                break
            sub_ffn(G, l, 2, 1)
            if debug == "ffn2":
                break
        if debug:
            P = Phase(nc)
            t = P.rot(2, [128, NCH, 512])
            for (t0, n) in TG:
                b = t.next()
                P.dma(b[:, :, 0:n], fm(G.hT_d)[:, :, t0:t0 + n], [], [b.k])
                P.dma(fm(G.dbg_h)[:, :, t0:t0 + n], b[:, :, 0:n], [b.k], [], q="pool")
            P.end()
        if debug not in ("init", "norm", "up", "inproj", "branches"):
            phase_final(G)
    return nc


def make_in_maps(inputs):
    f = lambda a: np.ascontiguousarray(a, dtype=np.float32)
    ident = np.eye(128, dtype=np.float32)
    ii = np.arange(128)
    trid = np.stack([(ii[:, None] <= ii[None, :]), (ii[:, None] >= ii[None, :]),
                     (ii[:, None] < ii[None, :]), (ii[:, None] > ii[None, :])], axis=1).astype(np.float32)
    bonesd = ((ii[:, None] // 64) == (ii[None, :] // 64)).astype(np.float32)
    pmaskd = np.stack([(ii % 4 == 0), (ii % 4 == 1), (ii % 4 == 2), (ii % 4 == 3), (ii % 2 == 0), (ii % 2 == 1)],
                      axis=1).astype(np.float32)
    shared = {
        "c_ctx": f(inputs["c_ctx"]).reshape(1, D),
        "ada_w": f(inputs["ada_w"]), "ada_b": f(inputs["ada_b"]), "norm_g": f(inputs["norm_g"]),
        "ffn_w1": f(inputs["ffn_w1"]), "ffn_w3": f(inputs["ffn_w3"]), "ffn_w2": f(inputs["ffn_w2"]),
        "final_g": f(inputs["final_g"]).reshape(1, D), "identd": ident, "trid": trid, "bonesd": bonesd, "pmaskd": pmaskd,
    }
    for nm in ("in_w", "in_b", "a_mu", "a_w0", "a_wup", "a_a0", "a_aup", "a_gup", "a_kk", "a_ka", "a_ln_g", "a_ln_b",
               "b_ws", "b_bs", "b_ln_g", "b_ln_b", "c_conv_w", "c_conv_b", "c_gate_b", "c_ln_g", "d_aup", "d_ab",
               "d_ln_g", "br_w", "out_w"):
        shared[nm] = f(inputs[nm])
    shared["a_rk"] = f(inputs["a_rk"]).reshape(DEPTH, 512)
    maps = []
    for core in range(8):
        b = core % 4
        m = dict(shared)
        m["x_b"] = f(inputs["x"][b])
        m["ctx_b"] = f(inputs["ctx"][b])
        m["c_b"] = f(inputs["c"][b]).reshape(1, D)
        maps.append(m)
    return maps


def kernel(**inputs):
    nc = build()
    maps = make_in_maps(inputs)[:4]
    res = run_bass_kernel_spmd(nc, maps, core_ids=list(range(4)))
    return np.stack([np.asarray(res.results[b]["out"], dtype=np.float32) for b in range(4)], axis=0)
```
